# Optimizing a Trainium2 kernel written in Bass

```python
import jax
import jax.numpy as jnp
from jax import lax
import numpy as np

D_MODEL = 1024
BATCH = 4
SEQ = 4096
DEPTH = 2

GRID_W = 64
CTX_LEN = 256
N_HEADS = 8
HEAD_DIM = 64
ATTN_W = N_HEADS * HEAD_DIM
WIN_ROWS_MAX = 8
WIN_COLS = 16
FOURIER_GROUPS = 4
FOURIER_GROUP_DIM = 64
FOURIER_W = FOURIER_GROUPS * FOURIER_GROUP_DIM
CONV_W = 256
CONV_K = 31
N_BRANCHES = 3
NORM_EPS = 1e-6
IN_SIZES = (ATTN_W,) * 4 + (FOURIER_W,) * 2 + (CONV_W,) * 3 + (D_MODEL,) * N_BRANCHES
IN_W = sum(IN_SIZES)

kernel_name = 'hybrid_natten_fourier_conformer_prefix_block'


def rms_norm(x, g):
    xf = x.astype(jnp.float32)
    y = xf * lax.rsqrt(jnp.mean(xf * xf, axis=-1, keepdims=True) + NORM_EPS)
    return (y * g.astype(jnp.float32)).astype(x.dtype)


def layer_norm(x, g, b):
    xf = x.astype(jnp.float32)
    xc = xf - jnp.mean(xf, axis=-1, keepdims=True)
    var = jnp.mean(xc * xc, axis=-1, keepdims=True)
    y = xc * lax.rsqrt(var + NORM_EPS) * g.astype(jnp.float32) + b.astype(jnp.float32)
    return y.astype(x.dtype)


def split_in(p):
    return jnp.split(p, np.cumsum(IN_SIZES)[:-1].tolist(), axis=-1)


def heads(t):
    return t.reshape(t.shape[0], t.shape[1], N_HEADS, HEAD_DIM)


def context_attention(q, k, v):
    s = jnp.einsum('bqhd,bkhd->bhqk', q, k, preferred_element_type=jnp.float32) * (HEAD_DIM ** -0.5)
    p = jax.nn.softmax(s, axis=-1).astype(v.dtype)
    return jnp.einsum('bhqk,bkhd->bqhd', p, v)


def neighbourhood_attention(q, k, v, k_ctx, v_ctx, rpb):
    B, S, H, Dh = q.shape
    rows = S // GRID_W
    kr = min(WIN_ROWS_MAX, rows)
    nw = kr * WIN_COLS
    scale = Dh ** -0.5
    cols = jnp.arange(GRID_W)
    col_start = jnp.clip(cols - WIN_COLS // 2, 0, GRID_W - WIN_COLS)
    key_cols = col_start[:, None] + jnp.arange(WIN_COLS)
    dc_idx = key_cols - cols[:, None] + (WIN_COLS - 1)
    q_rows = jnp.moveaxis(q.reshape(B, rows, GRID_W, H, Dh), 1, 0)

    def one_row(args):
        r, q_r = args
        row_start = jnp.clip(r - kr // 2, 0, rows - kr)
        key_rows = row_start + jnp.arange(kr)
        tok = (key_rows[None, :, None] * GRID_W + key_cols[:, None, :]).reshape(-1)
        k_w = jnp.take(k, tok, axis=1).reshape(B, GRID_W, nw, H, Dh)
        v_w = jnp.take(v, tok, axis=1).reshape(B, GRID_W, nw, H, Dh)
        dr_idx = key_rows - r + (WIN_ROWS_MAX - 1)
        bias = rpb[:, dr_idx[None, :, None], dc_idx[:, None, :]].reshape(H, GRID_W, nw)
        s_win = jnp.einsum('bqhd,bqkhd->bhqk', q_r, k_w, preferred_element_type=jnp.float32) * scale + bias
        s_ctx = jnp.einsum('bqhd,blhd->bhql', q_r, k_ctx, preferred_element_type=jnp.float32) * scale
        p = jax.nn.softmax(jnp.concatenate([s_win, s_ctx], axis=-1), axis=-1).astype(v.dtype)
        return (jnp.einsum('bhqk,bqkhd->bqhd', p[..., :nw], v_w)
                + jnp.einsum('bhql,blhd->bqhd', p[..., nw:], v_ctx))

    out = lax.map(one_row, (jnp.arange(rows), q_rows))
    return jnp.moveaxis(out, 0, 1).reshape(B, S, H, Dh)


def fourier_mix(u):
    B, N, _ = u.shape
    ug = u.astype(jnp.float32).reshape(B, N, FOURIER_GROUPS, FOURIER_GROUP_DIM)
    f = jnp.fft.fftn(ug, axes=(1, 3), norm='ortho').real
    return f.reshape(B, N, FOURIER_W).astype(u.dtype)


def conv_module(c_a, c_b, w_dw, b_dw, ln_g, ln_b, w_pw):
    u = c_a * jax.nn.sigmoid(c_b)
    u = lax.conv_general_dilated(u, w_dw[:, None, :], window_strides=(1,),
                                 padding=[(CONV_K // 2, CONV_K // 2)],
                                 dimension_numbers=('NWC', 'WIO', 'NWC'),
                                 feature_group_count=CONV_W) + b_dw
    u = jax.nn.silu(layer_norm(u, ln_g, ln_b))
    return u @ w_pw


def branch_tail(attn_o, g_a, u_f, g_f, c_a, c_b, g_c, s_a, s_f, s_c,
                w_four, conv_dw, conv_db, ln_g, ln_b, w_pw, p_attn, p_four, p_conv, w_out):
    B, N = attn_o.shape[:2]
    y_a = (attn_o.reshape(B, N, ATTN_W) * jax.nn.silu(g_a)) @ p_attn
    y_f = ((fourier_mix(u_f) @ w_four) * jax.nn.silu(g_f)) @ p_four
    y_c = (conv_module(c_a, c_b, conv_dw, conv_db, ln_g, ln_b, w_pw) * jax.nn.silu(g_c)) @ p_conv
    merged = jax.nn.sigmoid(s_a) * y_a + jax.nn.sigmoid(s_f) * y_f + jax.nn.sigmoid(s_c) * y_c
    return merged @ w_out


def setup_inputs(seed: int = 0) -> dict:
    key = jax.random.key(seed)
    ks = jax.random.split(key, 20)

    def nrm(k, shape, s):
        return jax.random.normal(k, shape, jnp.float32) * s

    return {
        'x': nrm(ks[0], (BATCH, SEQ, D_MODEL), 1.0),
        'c': nrm(ks[1], (BATCH, D_MODEL), 1.0),
        'ctx': nrm(ks[2], (BATCH, CTX_LEN, D_MODEL), 1.0),
        'c_ctx': nrm(ks[3], (D_MODEL,), 1.0),
        'w_mod': nrm(ks[4], (DEPTH, D_MODEL, 3 * D_MODEL), 0.5 * D_MODEL ** -0.5),
        'b_mod': nrm(ks[5], (DEPTH, 3 * D_MODEL), 0.01),
        'g_pre': 1.0 + nrm(ks[6], (DEPTH, D_MODEL), 0.02),
        'g_post': 1.0 + nrm(ks[7], (DEPTH, D_MODEL), 0.02),
        'w_in': nrm(ks[8], (DEPTH, D_MODEL, IN_W), D_MODEL ** -0.5),
        'rpb': nrm(ks[9], (DEPTH, N_HEADS, 2 * WIN_ROWS_MAX - 1, 2 * WIN_COLS - 1), 0.5),
        'w_four': nrm(ks[10], (DEPTH, FOURIER_W, FOURIER_W), FOURIER_W ** -0.5),
        'conv_dw': nrm(ks[11], (DEPTH, CONV_K, CONV_W), CONV_K ** -0.5),
        'conv_db': nrm(ks[12], (DEPTH, CONV_W), 0.01),
        'conv_ln_g': 1.0 + nrm(ks[13], (DEPTH, CONV_W), 0.02),
        'conv_ln_b': nrm(ks[14], (DEPTH, CONV_W), 0.01),
        'w_pw': nrm(ks[15], (DEPTH, CONV_W, CONV_W), CONV_W ** -0.5),
        'p_attn': nrm(ks[16], (DEPTH, ATTN_W, D_MODEL), ATTN_W ** -0.5),
        'p_four': nrm(ks[17], (DEPTH, FOURIER_W, D_MODEL), FOURIER_W ** -0.5),
        'p_conv': nrm(ks[18], (DEPTH, CONV_W, D_MODEL), CONV_W ** -0.5),
        'w_out': nrm(ks[19], (DEPTH, D_MODEL, D_MODEL), D_MODEL ** -0.5),
    }


def reference(x, c, ctx, c_ctx, w_mod, b_mod, g_pre, g_post, w_in, rpb, w_four,
              conv_dw, conv_db, conv_ln_g, conv_ln_b, w_pw, p_attn, p_four, p_conv, w_out):
    sc = jax.nn.silu(c)
    sc_ctx = jax.nn.silu(c_ctx)
    for l in range(DEPTH):
        tail_w = (w_four[l], conv_dw[l], conv_db[l], conv_ln_g[l], conv_ln_b[l], w_pw[l],
                  p_attn[l], p_four[l], p_conv[l], w_out[l])
        shift, scale, gate = jnp.split(sc @ w_mod[l] + b_mod[l], 3, axis=-1)
        shift_c, scale_c, gate_c = jnp.split(sc_ctx @ w_mod[l] + b_mod[l], 3, axis=-1)
        h = rms_norm(x, g_pre[l]) * (1.0 + scale[:, None, :]) + shift[:, None, :]
        hc = rms_norm(ctx, g_pre[l]) * (1.0 + scale_c) + shift_c
        update_ctx = l < DEPTH - 1
        if update_ctx:
            qc, kc, vc, *rest_c = split_in(hc @ w_in[l])
        else:
            kc, vc = jnp.split(hc @ w_in[l][:, ATTN_W:3 * ATTN_W], 2, axis=-1)
        kc, vc = heads(kc), heads(vc)
        q, k, v, *rest = split_in(h @ w_in[l])
        attn = neighbourhood_attention(heads(q), heads(k), heads(v), kc, vc, rpb[l])
        y = branch_tail(attn, *rest, *tail_w)
        x = x + gate[:, None, :] * rms_norm(y, g_post[l])
        if update_ctx:
            attn_c = context_attention(heads(qc), kc, vc)
            yc = branch_tail(attn_c, *rest_c, *tail_w)
            ctx = ctx + gate_c * rms_norm(yc, g_post[l])
    return x
```

```python
import numpy as np
import ml_dtypes
import concourse.bass as bass
import concourse.mybir as mybir
from concourse.bass_utils import run_bass_kernel_spmd

F32 = mybir.dt.float32
BF16 = mybir.dt.bfloat16
AF = mybir.ActivationFunctionType
ALU = mybir.AluOpType

D = 1024; NL = 4096; NC_ = 256; NT = NL + NC_; DEPTH = 2
W1C = 2304; W2C = 4096
EPS = 1e-6
NEG = -30000.0
DEBUG = False
ACTIVE_LAYERS = 2


class Op:
    __slots__ = ("eng", "fn", "dma", "idx", "waits", "needs_inc", "semi", "val", "slot")


class Sched:
    ENGS = ("tensor", "vector", "scalar", "gpsimd", "sync")
    K = 8
    ROT = 8000

    def __init__(self):
        self.ops = {e: [] for e in self.ENGS}
        self.lw = {}
        self.rd = {}
        self.waited = {e: {} for e in self.ENGS}
        self.waited_dma = {e: set() for e in self.ENGS}
        self.dma_ops = {e: [] for e in self.ENGS}
        self.final = []
        self._pending = {}

    def add(self, eng, fn, reads=(), writes=(), dma=False):
        op = Op()
        op.eng = eng; op.fn = fn; op.dma = dma; op.waits = []; op.needs_inc = dma
        op.semi = 0; op.val = 0; op.slot = 0
        deps = []
        pb = self._pending.pop(eng, None)
        if pb:
            deps.extend(pb)
        for k in reads:
            w = self.lw.get(k)
            if w is not None:
                deps.append(w)
            if k.startswith("pb") or k.startswith("pt"):
                r = self.rd.get(k)
                if r:
                    deps.extend(o for e2, o in r[0].items() if e2 != eng)
        for k in writes:
            w = self.lw.get(k)
            if w is not None:
                deps.append(w)
            r = self.rd.get(k)
            if r:
                deps.extend(r[0].values()); deps.extend(r[1])
        if dma:
            n = len(self.dma_ops[eng])
            op.slot = n % self.K; op.val = 16 * (n // self.K + 1)
            if n >= self.K:
                deps.append(self.dma_ops[eng][n - self.K])
            self.dma_ops[eng].append(op)
        for d in deps:
            if d is op:
                continue
            if d.dma:
                if d in self.waited_dma[eng]:
                    continue
                self.waited_dma[eng].add(d); op.waits.append(d)
            else:
                if d.eng == eng and eng == "tensor":
                    continue
                if self.waited[eng].get(d.eng, -1) >= d.idx:
                    continue
                self.waited[eng][d.eng] = d.idx
                d.needs_inc = True; op.waits.append(d)
        op.idx = len(self.ops[eng]); self.ops[eng].append(op)
        for k in reads:
            r = self.rd.setdefault(k, [{}, []])
            if dma:
                r[1].append(op)
            else:
                r[0][eng] = op
        for k in writes:
            self.lw[k] = op; self.rd[k] = [{}, []]
        return op

    def barrier(self):
        lasts = []
        for e in self.ENGS:
            for o in reversed(self.ops[e]):
                if not o.dma:
                    lasts.append(o)
                    break
            lasts.extend(self.dma_ops[e][-self.K:])
        for e in self.ENGS:
            self._pending[e] = list(lasts)

    def emit(self, nc, block, csem, dsem):
        for e in self.ENGS:
            cnt = 0
            for op in self.ops[e]:
                if not op.dma and op.needs_inc:
                    op.semi = cnt // self.ROT; op.val = cnt % self.ROT + 1; cnt += 1
            assert cnt // self.ROT < len(csem[e]), (e, cnt)
        final = self.final

        def run(e, eng):
            for op in self.ops[e]:
                for d in op.waits:
                    if d.dma:
                        eng.wait_ge(dsem[d.eng][d.slot], d.val)
                    else:
                        eng.wait_ge(csem[d.eng][d.semi], d.val)
                ins = op.fn(eng)
                if op.needs_inc:
                    if op.dma:
                        ins.then_inc(dsem[e][op.slot], 16)
                    else:
                        ins.then_inc(csem[e][op.semi], 1)
            if e == "sync":
                for d in final:
                    eng.wait_ge(dsem[d.eng][d.slot], d.val)

        block.tensor(lambda t: run("tensor", t))
        block.vector(lambda t: run("vector", t))
        block.scalar(lambda t: run("scalar", t))
        block.gpsimd(lambda t: run("gpsimd", t))
        block.sync(lambda t: run("sync", t))


class SBA:
    BASE = 16640
    END = 229376

    def __init__(self, nc):
        self.nc = nc; self.top = self.BASE; self.n = 0

    def alloc(self, name, shape, dtype):
        per = 1
        for s in shape[1:]:
            per *= s
        nbytes = per * (4 if dtype == F32 else 2)
        off = (self.top + 63) // 64 * 64
        assert off + nbytes <= self.END, (name, off, nbytes)
        self.top = off + nbytes
        self.hw = max(getattr(self, "hw", 0), self.top)
        self.n += 1
        key = (name, tuple(shape), str(dtype), off)
        cache = self.__dict__.setdefault("cache", {})
        if key not in cache:
            cache[key] = self.nc.alloc_sbuf_tensor_at(f"{name}{self.n}", list(shape), dtype, offset=off)
        return cache[key]


class _Stop(Exception):
    pass


STOP = None
P2_BLOCKS = None
P2_PART = None


def build(debug=False, nlayers=2):
    nc = bass.Bass("TRN2", target_bir_lowering=False)

    def check(name):
        if STOP == name:
            raise _Stop()
    S = Sched()
    sb = SBA(nc)
    S.sb = sb

    def din(name, shape, dt=F32):
        return nc.dram_tensor(name, list(shape), dt, kind="ExternalInput")

    def dscr(name, shape, dt=BF16):
        return nc.dram_tensor(name, list(shape), dt, kind="ExternalOutput" if debug else "Internal")

    xin = din("xin", [NT, D])
    cfm = din("cfm", [128, 8, 2])
    w_mod = din("w_mod", [DEPTH, D, 3 * D])
    bmod_d = din("bmod", [DEPTH, 128, 24])
    gpre_d = din("gpre", [DEPTH, 128, 8])
    gpost_d = din("gpost", [DEPTH, 128, 8])
    w1_d = din("w1", [DEPTH, D, W1C])
    w2_d = din("w2", [DEPTH, D, W2C])
    tabi_d = din("tabi", [DEPTH, 8, 5, 128, 128])
    tabe_d = din("tabe", [DEPTH, 4, 8, 4, 128, 128])
    wfour_d = din("w_four", [DEPTH, 256, 256])
    dw_d = din("dwfm", [DEPTH, 128, 2, 31])
    cvec_d = din("cvec", [DEPTH, 128, 3, 2])
    wpw_d = din("w_pw", [DEPTH, 256, 256])
    pattn_d = din("p_attn", [DEPTH, 512, D])
    pfour_d = din("p_four", [DEPTH, 256, D])
    pconv_d = din("p_conv", [DEPTH, 256, D])
    wout_d = din("w_out", [DEPTH, D, D])
    identb_d = din("identb", [128, 128], BF16)
    identf_d = din("identf", [128, 128])
    onesf_d = din("onesf", [128, 128])
    cj_d = din("cj", [128, 128], BF16)
    sj_d = din("sj", [128, 128], BF16)
    m1_d = din("m1", [128, 128], BF16)
    g2_d = din("g2", [128, 64, 64], BF16)
    c256_d = din("c256", [128, 2, 256], BF16)
    s256_d = din("s256", [128, 2, 256], BF16)
    yout = nc.dram_tensor("y", [NL, D], F32, kind="ExternalOutput")

    hT_d = dscr("hT_d", [128, 8, NT])
    qT_d = dscr("qT_d", [128, 4, NT])
    kT_d = dscr("kT_d", [128, 4, NT])
    v_d = dscr("v_d", [NT, 520])
    Z_d = dscr("Z_d", [2, NL, 256])
    A_d = dscr("A_d", [2, 64, 64, 256])
    XaT_d = dscr("XaT_d", [128, 4, NT])
    XfT_d = dscr("XfT_d", [128, 2, NT])
    XcT_d = dscr("XcT_d", [128, 2, NT])
    x1_d = dscr("x1_d", [NT, D], F32)
    dbg_mod = dscr("dbg_mod", [128, 24, 2], F32)

    PB = [nc.alloc_psum_tensor(f"pb{i}", [128, 512], F32) for i in range(6)]
    PT = [nc.alloc_psum_tensor(f"pt{i}", [128, 1024], BF16) for i in range(2)]
    pbk = [f"pb{i}" for i in range(6)]
    ptk = [f"pt{i}" for i in range(2)]
    rot = {"pb": 0, "pt": 0, "ev": 0}

    def next_pb():
        i = rot["pb"]; rot["pb"] = (i + 1) % 6
        return PB[i], pbk[i]

    def next_pt():
        i = rot["pt"]; rot["pt"] = (i + 1) % 2
        return PT[i], ptk[i]

    def ev_eng():
        rot["ev"] ^= 1
        return "vector" if rot["ev"] else "scalar"

    def dma(eng, out, in_, reads, writes):
        return S.add(eng, lambda e, o=out, i=in_: e.dma_start(out=o, in_=i), reads, writes, dma=True)

    def copy_op(eng, out, in_, reads, writes):
        if eng == "scalar":
            return S.add(eng, lambda e, o=out, i=in_: e.activation(out=o, in_=i, func=AF.Identity), reads, writes)
        return S.add(eng, lambda e, o=out, i=in_: e.tensor_copy(out=o, in_=i), reads, writes)

    identb = sb.alloc("identb", [128, 128], BF16)
    identf = sb.alloc("identf", [128, 128], F32)
    onesf = sb.alloc("onesf", [128, 128], F32)
    cj = sb.alloc("cj", [128, 128], BF16)
    sj = sb.alloc("sj", [128, 128], BF16)
    m1 = sb.alloc("m1", [128, 128], BF16)
    g2 = sb.alloc("g2", [128, 64, 64], BF16)
    c256 = sb.alloc("c256", [128, 2, 256], BF16)
    s256 = sb.alloc("s256", [128, 2, 256], BF16)
    for t, dsrc, k in ((identb, identb_d, "identb"), (identf, identf_d, "identf"), (onesf, onesf_d, "onesf"),
                       (cj, cj_d, "cj"), (sj, sj_d, "sj"), (m1, m1_d, "m1"), (g2, g2_d, "g2"),
                       (c256, c256_d, "c256"), (s256, s256_d, "s256")):
        dma("sync", t[:], dsrc[:], [], [k])
    craw = sb.alloc("craw", [128, 8, 2], F32)
    sc2 = sb.alloc("sc2", [128, 8, 2], F32)
    dma("sync", craw[:], cfm[:], [], ["craw"])
    S.add("scalar", lambda e: e.activation(out=sc2[:], in_=craw[:], func=AF.Silu), ["craw"], ["sc2"])
    gpost = sb.alloc("gpost", [128, 8], F32)
    mod_l = [sb.alloc("mod", [128, 24, 2], F32) for _ in range(DEPTH)]
    amod_l = [sb.alloc("amod", [128, 8, 2], F32) for _ in range(DEPTH)]
    bmod_l = [sb.alloc("bmodl", [128, 24], F32) for _ in range(DEPTH)]
    gpre_l = [sb.alloc("gprel", [128, 8], F32) for _ in range(DEPTH)]
    ggm = sb.alloc("ggm", [128, 8, 2], F32)
    ggbc = [sb.alloc("ggbc", [128, D], F32) for _ in range(2)]
    dwc = sb.alloc("dwc", [128, 2, 31], F32)
    cvec = sb.alloc("cvec", [128, 3, 2], F32)
    dgt = sb.alloc("dgt", [128, 128], F32)
    epsc = sb.alloc("epsc", [128, 1], F32)
    zc = sb.alloc("zc", [128, 2, 2, 256], BF16)
    S.add("vector", lambda e: e.memset(epsc[:], EPS), [], ["epsc"])
    GLOBAL_TOP = sb.top

    try:
        for l in range(nlayers):
            last = (l == DEPTH - 1)
            xsrc = xin if l == 0 else x1_d
            xsrc_k = "xin" if l == 0 else "x1_d"
            sb.top = GLOBAL_TOP
            dma("sync", gpost[:], gpost_d[l], [], ["gpost"])
            dma("sync", dwc[:], dw_d[l], [], ["dwc"])
            dma("sync", cvec[:], cvec_d[l], [], ["cvec"])
            wm = [sb.alloc("wm", [128, 8, 512], F32) for _ in range(2)]
            psM, psMk = PB[0], pbk[0]
            psMv = psM[:, 0:48].rearrange("p (j v) -> p j v", v=2)

            def mod_loads(ll):
                dma("sync", bmod_l[ll][:], bmod_d[ll], [], [f"bmod{ll}"])
                dma("sync", gpre_l[ll][:], gpre_d[ll], [], [f"gpre{ll}"])

            def mod_slice(ll, s):
                b = s % 2
                wmv = w_mod[ll].rearrange("(k p) c -> p k c", p=128)
                dma("sync", wm[b][:], wmv[:, :, s * 512:(s + 1) * 512], [], [f"wm{b}"])

                def mm_mod(e):
                    ins = None
                    for cc in range(4):
                        j = s * 4 + cc
                        for k in range(8):
                            ins = e.matmul(psMv[:, j, :], lhsT=wm[b][:, k, cc * 128:(cc + 1) * 128], rhs=sc2[:, k, :],
                                           start=(k == 0), stop=(k == 7))
                    return ins
                S.add("tensor", mm_mod, [f"wm{b}", "sc2"], [psMk])
                for v in range(2):
                    S.add("vector", lambda e, v=v: e.tensor_tensor(out=mod_l[ll][:, 4 * s:4 * s + 4, v], in0=psMv[:, 4 * s:4 * s + 4, v],
                                                                    in1=bmod_l[ll][:, 4 * s:4 * s + 4], op=ALU.add),
                          [psMk, f"bmod{ll}"], [f"mod{ll}"])

            def mod_finish(ll):
                for v in range(2):
                    S.add("vector", lambda e, v=v: e.scalar_tensor_tensor(out=amod_l[ll][:, :, v], in0=mod_l[ll][:, 8:16, v], scalar=1.0,
                                                                           in1=gpre_l[ll][:], op0=ALU.add, op1=ALU.mult),
                          [f"mod{ll}", f"gpre{ll}"], [f"amod{ll}"])

            if l == 0:
                mod_loads(0)
                for s_ in range(6):
                    mod_slice(0, s_)
                mod_finish(0)
            mod = mod_l[l]
            amod = amod_l[l]
            for v in range(2):
                S.add("vector", lambda e, v=v, mod=mod: e.tensor_tensor(out=ggm[:, :, v], in0=mod[:, 16:24, v], in1=gpost[:], op=ALU.mult),
                      [f"mod{l}", "gpost"], ["ggm"])
            if debug and l == 0:
                dma("sync", dbg_mod[:], mod[:], [f"mod{l}"], ["dbg_mod"])
            for v in range(2):
                for half in range(2):
                    ps, pk = next_pb()
                    for jj in range(4):
                        j = half * 4 + jj
                        S.add("vector", lambda e, j=j, v=v: e.tensor_scalar(out=dgt[:], in0=identf[:], scalar1=ggm[:, j, v:v + 1],
                                                                              scalar2=None, op0=ALU.mult),
                              ["identf", "ggm"], ["dgt"])
                        S.add("tensor", lambda e, ps=ps, jj=jj: e.matmul(ps[:, jj * 128:(jj + 1) * 128], lhsT=onesf[:], rhs=dgt[:],
                                                                          start=True, stop=True),
                              ["onesf", "dgt"], [pk])
                    copy_op("vector", ggbc[v][:, half * 512:(half + 1) * 512], ps[:, :], [pk], [f"ggbc{v}"])
            MOD_TOP = GLOBAL_TOP
            check(f"mod{l}")

            sb.top = MOD_TOP + 2 * 16384 + 128
            P1_BASE = sb.top
            uT = sb.alloc("uT", [128, 2, NL + 30], BF16)
            uTc = sb.alloc("uTc", [128, 2, NC_ + 30], BF16)
            for j in range(2):
                S.add("gpsimd", lambda e, j=j: e.memset(uT[:, j, 0:15], 0.0), [], ["uT"])
                S.add("gpsimd", lambda e, j=j: e.memset(uT[:, j, NL + 15:NL + 30], 0.0), [], ["uT"])
                S.add("gpsimd", lambda e, j=j: e.memset(uTc[:, j, 0:15], 0.0), [], ["uTc"])
                S.add("gpsimd", lambda e, j=j: e.memset(uTc[:, j, NC_ + 15:NC_ + 30], 0.0), [], ["uTc"])
            CONV_KEEP = sb.top
            w1 = sb.alloc("w1", [128, 8, W1C], BF16)
            w1v = w1_d[l].rearrange("(k p) c -> p k c", p=128)
            for cg in range(5):
                ca, cb_ = cg * 512, min(W1C, (cg + 1) * 512)
                dma("gpsimd", w1[:, :, ca:cb_], w1v[:, :, ca:cb_], [], [f"w1_{cg}"])
            xt = [sb.alloc("xt", [128, D], F32) for _ in range(2)]
            junk = sb.alloc("junk", [128, D], BF16)
            xn = [sb.alloc("xn", [128, D], BF16) for _ in range(2)]
            ssq = [sb.alloc("ssq", [128, 1], F32) for _ in range(2)]
            rstd = [sb.alloc("rstd", [128, 1], F32) for _ in range(2)]
            hT = [sb.alloc("hT", [128, 8, 512], BF16) for _ in range(2)]
            qs = [sb.alloc("qs", [128, 4, 512], BF16) for _ in range(2)]
            ks = [sb.alloc("ks", [128, 4, 512], BF16) for _ in range(2)]
            vt = [sb.alloc("vt", [128, 8, 65], BF16) for _ in range(3)]
            ufT = [sb.alloc("ufT", [128, 2, 512], BF16) for _ in range(2)]
            zst = [sb.alloc("zst", [128, 2, 256], BF16) for _ in range(2)]
            sig = [sb.alloc("sig", [128, 512], F32) for _ in range(2)]
            for i in range(3):
                S.add("gpsimd", lambda e, i=i: e.memset(vt[i][:, :, 64:65], 1.0), [], [f"vt{i}"])
            nblk = 9

            def blk_info(tb):
                ctxb = (tb == 8)
                ntile = 2 if ctxb else 4
                return ctxb, ntile, ntile * 128, (1 if ctxb else 0), tb % 2

            def normA(tb, t):
                ctxb, ntile, ntok, v, hb = blk_info(tb)
                r0 = tb * 512 + t * 128
                xb = (tb * 4 + t) % 2
                dma("sync", xt[xb][:], xsrc[r0:r0 + 128, :], [xsrc_k + f"_{r0 // 128}"], [f"xt{xb}"])
                S.add("vector", lambda e: e.memset(ssq[xb][:], 0.0), [], [f"ssq{xb}"])
                S.add("scalar", lambda e: e.activation(out=junk[:], in_=xt[xb][:], func=AF.Square, accum_out=ssq[xb][:]),
                      [f"xt{xb}", f"ssq{xb}"], ["junk", f"ssq{xb}"])
                S.add("scalar", lambda e: e.activation(out=rstd[xb][:], in_=ssq[xb][:], func=AF.Sqrt, scale=1.0 / D, bias=epsc[:, 0:1]),
                      [f"ssq{xb}", "epsc"], [f"rstd{xb}"])
                S.add("vector", lambda e: e.reciprocal(out=rstd[xb][:], in_=rstd[xb][:]), [f"rstd{xb}"], [f"rstd{xb}"])
                S.add("vector", lambda e: e.tensor_scalar(out=xn[xb][:], in0=xt[xb][:], scalar1=rstd[xb][:, 0:1], scalar2=None, op0=ALU.mult),
                      [f"xt{xb}", f"rstd{xb}"], [f"xn{xb}"])

            def normB(tb, t):
                ctxb, ntile, ntok, v, hb = blk_info(tb)
                xb = (tb * 4 + t) % 2
                pt, ptkey = next_pt()
                ptv = pt[:, :].rearrange("p (k c) -> p k c", c=128)

                def tr8(e):
                    ins = None
                    for k in range(8):
                        ins = e.transpose(ptv[:, k, :], xn[xb][:, k * 128:(k + 1) * 128], identb[:])
                    return ins
                S.add("tensor", tr8, [f"xn{xb}", "identb"], [ptkey])
                for k in range(8):
                    if k % 2 == 0:
                        S.add("vector", lambda e, k=k, mod=mod, amod=amod: e.tensor_scalar(
                            out=hT[hb][:, k, t * 128:(t + 1) * 128], in0=ptv[:, k, :], scalar1=amod[:, k, v:v + 1],
                            scalar2=mod[:, k, v:v + 1], op0=ALU.mult, op1=ALU.add),
                            [ptkey, f"amod{l}", f"mod{l}"], [f"hT{hb}"])
                    else:
                        S.add("scalar", lambda e, k=k, mod=mod, amod=amod: e.activation(
                            out=hT[hb][:, k, t * 128:(t + 1) * 128], in_=ptv[:, k, :], func=AF.Identity,
                            scale=amod[:, k, v:v + 1], bias=mod[:, k, v:v + 1]),
                            [ptkey, f"amod{l}", f"mod{l}"], [f"hT{hb}"])
                if t == ntile - 1:
                    c0 = tb * 512
                    dma("sync", hT_d[:, :, c0:c0 + ntok], hT[hb][:, :, 0:ntok], [f"hT{hb}"], [f"hT_d{tb}"])

            def units(tb):
                ctxb, ntile, ntok, v, hb = blk_info(tb)
                c0 = tb * 512
                us = []

                def proj_fm(col0):
                    ps, pk = next_pb()

                    def f(e):
                        ins = None
                        for k in range(8):
                            ins = e.matmul(ps[:, 0:ntok], lhsT=w1[:, k, col0:col0 + 128], rhs=hT[hb][:, k, 0:ntok],
                                           start=(k == 0), stop=(k == 7))
                        return ins
                    S.add("tensor", f, [f"w1_{col0 // 512}", f"hT{hb}"], [pk])
                    return ps, pk
                need_q = not (ctxb and last)
                for nm, stg, dd, base, need in (("q", qs, qT_d, 0, need_q), ("k", ks, kT_d, 512, True)):
                    if not need:
                        continue
                    for cch in range(4):
                        def u(nm=nm, stg=stg, dd=dd, base=base, cch=cch):
                            ps, pk = proj_fm(base + cch * 128)
                            copy_op(ev_eng(), stg[hb][:, cch, 0:ntok], ps[:, 0:ntok], [pk], [f"{nm}s{hb}"])
                            if cch == 3:
                                dma("sync", dd[:, :, c0:c0 + ntok], stg[hb][:, :, 0:ntok], [f"{nm}s{hb}"], [f"{nm}T_d"])
                        us.append(u)
                for t in range(ntile):
                    def u(t=t):
                        ps, pk = next_pb()
                        vb = (tb * 4 + t) % 3

                        def fv(e):
                            ins = None
                            for k in range(8):
                                ins = e.matmul(ps[:, :], lhsT=hT[hb][:, k, t * 128:(t + 1) * 128], rhs=w1[:, k, 1024:1536],
                                               start=(k == 0), stop=(k == 7))
                            return ins
                        S.add("tensor", fv, ["w1_2", f"hT{hb}"], [pk])
                        copy_op(ev_eng(), vt[vb][:, :, 0:64], ps[:, :].rearrange("p (h d) -> p h d", d=64), [pk], [f"vt{vb}"])
                        r0 = c0 + t * 128
                        dma("sync", v_d[r0:r0 + 128, :], vt[vb][:].rearrange("p h d -> p (h d)"), [f"vt{vb}"], ["v_d"])
                    us.append(u)
                if not (ctxb and last):
                    for j in range(2):
                        def u(j=j):
                            ps, pk = proj_fm(1536 + j * 128)
                            copy_op(ev_eng(), ufT[hb][:, j, 0:ntok], ps[:, 0:ntok], [pk], [f"ufT{hb}"])
                        us.append(u)
                    for t in range(ntile):
                        def u(t=t):
                            ps, pk = next_pb()
                            psz = ps[:, :].rearrange("p (r m) -> p r m", m=256)

                            def fz(e):
                                ins = None
                                for r, tabm in ((0, cj), (1, sj)):
                                    for j in range(2):
                                        ins = e.matmul(psz[:, r, j * 128:(j + 1) * 128], lhsT=ufT[hb][:, j, t * 128:(t + 1) * 128],
                                                       rhs=tabm[:], start=True, stop=True)
                                return ins
                            S.add("tensor", fz, [f"ufT{hb}", "cj", "sj"], [pk])
                            if ctxb:
                                copy_op(ev_eng(), zc[:, t, :, :], psz, [pk], ["zc"])
                            else:
                                zb = t % 2
                                copy_op(ev_eng(), zst[zb][:], psz, [pk], [f"zst{zb}"])
                                r0 = c0 + t * 128
                                for r in range(2):
                                    dma("sync", Z_d[r, r0:r0 + 128, :], zst[zb][:, r, :], [f"zst{zb}"], ["Z_d"])
                        us.append(u)
                    for j in range(2):
                        def u(j=j):
                            sgb = j
                            ps, pk = proj_fm(2048 + j * 128)
                            S.add("scalar", lambda e: e.activation(out=sig[sgb][:, 0:ntok], in_=ps[:, 0:ntok], func=AF.Sigmoid), [pk], [f"sig{sgb}"])
                            ps2, pk2 = proj_fm(1792 + j * 128)
                            dst = uTc[:, j, 15:15 + ntok] if ctxb else uT[:, j, 15 + c0:15 + c0 + ntok]
                            S.add("vector", lambda e: e.tensor_tensor(out=dst, in0=ps2[:, 0:ntok], in1=sig[sgb][:, 0:ntok], op=ALU.mult),
                                  [pk2, f"sig{sgb}"], ["uTc" if ctxb else "uT"])
                        us.append(u)
                return us

            for t in range(4):
                normA(0, t)
                normB(0, t)
            for tb in range(nblk):
                us = units(tb)
                nxt = tb + 1 if tb + 1 < nblk else None
                nt_next = blk_info(nxt)[1] if nxt is not None else 0
                nu = len(us)
                slots = {}
                if nxt is not None:
                    step = max(1, nu // (nt_next + 1))
                    for t in range(nt_next):
                        slots.setdefault(min(nu - 1, step * t), []).append(("A", t))
                        slots.setdefault(min(nu - 1, step * (t + 1)), []).append(("B", t))
                for ui, u in enumerate(us):
                    u()
                    for kind, t in slots.get(ui, []):
                        if kind == "A":
                            normA(nxt, t)
                        else:
                            normB(nxt, t)
            S.barrier()
            check(f"p1{l}")
            sb.top = CONV_KEEP
            wpw = sb.alloc("wpw", [128, 2, 256], BF16)
            dma("gpsimd", wpw[:], wpw_d[l].rearrange("(k p) c -> p k c", p=128), [], ["wpw"])
            acc = [sb.alloc("acc", [128, 2, 512], F32) for _ in range(3)]
            dg = sb.alloc("dg", [128, 62, 128], BF16)
            for j in range(2):
                for kk in range(31):
                    S.add("vector", lambda e, j=j, kk=kk: e.tensor_scalar(out=dg[:, j * 31 + kk, :], in0=identf[:], scalar1=dwc[:, j, kk:kk + 1],
                                                                          scalar2=None, op0=ALU.mult), ["identf", "dwc"], ["dg"])
            sq = sb.alloc("sq", [128, 2, 512], F32)
            ctmp = sb.alloc("ctmp", [128, 512], F32)
            var = sb.alloc("var", [128, 512], F32)
            msq = sb.alloc("msq", [128, 512], F32)
            tn = sb.alloc("tn", [128, 2, 512], F32)
            zz = [sb.alloc("zz", [128, 2, 512], BF16) for _ in range(2)]
            xcs = [sb.alloc("xcs", [128, 2, 512], BF16) for _ in range(2)]
            def convA(tb):
                ctxb = (tb == 8)
                ntok = 256 if ctxb else 512
                c0 = tb * 512
                ab = tb % 3
                zb_ = tb % 2
                src = uTc if ctxb else uT
                off = 0 if ctxb else c0
                skey = "uTc" if ctxb else "uT"
                for j in range(2):
                    ps, pk = next_pb()

                    def fconv(e, ps=ps, j=j, src=src, off=off, ntok=ntok):
                        ins = None
                        for kk in range(31):
                            ins = e.matmul(ps[:, 0:ntok], lhsT=dg[:, j * 31 + kk, :], rhs=src[:, j, off + kk:off + kk + ntok],
                                           start=(kk == 0), stop=(kk == 30))
                        return ins
                    S.add("tensor", fconv, [skey, "dg"], [pk])
                    S.add("vector", lambda e, ps=ps, j=j, ab=ab, zb_=zb_, ntok=ntok: e.tensor_scalar(
                        out=acc[ab][:, j, 0:ntok], in0=ps[:, 0:ntok], scalar1=cvec[:, 0, j:j + 1], scalar2=None, op0=ALU.add),
                        [pk, "cvec"], [f"acc{ab}_{j}"])
            def convB(tb):
                ctxb = (tb == 8)
                ntok = 256 if ctxb else 512
                c0 = tb * 512
                ab = tb % 3
                zb_ = tb % 2
                akeys = [f"acc{ab}_0", f"acc{ab}_1"]
                S.add("scalar", lambda e, ab=ab, zb_=zb_, ntok=ntok: e.activation(out=sq[:, :, 0:ntok], in_=acc[ab][:, :, 0:ntok], func=AF.Square),
                      akeys, ["sq"])
                psm, pkm = next_pb()
                pss, pks = next_pb()

                def fstat(e, psm=psm, pss=pss, ab=ab, zb_=zb_, ntok=ntok):
                    ins = None
                    for j in range(2):
                        ins = e.matmul(psm[:, 0:ntok], lhsT=onesf[:], rhs=acc[ab][:, j, 0:ntok], start=(j == 0), stop=(j == 1))
                    for j in range(2):
                        ins = e.matmul(pss[:, 0:ntok], lhsT=onesf[:], rhs=sq[:, j, 0:ntok], start=(j == 0), stop=(j == 1))
                    return ins
                S.add("tensor", fstat, akeys + ["sq", "onesf"], [pkm, pks])
                S.add("scalar", lambda e, psm=psm, ntok=ntok: e.activation(out=var[:, 0:ntok], in_=psm[:, 0:ntok], func=AF.Square,
                                                                          scale=1.0 / 256), [pkm], ["var"])
                S.add("vector", lambda e, pss=pss, ntok=ntok: e.scalar_tensor_tensor(
                    out=var[:, 0:ntok], in0=pss[:, 0:ntok], scalar=1.0 / 256, in1=var[:, 0:ntok], op0=ALU.mult, op1=ALU.subtract),
                    [pks, "var"], ["var"])
                S.add("scalar", lambda e, ntok=ntok: e.activation(out=var[:, 0:ntok], in_=var[:, 0:ntok], func=AF.Sqrt, bias=epsc[:, 0:1]),
                      ["var", "epsc"], ["var"])
                S.add("vector", lambda e, ntok=ntok: e.reciprocal(out=var[:, 0:ntok], in_=var[:, 0:ntok]), ["var"], ["var"])
                for j in range(2):
                    S.add("vector", lambda e, j=j, psm=psm, ab=ab, zb_=zb_, ntok=ntok: e.scalar_tensor_tensor(
                        out=tn[:, j, 0:ntok], in0=psm[:, 0:ntok], scalar=-1.0 / 256, in1=acc[ab][:, j, 0:ntok],
                        op0=ALU.mult, op1=ALU.add), [pkm] + akeys, [f"tn{j}"])
                    S.add("vector", lambda e, j=j, ntok=ntok: e.tensor_tensor(out=tn[:, j, 0:ntok], in0=tn[:, j, 0:ntok], in1=var[:, 0:ntok],
                                                                            op=ALU.mult), [f"tn{j}", "var"], [f"tn{j}"])
                    S.add("scalar", lambda e, j=j, ab=ab, zb_=zb_, ntok=ntok: e.activation(out=zz[zb_][:, j, 0:ntok], in_=tn[:, j, 0:ntok], func=AF.Silu,
                                                                               scale=cvec[:, 1, j:j + 1], bias=cvec[:, 2, j:j + 1]),
                          [f"tn{j}", "cvec"], [f"zz{zb_}"])
            def convB2(tb):
                ctxb = (tb == 8)
                ntok = 256 if ctxb else 512
                c0 = tb * 512
                ab = tb % 3
                zb_ = tb % 2
                for mc in range(2):
                    ps, pk = next_pb()

                    def fpw(e, ps=ps, mc=mc, ab=ab, zb_=zb_, ntok=ntok):
                        ins = None
                        for j in range(2):
                            ins = e.matmul(ps[:, 0:ntok], lhsT=wpw[:, j, mc * 128:(mc + 1) * 128], rhs=zz[zb_][:, j, 0:ntok],
                                           start=(j == 0), stop=(j == 1))
                        return ins
                    S.add("tensor", fpw, ["wpw", f"zz{zb_}"], [pk])
                    copy_op(ev_eng(), xcs[zb_][:, mc, 0:ntok], ps[:, 0:ntok], [pk], [f"xcs{zb_}"])
                dma("sync", XcT_d[:, :, c0:c0 + ntok], xcs[zb_][:, :, 0:ntok], [f"xcs{zb_}"], ["XcT_d"])

            ncb = nblk if not last else 8
            wfr = sb.alloc("wfr", [128, 2, 256], BF16)
            dma("gpsimd", wfr[:], wfour_d[l].rearrange("(k p) c -> p k c", p=128), [], ["wfr"])
            zstack = sb.alloc("zstack", [128, 16384], BF16)
            astack = zstack[:, :].rearrange("p (k m) -> p k m", m=256)
            asb = [sb.alloc("asb", [128, 2048], BF16) for _ in range(2)]
            fourT = sb.alloc("fourT", [128, 2, NT], BF16)
            xfs = [sb.alloc("xfs", [128, 2, 512], BF16) for _ in range(2)]
            A_dv = A_d.rearrange("r k n m -> (r k) (n m)")
            fT4 = [fourT[:, mc, 0:NL].rearrange("p (a b) -> p a b", b=64) for mc in range(2)]

            def fft_load_z():
                for r in range(2):
                    dma("sync", zstack[r * 64:(r + 1) * 64, :], Z_d[r].rearrange("(a b) m -> a (b m)", b=64), ["Z_d"], ["zstack"])

            def fft_stage1(g0, g1):
                for g in range(g0, g1):
                    gb = g % 2
                    for cbk in range(4):
                        ps, pk = next_pb()
                        col = g * 2048 + cbk * 512
                        S.add("tensor", lambda e, ps=ps, col=col: e.matmul(ps[:, :], lhsT=m1[:], rhs=zstack[:, col:col + 512], start=True, stop=True),
                              ["m1", "zstack"], [pk])
                        copy_op(ev_eng(), asb[gb][:, cbk * 512:(cbk + 1) * 512], ps[:, :], [pk], [f"asb{gb}"])
                    dma("sync", A_dv[:, g * 2048:(g + 1) * 2048], asb[gb][:], [f"asb{gb}"], ["A_d"])

            def fft_load_a():
                for r in range(2):
                    dma("sync", astack[r * 64:(r + 1) * 64, :, :], A_d[r].rearrange("k n m -> n k m"), ["A_d"], ["zstack"])

            def fft_stage2(g0, g1):
                for g in range(g0, g1):
                    for mc in range(2):
                        ps, pk = next_pb()
                        psv = ps[:, :].rearrange("p (a b) -> p a b", b=8)

                        def fs2(e, psv=psv, g=g, mc=mc):
                            ins = None
                            for kl in range(8):
                                k1 = g * 8 + kl
                                ins = e.matmul(psv[:, :, kl], lhsT=astack[:, k1, mc * 128:(mc + 1) * 128], rhs=g2[:, k1, :], start=True, stop=True)
                            return ins
                        S.add("tensor", fs2, ["zstack", "g2"], [pk])
                        copy_op(ev_eng(), fT4[mc][:, :, g * 8:(g + 1) * 8], psv, [pk], ["fourT"])

            def fft_ctx():
                for mc in range(2):
                    ps, pk = next_pb()

                    def fc(e, ps=ps, mc=mc):
                        ins = None
                        n = 0
                        for nt_ in range(2):
                            for r, tabm in ((0, c256), (1, s256)):
                                ins = e.matmul(ps[:, 0:256], lhsT=zc[:, nt_, r, mc * 128:(mc + 1) * 128], rhs=tabm[:, nt_, :],
                                               start=(n == 0), stop=(n == 3))
                                n += 1
                        return ins
                    S.add("tensor", fc, ["zc", "c256", "s256"], [pk])
                    copy_op(ev_eng(), fourT[:, mc, NL:NT], ps[:, 0:256], [pk], ["fourT"])

            def fft_fw(t0, t1):
                for tb in range(t0, t1):
                    ntok = 256 if tb == 8 else 512
                    c0 = tb * 512
                    fb = tb % 2
                    for mc in range(2):
                        ps, pk = next_pb()

                        def fw(e, ps=ps, mc=mc, c0=c0, ntok=ntok):
                            ins = None
                            for j in range(2):
                                ins = e.matmul(ps[:, 0:ntok], lhsT=wfr[:, j, mc * 128:(mc + 1) * 128], rhs=fourT[:, j, c0:c0 + ntok],
                                               start=(j == 0), stop=(j == 1))
                            return ins
                        S.add("tensor", fw, ["wfr", "fourT"], [pk])
                        copy_op(ev_eng(), xfs[fb][:, mc, 0:ntok], ps[:, 0:ntok], [pk], [f"xfs{fb}"])
                    dma("sync", XfT_d[:, :, c0:c0 + ntok], xfs[fb][:, :, 0:ntok], [f"xfs{fb}"], ["XfT_d"])

            nfw = 9 if not last else 8
            fft_sched = {0: [lambda: fft_stage1(0, 4)], 1: [lambda: fft_stage1(4, 8), fft_load_a],
                         3: [lambda: fft_stage2(0, 4)], 4: [lambda: fft_stage2(4, 8)] + ([fft_ctx] if not last else []),
                         5: [lambda: fft_fw(0, 3)], 6: [lambda: fft_fw(3, 6)], 7: [lambda: fft_fw(6, nfw)]}
            fft_load_z()
            convA(0)
            if ncb > 1:
                convA(1)
            for tb in range(ncb):
                convB(tb)
                if tb + 2 < ncb:
                    convA(tb + 2)
                for fn_ in fft_sched.get(tb, []):
                    fn_()
                if l + 1 < nlayers:
                    if tb == 0:
                        mod_loads(l + 1)
                    if tb < 6:
                        mod_slice(l + 1, tb)
                    if tb == 6:
                        mod_finish(l + 1)
                convB2(tb)
            S.barrier()

            check(f"fft{l}")
            sb.top = MOD_TOP
            qz = [sb.alloc("qz", [128, 4, 2, 512], BF16) for _ in range(2)]
            for b_ in range(2):
                S.add("vector", lambda e, b_=b_: e.memset(qz[b_][:], 0.0), [], [f"qz{b_}"])

            def load_q(c):
                ca = c * 512
                n_ = min(NT, ca + 512) - ca
                for par in range(2):
                    dma("sync", qz[c % 2][par * 64:(par + 1) * 64, :, par, 0:n_], qT_d[par * 64:(par + 1) * 64, :, ca:ca + n_],
                        ["qT_d"], [f"qz{c % 2}"])
            ksb = sb.alloc("ksb", [128, 4, NT], BF16)
            vsb = sb.alloc("vsb", [128, 34, 520], BF16)
            vdv = v_d.rearrange("(t p) f -> p t f", p=128)
            for c in [8] + list(range(8)):
                ca = c * 512
                cb_ = min(NT, ca + 512)
                ta, tb_ = c * 4, min(34, c * 4 + 4)
                dma("sync", ksb[:, :, ca:cb_], kT_d[:, :, ca:cb_], ["kT_d"], [f"ksb_{c}"])
                dma("sync", vsb[:, ta:tb_, :], vdv[:, ta:tb_, :], ["v_d"], [f"vsb_{c}"])
            etab = sb.alloc("etab", [128, 8, 5, 128], BF16)
            eedge = sb.alloc("eedge", [128, 8, 4, 128], BF16)
            tstg = [sb.alloc("tstg", [128, 5, 128], F32) for _ in range(2)]
            pT = [sb.alloc("pT", [128, 8, 128], BF16) for _ in range(3)]
            atile = [sb.alloc("atile", [128, 512], BF16) for _ in range(2)]
            xas = [sb.alloc("xas", [128, 4, 512], BF16) for _ in range(2)]
            rec = [sb.alloc("rec", [128, 1], F32) for _ in range(2)]
            for h in range(8):
                tsb = h % 2
                dma("sync", tstg[tsb][:], tabi_d[l, h].rearrange("j k q -> k j q"), [], [f"tstg{tsb}"])
                S.add("scalar", lambda e, h=h, tsb=tsb: e.activation(out=etab[:, h, :, :], in_=tstg[tsb][:], func=AF.Exp),
                      [f"tstg{tsb}"], ["etab"])
            nqb = 34 if not last else 32
            items = []
            hcount = 0
            for i in range(nqb):
                ctxq = (i >= 32)
                r0 = 2 * i
                edge = None
                if ctxq:
                    nwt = 0; kt0 = 0
                elif r0 < 4:
                    nwt = 4; kt0 = 0; edge = r0 // 2
                elif r0 >= 60:
                    nwt = 4; kt0 = 28; edge = 2 + (r0 - 60) // 2
                else:
                    nwt = 5; kt0 = (r0 - 4) // 2
                ktiles = [kt0 + j for j in range(nwt)] + [32, 33]
                for h in range(8):
                    items.append((i, h, edge, nwt, ktiles, hcount))
                    hcount += 1

            nchunk = (nqb + 3) // 4
            load_q(0)

            def stage_a(i, h, edge, nwt, ktiles, cnt_):
                pb_ = cnt_ % 2
                pi_ = cnt_ % 3
                nk = len(ktiles)
                if h == 0 and i % 4 == 0 and i // 4 + 1 < nchunk:
                    load_q(i // 4 + 1)
                if edge is not None and h == 0:
                    for hh in range(8):
                        tsb = hh % 2
                        dma("sync", tstg[tsb][:, 0:4, :], tabe_d[l, edge, hh].rearrange("j k q -> k j q"), [], [f"tstg{tsb}"])
                        S.add("scalar", lambda e, hh=hh, tsb=tsb: e.activation(out=eedge[:, hh, :, :], in_=tstg[tsb][:, 0:4, :], func=AF.Exp),
                              [f"tstg{tsb}"], ["eedge"])
                hp = h // 2; po = (h % 2) * 64
                psA, pkA = PB[2 * pb_], pbk[2 * pb_]
                psB, pkB = PB[2 * pb_ + 1], pbk[2 * pb_ + 1]
                psAv = psA[:, :].rearrange("p (j q) -> p j q", q=128)
                psBv = psB[:, :].rearrange("p (j q) -> p j q", q=128)

                def fqk(e):
                    ins = None
                    for j, kt in enumerate(ktiles):
                        o = psAv[:, j, :] if j < 4 else psBv[:, j - 4, :]
                        ins = e.matmul(o, lhsT=ksb[:, hp, kt * 128:(kt + 1) * 128], rhs=qz[(i // 4) % 2][:, hp, h % 2, (i % 4) * 128:(i % 4 + 1) * 128],
                                       start=True, stop=True)
                    return ins
                S.add("tensor", fqk, sorted({f"ksb_{kt // 4}" for kt in ktiles}) + [f"qz{(i // 4) % 2}"], [pkA, pkB])
                na = min(4, nk)
                S.add("scalar", lambda e: e.activation(out=pT[pi_][:, 0:na, :], in_=psAv[:, 0:na, :], func=AF.Exp, scale=0.125),
                      [pkA], [f"pT{pi_}_lo"])
                if nk > 4:
                    S.add("scalar", lambda e: e.activation(out=pT[pi_][:, 4:nk, :], in_=psBv[:, 0:nk - 4, :], func=AF.Exp, scale=0.125),
                          [pkB], [f"pT{pi_}_hi"])
                if nwt:
                    tab = etab if edge is None else eedge
                    tkey = "etab" if edge is None else "eedge"
                    nlo = min(nwt, 4)
                    S.add("vector", lambda e: e.tensor_tensor(out=pT[pi_][:, 0:nlo, :], in0=pT[pi_][:, 0:nlo, :], in1=tab[:, h, 0:nlo, :], op=ALU.mult),
                          [f"pT{pi_}_lo", tkey], [f"pT{pi_}_lo"])
                    if nwt > 4:
                        S.add("gpsimd", lambda e: e.tensor_tensor(out=pT[pi_][:, 4:nwt, :], in0=pT[pi_][:, 4:nwt, :], in1=tab[:, h, 4:nwt, :], op=ALU.mult),
                              [f"pT{pi_}_hi", tkey], [f"pT{pi_}_hi"])

            def stage_b(i, h, edge, nwt, ktiles, cnt_):
                pb_ = cnt_ % 2
                pi_ = cnt_ % 3
                nk = len(ktiles)
                ab = i % 2
                psO, pkO = PB[4 + pb_], pbk[4 + pb_]

                def fpv(e):
                    ins = None
                    for j, kt in enumerate(ktiles):
                        ins = e.matmul(psO[:, 0:65], lhsT=pT[pi_][:, j, :], rhs=vsb[:, kt, h * 65:(h + 1) * 65], start=(j == 0), stop=(j == nk - 1))
                    return ins
                S.add("tensor", fpv, [f"pT{pi_}_lo", f"pT{pi_}_hi"] + sorted({f"vsb_{kt // 4}" for kt in ktiles}), [pkO])
                S.add("vector", lambda e: e.reciprocal(out=rec[pb_][:], in_=psO[:, 64:65]), [pkO], [f"rec{pb_}"])
                S.add("vector", lambda e: e.tensor_scalar(out=atile[ab][:, h * 64:(h + 1) * 64], in0=psO[:, 0:64], scalar1=rec[pb_][:, 0:1], scalar2=None, op0=ALU.mult),
                      [pkO, f"rec{pb_}"], [f"atile{ab}"])
                if h == 7:
                    pt, ptkey = next_pt()
                    ptv = pt[:, 0:512].rearrange("p (k c) -> p k c", c=128)

                    def tr4(e):
                        ins = None
                        for k in range(4):
                            ins = e.transpose(ptv[:, k, :], atile[ab][:, k * 128:(k + 1) * 128], identb[:])
                        return ins
                    S.add("tensor", tr4, [f"atile{ab}", "identb"], [ptkey])
                    sbi = (i // 4) % 2
                    copy_op("vector", xas[sbi][:, :, (i % 4) * 128:(i % 4 + 1) * 128], ptv, [ptkey], [f"xas{sbi}"])
                    if i % 4 == 3 or i == nqb - 1:
                        c0 = (i // 4) * 512
                        ntok = ((i % 4) + 1) * 128
                        dma("sync", XaT_d[:, :, c0:c0 + ntok], xas[sbi][:, :, 0:ntok], [f"xas{sbi}"], ["XaT_d"])

            for n, it in enumerate(items):
                stage_a(*it)
                if n >= 2:
                    stage_b(*items[n - 2])
            for it in items[-2:]:
                stage_b(*it)
            S.barrier()

            check(f"att{l}")
            sb.top = MOD_TOP
            w2 = sb.alloc("w2", [128, 8, W2C], BF16)
            w2v = w2_d[l].rearrange("(k p) c -> p k c", p=128)
            for cg in range(8):
                dma("gpsimd", w2[:, :, cg * 512:(cg + 1) * 512], w2v[:, :, cg * 512:(cg + 1) * 512], [], [f"w2_{cg}"])
            pat = sb.alloc("pat", [128, 4, D], BF16)
            pfo = sb.alloc("pfo", [128, 2, D], BF16)
            pco = sb.alloc("pco", [128, 2, D], BF16)
            wo = sb.alloc("wo", [128, 8, D], BF16)
            dma("gpsimd", pat[:], pattn_d[l].rearrange("(k p) c -> p k c", p=128), [], ["pat"])
            dma("gpsimd", pfo[:], pfour_d[l].rearrange("(k p) c -> p k c", p=128), [], ["pfo"])
            dma("gpsimd", pco[:], pconv_d[l].rearrange("(k p) c -> p k c", p=128), [], ["pco"])
            dma("gpsimd", wo[:], wout_d[l].rearrange("(k p) c -> p k c", p=128), [], ["wo"])
            h2 = [sb.alloc("h2", [128, 8, 512], BF16) for _ in range(2)]
            xg = [sb.alloc("xg", [128, 8, 512], BF16) for _ in range(2)]
            gsl = sb.alloc("gsl", [128, 512], BF16)
            sgs = [sb.alloc("sgs", [128, 512], F32) for _ in range(3)]
            mtmp = sb.alloc("mtmp", [128, 512], F32)
            macc = sb.alloc("macc", [128, 512], F32)
            mT_ = sb.alloc("mT", [128, 8, 512], BF16); mT = [mT_, mT_]
            xr_ = sb.alloc("xr", [128, D], F32); xr = [xr_, xr_]
            yn = [sb.alloc("yn", [128, D], F32) for _ in range(2)]
            junk2 = sb.alloc("junk2", [128, 512], BF16)
            ss2 = [sb.alloc("ss2", [128, 2], F32) for _ in range(2)]
            rs2 = [sb.alloc("rs2", [128, 1], F32) for _ in range(2)]
            tcount = 0
            for tb in range((9 if not last else 8) if P2_BLOCKS is None else P2_BLOCKS):
                ctxb = (tb == 8)
                ntile = 2 if ctxb else 4
                ntok = ntile * 128
                c0 = tb * 512
                v = 1 if ctxb else 0
                hb = tb % 2
                def p2_loads(tb2):
                    n2 = 256 if tb2 == 8 else 512
                    cc = tb2 * 512
                    b2 = tb2 % 2
                    dma("sync", h2[b2][:, :, 0:n2], hT_d[:, :, cc:cc + n2], [f"hT_d{tb2}"], [f"h2{b2}"])
                    dma("sync", xg[b2][:, 0:4, 0:n2], XaT_d[:, :, cc:cc + n2], ["XaT_d"], [f"xg{b2}"])
                    dma("sync", xg[b2][:, 4:6, 0:n2], XfT_d[:, :, cc:cc + n2], ["XfT_d"], [f"xg{b2}"])
                    dma("sync", xg[b2][:, 6:8, 0:n2], XcT_d[:, :, cc:cc + n2], ["XcT_d"], [f"xg{b2}"])
                np2 = (9 if not last else 8) if P2_BLOCKS is None else P2_BLOCKS
                if tb == 0:
                    p2_loads(0)
                if tb + 1 < np2:
                    p2_loads(tb + 1)

                def proj2(col0, hb=hb, ntok=ntok):
                    ps, pk = next_pb()

                    def f(e, ps=ps, col0=col0):
                        ins = None
                        for k in range(8):
                            ins = e.matmul(ps[:, 0:ntok], lhsT=w2[:, k, col0:col0 + 128], rhs=h2[hb][:, k, 0:ntok], start=(k == 0), stop=(k == 7))
                        return ins
                    S.add("tensor", f, [f"w2_{col0 // 512}", f"h2{hb}"], [pk])
                    return ps, pk
                for cch in range(8):
                    ps, pk = proj2(cch * 128)
                    S.add("scalar", lambda e, ps=ps, ntok=ntok: e.activation(out=gsl[:, 0:ntok], in_=ps[:, 0:ntok], func=AF.Silu), [pk], ["gsl"])
                    S.add("vector", lambda e, cch=cch, hb=hb, ntok=ntok: e.tensor_tensor(out=xg[hb][:, cch, 0:ntok], in0=xg[hb][:, cch, 0:ntok],
                                                                                      in1=gsl[:, 0:ntok], op=ALU.mult), ["gsl", f"xg{hb}"], [f"xg{hb}"])
                check("p2g")
                for oc in range(8):
                    first = True
                    for bi, (wt, wk, kc0, nkc, scol) in enumerate(((pat, "pat", 0, 4, 1024), (pfo, "pfo", 4, 2, 2048), (pco, "pco", 6, 2, 3072))):
                        pss, pks = proj2(scol + oc * 128)
                        S.add("scalar", lambda e, pss=pss, bi=bi, ntok=ntok: e.activation(out=sgs[bi][:, 0:ntok], in_=pss[:, 0:ntok], func=AF.Sigmoid),
                              [pks], [f"sgs{bi}"])
                        psy, pky = next_pb()

                        def fy(e, psy=psy, wt=wt, kc0=kc0, nkc=nkc, oc=oc, hb=hb, ntok=ntok):
                            ins = None
                            for k in range(nkc):
                                ins = e.matmul(psy[:, 0:ntok], lhsT=wt[:, k, oc * 128:(oc + 1) * 128], rhs=xg[hb][:, kc0 + k, 0:ntok],
                                               start=(k == 0), stop=(k == nkc - 1))
                            return ins
                        S.add("tensor", fy, [wk, f"xg{hb}"], [pky])
                        if bi == 0:
                            S.add("vector", lambda e, psy=psy, bi=bi, ntok=ntok: e.tensor_tensor(out=macc[:, 0:ntok], in0=psy[:, 0:ntok], in1=sgs[bi][:, 0:ntok], op=ALU.mult),
                                  [pky, f"sgs{bi}"], ["macc"])
                        elif bi == 1:
                            S.add("vector", lambda e, psy=psy, bi=bi, ntok=ntok: e.tensor_tensor(out=mtmp[:, 0:ntok], in0=psy[:, 0:ntok], in1=sgs[bi][:, 0:ntok], op=ALU.mult),
                                  [pky, f"sgs{bi}"], ["mtmp"])
                            S.add("gpsimd", lambda e, ntok=ntok: e.tensor_tensor(out=macc[:, 0:ntok], in0=macc[:, 0:ntok], in1=mtmp[:, 0:ntok], op=ALU.add),
                                  ["macc", "mtmp"], ["macc"])
                        else:
                            S.add("vector", lambda e, psy=psy, bi=bi, ntok=ntok: e.tensor_tensor(out=mtmp[:, 0:ntok], in0=psy[:, 0:ntok], in1=sgs[bi][:, 0:ntok], op=ALU.mult),
                                  [pky, f"sgs{bi}"], ["mtmp"])
                            S.add("gpsimd", lambda e, oc=oc, hb=hb, ntok=ntok: e.tensor_tensor(out=mT[hb][:, oc, 0:ntok], in0=macc[:, 0:ntok], in1=mtmp[:, 0:ntok], op=ALU.add),
                                  ["macc", "mtmp"], ["mT0"])
                check("p2m")
                for t in range(ntile):
                    r0 = c0 + t * 128
                    xb = tcount % 2
                    dma("sync", xr[xb][:], xsrc[r0:r0 + 128, :], [xsrc_k + f"_{r0 // 128}"], ["xr0"])
                    S.add("vector", lambda e, xb=xb: e.memset(ss2[xb][:], 0.0), [], [f"ss2{xb}"])
                    halves = []
                    for hf in range(2):
                        ps, pk = next_pb()

                        def fo(e, ps=ps, hf=hf, t=t, hb=hb):
                            ins = None
                            for k in range(8):
                                ins = e.matmul(ps[:, :], lhsT=mT[hb][:, k, t * 128:(t + 1) * 128], rhs=wo[:, k, hf * 512:(hf + 1) * 512],
                                               start=(k == 0), stop=(k == 7))
                            return ins
                        S.add("tensor", fo, ["wo", "mT0"], [pk])
                        S.add("scalar", lambda e, ps=ps, xb=xb, hf=hf: e.activation(out=junk2[:], in_=ps[:, :], func=AF.Square, accum_out=ss2[xb][:, hf:hf + 1]),
                              [pk, f"ss2{xb}"], ["junk2", f"ss2{xb}"])
                        copy_op("vector", yn[xb][:, hf * 512:(hf + 1) * 512], ps[:, :], [pk], [f"yn{xb}"])
                    S.add("vector", lambda e, xb=xb: e.tensor_tensor(out=rs2[xb][:], in0=ss2[xb][:, 0:1], in1=ss2[xb][:, 1:2], op=ALU.add),
                          [f"ss2{xb}"], [f"rs2{xb}"])
                    S.add("scalar", lambda e, xb=xb: e.activation(out=rs2[xb][:], in_=rs2[xb][:], func=AF.Sqrt, scale=1.0 / D, bias=epsc[:, 0:1]),
                          [f"rs2{xb}", "epsc"], [f"rs2{xb}"])
                    S.add("vector", lambda e, xb=xb: e.reciprocal(out=rs2[xb][:], in_=rs2[xb][:]), [f"rs2{xb}"], [f"rs2{xb}"])
                    S.add("vector", lambda e, xb=xb, v=v: e.scalar_tensor_tensor(out=yn[xb][:], in0=yn[xb][:], scalar=rs2[xb][:, 0:1], in1=ggbc[v][:],
                                                                              op0=ALU.mult, op1=ALU.mult), [f"yn{xb}", f"rs2{xb}", f"ggbc{v}"], [f"yn{xb}"])
                    S.add("vector", lambda e, xb=xb: e.tensor_tensor(out=yn[xb][:], in0=yn[xb][:], in1=xr[xb][:], op=ALU.add),
                          [f"yn{xb}", "xr0"], [f"yn{xb}"])
                    if last:
                        o = dma("sync", yout[r0:r0 + 128, :], yn[xb][:], [f"yn{xb}"], [f"y_{r0 // 128}"])
                        S.final.append(o)
                    else:
                        dma("sync", x1_d[r0:r0 + 128, :], yn[xb][:], [f"yn{xb}"], [f"x1_d_{r0 // 128}"])
                    tcount += 1
            S.barrier()
    except _Stop:
        pass

    if nlayers < DEPTH:
        pass
    return nc, S


def _emit(nc, S):
    import contextlib
    with contextlib.ExitStack() as st:
        csem = {e: [st.enter_context(nc.semaphore(f"c_{e}_{i}")) for i in range(8)] for e in Sched.ENGS}
        dsem = {e: [st.enter_context(nc.semaphore(f"d_{e}_{i}")) for i in range(Sched.K)] for e in ("sync", "gpsimd", "scalar")}
        block = st.enter_context(nc.Block())
        S.emit(nc, block, csem, dsem)


def _fm(vec, n):
    return np.ascontiguousarray(np.asarray(vec, np.float32).reshape(n, 128).T)


def _consts():
    bf = ml_dtypes.bfloat16
    c = {}
    c["identb"] = np.eye(128, dtype=np.float32).astype(bf)
    c["identf"] = np.eye(128, dtype=np.float32)
    c["onesf"] = np.ones((128, 128), np.float32)
    j = np.arange(64)
    ang = 2 * np.pi * np.outer(j, j) / 64.0
    cjm = np.zeros((128, 128)); sjm = np.zeros((128, 128))
    for b in range(2):
        cjm[b * 64:(b + 1) * 64, b * 64:(b + 1) * 64] = np.cos(ang) / 8
        sjm[b * 64:(b + 1) * 64, b * 64:(b + 1) * 64] = -np.sin(ang) / 8
    c["cj"] = cjm.astype(np.float32).astype(bf); c["sj"] = sjm.astype(np.float32).astype(bf)
    cc = np.cos(ang) / 8; ss = np.sin(ang) / 8
    m1 = np.zeros((128, 128))
    m1[0:64, 0:64] = cc; m1[64:128, 0:64] = ss; m1[0:64, 64:128] = -ss; m1[64:128, 64:128] = cc
    c["m1"] = m1.astype(np.float32).astype(bf)
    n2 = np.arange(64)[:, None, None]; k1 = np.arange(64)[None, :, None]; k2 = np.arange(64)[None, None, :]
    th = 2 * np.pi * (n2 * k2 / 64.0 + n2 * k1 / 4096.0)
    g2 = np.concatenate([np.cos(th) / 8, np.sin(th) / 8], 0)
    c["g2"] = g2.astype(np.float32).astype(bf)
    n = np.arange(256)
    psi = 2 * np.pi * np.outer(n, n) / 256.0
    c["c256"] = np.ascontiguousarray((np.cos(psi) / 16).reshape(2, 128, 256).transpose(1, 0, 2)).astype(np.float32).astype(bf)
    c["s256"] = np.ascontiguousarray((np.sin(psi) / 16).reshape(2, 128, 256).transpose(1, 0, 2)).astype(np.float32).astype(bf)
    return c


def _bias_tables(rpb):
    rpb = np.asarray(rpb, np.float32)
    L = rpb.shape[0]
    cols = np.arange(64)
    cs = np.clip(cols - 8, 0, 48)

    def table(r0, kb, ntile):
        out = np.full((L, 8, ntile, 128, 128), NEG, np.float32)
        for j in range(ntile):
            for krl in range(2):
                kr = kb + 2 * j + krl
                for rl in range(2):
                    r = r0 + rl
                    rs = min(max(r - 4, 0), 56)
                    if not (rs <= kr <= rs + 7) or kr > 63:
                        continue
                    dr = kr - r + 7
                    kc = cols[:, None]; c = cols[None, :]
                    valid = (kc >= cs[None, :]) & (kc <= cs[None, :] + 15)
                    dc = np.clip(kc - c + 15, 0, 30)
                    blk = np.where(valid[None, None], rpb[:, :, dr][:, :, dc], NEG)
                    out[:, :, j, krl * 64:(krl + 1) * 64, rl * 64:(rl + 1) * 64] = blk
        return out
    tabi = table(4, 0, 5)
    edges = [table(0, 0, 4), table(2, 0, 4), table(60, 56, 4), table(62, 56, 4)]
    tabe = np.stack(edges, 1)
    return np.ascontiguousarray(tabi), np.ascontiguousarray(tabe)


_CACHE = {}


def kernel(x, c, ctx, c_ctx, w_mod, b_mod, g_pre, g_post, w_in, rpb, w_four, conv_dw, conv_db, conv_ln_g, conv_ln_b,
           w_pw, p_attn, p_four, p_conv, w_out):
    f32 = np.float32
    x = np.asarray(x, f32); ctx = np.asarray(ctx, f32); c = np.asarray(c, f32); c_ctx = np.asarray(c_ctx, f32)
    w_in = np.asarray(w_in, f32)
    w1 = np.ascontiguousarray(np.concatenate([w_in[:, :, 0:1536], w_in[:, :, 2048:2304], w_in[:, :, 2560:3072]], -1))
    w2 = np.ascontiguousarray(np.concatenate([w_in[:, :, 1536:2048], w_in[:, :, 2304:2560], w_in[:, :, 3072:3328], w_in[:, :, 3328:6400]], -1))
    tabi, tabe = _bias_tables(rpb)
    shared = dict(
        w_mod=np.asarray(w_mod, f32),
        bmod=np.stack([_fm(b_mod[l], 24) for l in range(DEPTH)]),
        gpre=np.stack([_fm(g_pre[l], 8) for l in range(DEPTH)]),
        gpost=np.stack([_fm(g_post[l], 8) for l in range(DEPTH)]),
        w1=w1, w2=w2, tabi=tabi, tabe=tabe,
        w_four=np.asarray(w_four, f32),
        dwfm=np.ascontiguousarray(np.asarray(conv_dw, f32).transpose(0, 2, 1).reshape(DEPTH, 2, 128, 31).transpose(0, 2, 1, 3)),
        cvec=np.stack([np.stack([_fm(conv_db[l], 2), _fm(conv_ln_g[l], 2), _fm(conv_ln_b[l], 2)], 1) for l in range(DEPTH)]),
        w_pw=np.asarray(w_pw, f32), p_attn=np.asarray(p_attn, f32), p_four=np.asarray(p_four, f32),
        p_conv=np.asarray(p_conv, f32), w_out=np.asarray(w_out, f32),
    )
    shared.update(_consts())
    in_maps = []
    for core in range(8):
        b = core % 4
        m = dict(shared)
        m["xin"] = np.ascontiguousarray(np.concatenate([x[b], ctx[b]], 0))
        m["cfm"] = np.ascontiguousarray(np.stack([_fm(c[b], 8), _fm(c_ctx, 8)], -1))
        in_maps.append(m)
    if "nc" not in _CACHE:
        nc, S = build(debug=DEBUG, nlayers=ACTIVE_LAYERS)
        _emit(nc, S)
        _CACHE["nc"] = nc
    nc = _CACHE["nc"]
    res = run_bass_kernel_spmd(nc, in_maps, core_ids=list(range(8)))
    _CACHE["res"] = res
    out = np.stack([np.asarray(res.results[b]["y"], f32) for b in range(4)], 0)
    return out
```

```python
import numpy as np
import ml_dtypes
import concourse.bass as bass
import concourse.mybir as mybir
from concourse.bass_utils import run_bass_kernel_spmd

F32 = mybir.dt.float32
BF16 = mybir.dt.bfloat16
AF = mybir.ActivationFunctionType
ALU = mybir.AluOpType

D = 1024; NL = 4096; NC_ = 256; NT = NL + NC_; DEPTH = 2
W1C = 2304; W2C = 4096
EPS = 1e-6
NEG = -30000.0
DEBUG = False
ACTIVE_LAYERS = 2


class Op:
    __slots__ = ("eng", "fn", "dma", "idx", "waits", "needs_inc", "semi", "val", "slot")


class Sched:
    ENGS = ("tensor", "vector", "scalar", "gpsimd", "sync")
    K = 8
    ROT = 8000

    def __init__(self):
        self.ops = {e: [] for e in self.ENGS}
        self.lw = {}
        self.rd = {}
        self.waited = {e: {} for e in self.ENGS}
        self.waited_dma = {e: set() for e in self.ENGS}
        self.dma_ops = {e: [] for e in self.ENGS}
        self.final = []
        self._pending = {}

    def add(self, eng, fn, reads=(), writes=(), dma=False):
        op = Op()
        op.eng = eng; op.fn = fn; op.dma = dma; op.waits = []; op.needs_inc = dma
        op.semi = 0; op.val = 0; op.slot = 0
        deps = []
        pb = self._pending.pop(eng, None)
        if pb:
            deps.extend(pb)
        for k in reads:
            w = self.lw.get(k)
            if w is not None:
                deps.append(w)
            if k.startswith("pb") or k.startswith("pt"):
                r = self.rd.get(k)
                if r:
                    deps.extend(o for e2, o in r[0].items() if e2 != eng)
        for k in writes:
            w = self.lw.get(k)
            if w is not None:
                deps.append(w)
            r = self.rd.get(k)
            if r:
                deps.extend(r[0].values()); deps.extend(r[1])
        if dma:
            n = len(self.dma_ops[eng])
            op.slot = n % self.K; op.val = 16 * (n // self.K + 1)
            if n >= self.K:
                deps.append(self.dma_ops[eng][n - self.K])
            self.dma_ops[eng].append(op)
        for d in deps:
            if d is op:
                continue
            if d.dma:
                if d in self.waited_dma[eng]:
                    continue
                self.waited_dma[eng].add(d); op.waits.append(d)
            else:
                if d.eng == eng and eng == "tensor":
                    continue
                if self.waited[eng].get(d.eng, -1) >= d.idx:
                    continue
                self.waited[eng][d.eng] = d.idx
                d.needs_inc = True; op.waits.append(d)
        op.idx = len(self.ops[eng]); self.ops[eng].append(op)
        for k in reads:
            r = self.rd.setdefault(k, [{}, []])
            if dma:
                r[1].append(op)
            else:
                r[0][eng] = op
        for k in writes:
            self.lw[k] = op; self.rd[k] = [{}, []]
        return op

    def barrier(self):
        lasts = []
        for e in self.ENGS:
            for o in reversed(self.ops[e]):
                if not o.dma:
                    lasts.append(o)
                    break
            lasts.extend(self.dma_ops[e][-self.K:])
        for e in self.ENGS:
            self._pending[e] = list(lasts)

    def emit(self, nc, block, csem, dsem):
        for e in self.ENGS:
            cnt = 0
            for op in self.ops[e]:
                if not op.dma and op.needs_inc:
                    op.semi = cnt // self.ROT; op.val = cnt % self.ROT + 1; cnt += 1
            assert cnt // self.ROT < len(csem[e]), (e, cnt)
        final = self.final

        def run(e, eng):
            for op in self.ops[e]:
                for d in op.waits:
                    if d.dma:
                        eng.wait_ge(dsem[d.eng][d.slot], d.val)
                    else:
                        eng.wait_ge(csem[d.eng][d.semi], d.val)
                ins = op.fn(eng)
                if op.needs_inc:
                    if op.dma:
                        ins.then_inc(dsem[e][op.slot], 16)
                    else:
                        ins.then_inc(csem[e][op.semi], 1)
            if e == "sync":
                for d in final:
                    eng.wait_ge(dsem[d.eng][d.slot], d.val)

        block.tensor(lambda t: run("tensor", t))
        block.vector(lambda t: run("vector", t))
        block.scalar(lambda t: run("scalar", t))
        block.gpsimd(lambda t: run("gpsimd", t))
        block.sync(lambda t: run("sync", t))


class SBA:
    BASE = 16640
    END = 229376

    def __init__(self, nc):
        self.nc = nc; self.top = self.BASE; self.n = 0

    def alloc(self, name, shape, dtype):
        per = 1
        for s in shape[1:]:
            per *= s
        nbytes = per * (4 if dtype == F32 else 2)
        off = (self.top + 63) // 64 * 64
        assert off + nbytes <= self.END, (name, off, nbytes)
        self.top = off + nbytes
        self.hw = max(getattr(self, "hw", 0), self.top)
        self.n += 1
        key = (name, tuple(shape), str(dtype), off)
        cache = self.__dict__.setdefault("cache", {})
        if key not in cache:
            cache[key] = self.nc.alloc_sbuf_tensor_at(f"{name}{self.n}", list(shape), dtype, offset=off)
        return cache[key]


class _Stop(Exception):
    pass


STOP = None
P2_BLOCKS = None
P2_PART = None


def build(debug=False, nlayers=2):
    nc = bass.Bass("TRN2", target_bir_lowering=False)

    def check(name):
        if STOP == name:
            raise _Stop()
    S = Sched()
    sb = SBA(nc)
    S.sb = sb

    def din(name, shape, dt=F32):
        return nc.dram_tensor(name, list(shape), dt, kind="ExternalInput")

    def dscr(name, shape, dt=BF16):
        return nc.dram_tensor(name, list(shape), dt, kind="ExternalOutput" if debug else "Internal")

    xin = din("xin", [NT, D])
    cfm = din("cfm", [128, 8, 2])
    w_mod = din("w_mod", [DEPTH, D, 3 * D])
    bmod_d = din("bmod", [DEPTH, 128, 24])
    gpre_d = din("gpre", [DEPTH, 128, 8])
    gpost_d = din("gpost", [DEPTH, 128, 8])
    w1_d = din("w1", [DEPTH, D, W1C])
    w2_d = din("w2", [DEPTH, D, W2C])
    tabi_d = din("tabi", [DEPTH, 8, 5, 128, 128])
    tabe_d = din("tabe", [DEPTH, 4, 8, 4, 128, 128])
    wfour_d = din("w_four", [DEPTH, 256, 256])
    dw_d = din("dwfm", [DEPTH, 128, 2, 31])
    cvec_d = din("cvec", [DEPTH, 128, 3, 2])
    wpw_d = din("w_pw", [DEPTH, 256, 256])
    pattn_d = din("p_attn", [DEPTH, 512, D])
    pfour_d = din("p_four", [DEPTH, 256, D])
    pconv_d = din("p_conv", [DEPTH, 256, D])
    wout_d = din("w_out", [DEPTH, D, D])
    identb_d = din("identb", [128, 128], BF16)
    identf_d = din("identf", [128, 128])
    onesf_d = din("onesf", [128, 128])
    cj_d = din("cj", [128, 128], BF16)
    sj_d = din("sj", [128, 128], BF16)
    m1_d = din("m1", [128, 128], BF16)
    g2_d = din("g2", [128, 64, 64], BF16)
    c256_d = din("c256", [128, 2, 256], BF16)
    s256_d = din("s256", [128, 2, 256], BF16)
    yout = nc.dram_tensor("y", [NL, D], F32, kind="ExternalOutput")

    hT_d = dscr("hT_d", [128, 8, NT])
    qT_d = dscr("qT_d", [128, 4, NT])
    kT_d = dscr("kT_d", [128, 4, NT])
    v_d = dscr("v_d", [NT, 520])
    Z_d = dscr("Z_d", [2, NL, 256])
    A_d = dscr("A_d", [2, 64, 64, 256])
    XaT_d = dscr("XaT_d", [128, 4, NT])
    XfT_d = dscr("XfT_d", [128, 2, NT])
    XcT_d = dscr("XcT_d", [128, 2, NT])
    x1_d = dscr("x1_d", [NT, D], F32)
    dbg_mod = dscr("dbg_mod", [128, 24, 2], F32)

    PB = [nc.alloc_psum_tensor(f"pb{i}", [128, 512], F32) for i in range(6)]
    PT = [nc.alloc_psum_tensor(f"pt{i}", [128, 1024], BF16) for i in range(2)]
    pbk = [f"pb{i}" for i in range(6)]
    ptk = [f"pt{i}" for i in range(2)]
    rot = {"pb": 0, "pt": 0, "ev": 0}

    def next_pb():
        i = rot["pb"]; rot["pb"] = (i + 1) % 6
        return PB[i], pbk[i]

    def next_pt():
        i = rot["pt"]; rot["pt"] = (i + 1) % 2
        return PT[i], ptk[i]

    def ev_eng():
        rot["ev"] ^= 1
        return "vector" if rot["ev"] else "scalar"

    def dma(eng, out, in_, reads, writes):
        return S.add(eng, lambda e, o=out, i=in_: e.dma_start(out=o, in_=i), reads, writes, dma=True)

    def copy_op(eng, out, in_, reads, writes):
        if eng == "scalar":
            return S.add(eng, lambda e, o=out, i=in_: e.activation(out=o, in_=i, func=AF.Identity), reads, writes)
        return S.add(eng, lambda e, o=out, i=in_: e.tensor_copy(out=o, in_=i), reads, writes)

    identb = sb.alloc("identb", [128, 128], BF16)
    identf = sb.alloc("identf", [128, 128], F32)
    onesf = sb.alloc("onesf", [128, 128], F32)
    cj = sb.alloc("cj", [128, 128], BF16)
    sj = sb.alloc("sj", [128, 128], BF16)
    m1 = sb.alloc("m1", [128, 128], BF16)
    g2 = sb.alloc("g2", [128, 64, 64], BF16)
    c256 = sb.alloc("c256", [128, 2, 256], BF16)
    s256 = sb.alloc("s256", [128, 2, 256], BF16)
    for t, dsrc, k in ((identb, identb_d, "identb"), (identf, identf_d, "identf"), (onesf, onesf_d, "onesf"),
                       (cj, cj_d, "cj"), (sj, sj_d, "sj"), (m1, m1_d, "m1"), (g2, g2_d, "g2"),
                       (c256, c256_d, "c256"), (s256, s256_d, "s256")):
        dma("sync", t[:], dsrc[:], [], [k])
    craw = sb.alloc("craw", [128, 8, 2], F32)
    sc2 = sb.alloc("sc2", [128, 8, 2], F32)
    dma("sync", craw[:], cfm[:], [], ["craw"])
    S.add("scalar", lambda e: e.activation(out=sc2[:], in_=craw[:], func=AF.Silu), ["craw"], ["sc2"])
    sc2b = sb.alloc("sc2b", [128, 8, 2], BF16)
    S.add("vector", lambda e: e.tensor_copy(out=sc2b[:], in_=sc2[:]), ["sc2"], ["sc2b"])
    gpost = sb.alloc("gpost", [128, 8], F32)
    mod_l = [sb.alloc("mod", [128, 24, 2], F32) for _ in range(DEPTH)]
    amod_l = [sb.alloc("amod", [128, 8, 2], F32) for _ in range(DEPTH)]
    bmod_l = [sb.alloc("bmodl", [128, 24], F32) for _ in range(DEPTH)]
    gpre_l = [sb.alloc("gprel", [128, 8], F32) for _ in range(DEPTH)]
    ggm = sb.alloc("ggm", [128, 8, 2], F32)
    ggbc = [sb.alloc("ggbc", [128, D], F32) for _ in range(2)]
    dwc = sb.alloc("dwc", [128, 2, 31], F32)
    cvec = sb.alloc("cvec", [128, 3, 2], F32)
    dgt = sb.alloc("dgt", [128, 128], F32)
    epsc = sb.alloc("epsc", [128, 1], F32)
    zc = sb.alloc("zc", [128, 2, 2, 256], BF16)
    S.add("vector", lambda e: e.memset(epsc[:], EPS), [], ["epsc"])
    GLOBAL_TOP = sb.top

    try:
        for l in range(nlayers):
            last = (l == DEPTH - 1)
            xsrc = xin if l == 0 else x1_d
            xsrc_k = "xin" if l == 0 else "x1_d"
            sb.top = GLOBAL_TOP
            dma("sync", gpost[:], gpost_d[l], [], ["gpost"])
            dma("sync", dwc[:], dw_d[l], [], ["dwc"])
            dma("sync", cvec[:], cvec_d[l], [], ["cvec"])
            wmf = sb.alloc("wmf", [128, 8, 512], F32)
            wmb = [sb.alloc("wmb", [128, 8, 512], BF16) for _ in range(2)]
            psM, psMk = PB[0], pbk[0]
            psMv = psM[:, 0:48].rearrange("p (j v) -> p j v", v=2)

            def mod_loads(ll):
                dma("sync", bmod_l[ll][:], bmod_d[ll], [], [f"bmod{ll}"])
                dma("sync", gpre_l[ll][:], gpre_d[ll], [], [f"gpre{ll}"])

            def mod_slice(ll, s):
                b = s % 2
                wmv = w_mod[ll].rearrange("(k p) c -> p k c", p=128)
                dma("sync", wmf[:], wmv[:, :, s * 512:(s + 1) * 512], [], ["wmf"])
                S.add("vector", lambda e: e.tensor_copy(out=wmb[b][:, 0:4, :], in_=wmf[:, 0:4, :]), ["wmf"], [f"wmb{b}_lo"])
                S.add("scalar", lambda e: e.activation(out=wmb[b][:, 4:8, :], in_=wmf[:, 4:8, :], func=AF.Identity), ["wmf"], [f"wmb{b}_hi"])

                def mm_mod(e):
                    ins = None
                    for cc in range(4):
                        j = s * 4 + cc
                        for k in range(8):
                            ins = e.matmul(psMv[:, j, :], lhsT=wmb[b][:, k, cc * 128:(cc + 1) * 128], rhs=sc2b[:, k, :],
                                           start=(k == 0), stop=(k == 7))
                    return ins
                S.add("tensor", mm_mod, [f"wmb{b}_lo", f"wmb{b}_hi", "sc2b"], [psMk])
                for v in range(2):
                    S.add("vector", lambda e, v=v: e.tensor_tensor(out=mod_l[ll][:, 4 * s:4 * s + 4, v], in0=psMv[:, 4 * s:4 * s + 4, v],
                                                                    in1=bmod_l[ll][:, 4 * s:4 * s + 4], op=ALU.add),
                          [psMk, f"bmod{ll}"], [f"mod{ll}"])

            def mod_finish(ll):
                for v in range(2):
                    S.add("vector", lambda e, v=v: e.scalar_tensor_tensor(out=amod_l[ll][:, :, v], in0=mod_l[ll][:, 8:16, v], scalar=1.0,
                                                                           in1=gpre_l[ll][:], op0=ALU.add, op1=ALU.mult),
                          [f"mod{ll}", f"gpre{ll}"], [f"amod{ll}"])

            if l == 0:
                mod_loads(0)
                for s_ in range(6):
                    mod_slice(0, s_)
                mod_finish(0)
            mod = mod_l[l]
            amod = amod_l[l]
            for v in range(2):
                S.add("vector", lambda e, v=v, mod=mod: e.tensor_tensor(out=ggm[:, :, v], in0=mod[:, 16:24, v], in1=gpost[:], op=ALU.mult),
                      [f"mod{l}", "gpost"], ["ggm"])
            if debug and l == 0:
                dma("sync", dbg_mod[:], mod[:], [f"mod{l}"], ["dbg_mod"])
            for v in range(2):
                for half in range(2):
                    ps, pk = next_pb()
                    for jj in range(4):
                        j = half * 4 + jj
                        S.add("vector", lambda e, j=j, v=v: e.tensor_scalar(out=dgt[:], in0=identf[:], scalar1=ggm[:, j, v:v + 1],
                                                                              scalar2=None, op0=ALU.mult),
                              ["identf", "ggm"], ["dgt"])
                        S.add("tensor", lambda e, ps=ps, jj=jj: e.matmul(ps[:, jj * 128:(jj + 1) * 128], lhsT=onesf[:], rhs=dgt[:],
                                                                          start=True, stop=True),
                              ["onesf", "dgt"], [pk])
                    copy_op("vector", ggbc[v][:, half * 512:(half + 1) * 512], ps[:, :], [pk], [f"ggbc{v}"])
            MOD_TOP = GLOBAL_TOP
            check(f"mod{l}")

            sb.top = MOD_TOP + 2 * 16384 + 128
            P1_BASE = sb.top
            uT = sb.alloc("uT", [128, 2, NL + 30], BF16)
            uTc = sb.alloc("uTc", [128, 2, NC_ + 30], BF16)
            for j in range(2):
                S.add("gpsimd", lambda e, j=j: e.memset(uT[:, j, 0:15], 0.0), [], ["uT"])
                S.add("gpsimd", lambda e, j=j: e.memset(uT[:, j, NL + 15:NL + 30], 0.0), [], ["uT"])
                S.add("gpsimd", lambda e, j=j: e.memset(uTc[:, j, 0:15], 0.0), [], ["uTc"])
                S.add("gpsimd", lambda e, j=j: e.memset(uTc[:, j, NC_ + 15:NC_ + 30], 0.0), [], ["uTc"])
            CONV_KEEP = sb.top
            w1 = sb.alloc("w1", [128, 8, W1C], BF16)
            w1v = w1_d[l].rearrange("(k p) c -> p k c", p=128)
            for cg in range(5):
                ca, cb_ = cg * 512, min(W1C, (cg + 1) * 512)
                dma("gpsimd", w1[:, :, ca:cb_], w1v[:, :, ca:cb_], [], [f"w1_{cg}"])
            xt = [sb.alloc("xt", [128, D], F32) for _ in range(2)]
            junk = sb.alloc("junk", [128, D], BF16)
            xn = [sb.alloc("xn", [128, D], BF16) for _ in range(2)]
            ssq = [sb.alloc("ssq", [128, 1], F32) for _ in range(2)]
            rstd = [sb.alloc("rstd", [128, 1], F32) for _ in range(2)]
            hT = [sb.alloc("hT", [128, 8, 512], BF16) for _ in range(2)]
            qs = [sb.alloc("qs", [128, 4, 512], BF16) for _ in range(2)]
            ks = [sb.alloc("ks", [128, 4, 512], BF16) for _ in range(2)]
            vt = [sb.alloc("vt", [128, 8, 65], BF16) for _ in range(3)]
            ufT = [sb.alloc("ufT", [128, 2, 512], BF16) for _ in range(2)]
            zst = [sb.alloc("zst", [128, 2, 256], BF16) for _ in range(2)]
            sig = [sb.alloc("sig", [128, 512], F32) for _ in range(2)]
            for i in range(3):
                S.add("gpsimd", lambda e, i=i: e.memset(vt[i][:, :, 64:65], 1.0), [], [f"vt{i}"])
            nblk = 9

            def blk_info(tb):
                ctxb = (tb == 8)
                ntile = 2 if ctxb else 4
                return ctxb, ntile, ntile * 128, (1 if ctxb else 0), tb % 2

            def normA(tb, t):
                ctxb, ntile, ntok, v, hb = blk_info(tb)
                r0 = tb * 512 + t * 128
                xb = (tb * 4 + t) % 2
                dma("sync", xt[xb][:], xsrc[r0:r0 + 128, :], [xsrc_k + f"_{r0 // 128}"], [f"xt{xb}"])
                S.add("vector", lambda e: e.memset(ssq[xb][:], 0.0), [], [f"ssq{xb}"])
                S.add("scalar", lambda e: e.activation(out=junk[:], in_=xt[xb][:], func=AF.Square, accum_out=ssq[xb][:]),
                      [f"xt{xb}", f"ssq{xb}"], ["junk", f"ssq{xb}"])
                S.add("scalar", lambda e: e.activation(out=rstd[xb][:], in_=ssq[xb][:], func=AF.Sqrt, scale=1.0 / D, bias=epsc[:, 0:1]),
                      [f"ssq{xb}", "epsc"], [f"rstd{xb}"])
                S.add("vector", lambda e: e.reciprocal(out=rstd[xb][:], in_=rstd[xb][:]), [f"rstd{xb}"], [f"rstd{xb}"])
                S.add("vector", lambda e: e.tensor_scalar(out=xn[xb][:], in0=xt[xb][:], scalar1=rstd[xb][:, 0:1], scalar2=None, op0=ALU.mult),
                      [f"xt{xb}", f"rstd{xb}"], [f"xn{xb}"])

            def normB(tb, t):
                ctxb, ntile, ntok, v, hb = blk_info(tb)
                xb = (tb * 4 + t) % 2
                pt, ptkey = next_pt()
                ptv = pt[:, :].rearrange("p (k c) -> p k c", c=128)

                def tr8(e):
                    ins = None
                    for k in range(8):
                        ins = e.transpose(ptv[:, k, :], xn[xb][:, k * 128:(k + 1) * 128], identb[:])
                    return ins
                S.add("tensor", tr8, [f"xn{xb}", "identb"], [ptkey])
                for k in range(8):
                    if k % 2 == 0:
                        S.add("vector", lambda e, k=k, mod=mod, amod=amod: e.tensor_scalar(
                            out=hT[hb][:, k, t * 128:(t + 1) * 128], in0=ptv[:, k, :], scalar1=amod[:, k, v:v + 1],
                            scalar2=mod[:, k, v:v + 1], op0=ALU.mult, op1=ALU.add),
                            [ptkey, f"amod{l}", f"mod{l}"], [f"hT{hb}"])
                    else:
                        S.add("scalar", lambda e, k=k, mod=mod, amod=amod: e.activation(
                            out=hT[hb][:, k, t * 128:(t + 1) * 128], in_=ptv[:, k, :], func=AF.Identity,
                            scale=amod[:, k, v:v + 1], bias=mod[:, k, v:v + 1]),
                            [ptkey, f"amod{l}", f"mod{l}"], [f"hT{hb}"])
                if t == ntile - 1:
                    c0 = tb * 512
                    dma("sync", hT_d[:, :, c0:c0 + ntok], hT[hb][:, :, 0:ntok], [f"hT{hb}"], [f"hT_d{tb}"])

            def units(tb):
                ctxb, ntile, ntok, v, hb = blk_info(tb)
                c0 = tb * 512
                us = []

                def proj_fm(col0):
                    ps, pk = next_pb()

                    def f(e):
                        ins = None
                        for k in range(8):
                            ins = e.matmul(ps[:, 0:ntok], lhsT=w1[:, k, col0:col0 + 128], rhs=hT[hb][:, k, 0:ntok],
                                           start=(k == 0), stop=(k == 7))
                        return ins
                    S.add("tensor", f, [f"w1_{col0 // 512}", f"hT{hb}"], [pk])
                    return ps, pk
                need_q = not (ctxb and last)
                for nm, stg, dd, base, need in (("q", qs, qT_d, 0, need_q), ("k", ks, kT_d, 512, True)):
                    if not need:
                        continue
                    for cch in range(4):
                        def u(nm=nm, stg=stg, dd=dd, base=base, cch=cch):
                            ps, pk = proj_fm(base + cch * 128)
                            copy_op(ev_eng(), stg[hb][:, cch, 0:ntok], ps[:, 0:ntok], [pk], [f"{nm}s{hb}"])
                            if cch == 3:
                                dma("sync", dd[:, :, c0:c0 + ntok], stg[hb][:, :, 0:ntok], [f"{nm}s{hb}"], [f"{nm}T_d"])
                        us.append(u)
                for t in range(ntile):
                    def u(t=t):
                        ps, pk = next_pb()
                        vb = (tb * 4 + t) % 3

                        def fv(e):
                            ins = None
                            for k in range(8):
                                ins = e.matmul(ps[:, :], lhsT=hT[hb][:, k, t * 128:(t + 1) * 128], rhs=w1[:, k, 1024:1536],
                                               start=(k == 0), stop=(k == 7))
                            return ins
                        S.add("tensor", fv, ["w1_2", f"hT{hb}"], [pk])
                        copy_op(ev_eng(), vt[vb][:, :, 0:64], ps[:, :].rearrange("p (h d) -> p h d", d=64), [pk], [f"vt{vb}"])
                        r0 = c0 + t * 128
                        dma("sync", v_d[r0:r0 + 128, :], vt[vb][:].rearrange("p h d -> p (h d)"), [f"vt{vb}"], ["v_d"])
                    us.append(u)
                if not (ctxb and last):
                    for j in range(2):
                        def u(j=j):
                            ps, pk = proj_fm(1536 + j * 128)
                            copy_op(ev_eng(), ufT[hb][:, j, 0:ntok], ps[:, 0:ntok], [pk], [f"ufT{hb}"])
                        us.append(u)
                    for t in range(ntile):
                        def u(t=t):
                            ps, pk = next_pb()
                            psz = ps[:, :].rearrange("p (r m) -> p r m", m=256)

                            def fz(e):
                                ins = None
                                for r, tabm in ((0, cj), (1, sj)):
                                    for j in range(2):
                                        ins = e.matmul(psz[:, r, j * 128:(j + 1) * 128], lhsT=ufT[hb][:, j, t * 128:(t + 1) * 128],
                                                       rhs=tabm[:], start=True, stop=True)
                                return ins
                            S.add("tensor", fz, [f"ufT{hb}", "cj", "sj"], [pk])
                            if ctxb:
                                copy_op(ev_eng(), zc[:, t, :, :], psz, [pk], ["zc"])
                            else:
                                zb = t % 2
                                copy_op(ev_eng(), zst[zb][:], psz, [pk], [f"zst{zb}"])
                                r0 = c0 + t * 128
                                for r in range(2):
                                    dma("sync", Z_d[r, r0:r0 + 128, :], zst[zb][:, r, :], [f"zst{zb}"], ["Z_d"])
                        us.append(u)
                    for j in range(2):
                        def u(j=j):
                            sgb = j
                            ps, pk = proj_fm(2048 + j * 128)
                            S.add("scalar", lambda e: e.activation(out=sig[sgb][:, 0:ntok], in_=ps[:, 0:ntok], func=AF.Sigmoid), [pk], [f"sig{sgb}"])
                            ps2, pk2 = proj_fm(1792 + j * 128)
                            dst = uTc[:, j, 15:15 + ntok] if ctxb else uT[:, j, 15 + c0:15 + c0 + ntok]
                            S.add("vector", lambda e: e.tensor_tensor(out=dst, in0=ps2[:, 0:ntok], in1=sig[sgb][:, 0:ntok], op=ALU.mult),
                                  [pk2, f"sig{sgb}"], ["uTc" if ctxb else "uT"])
                        us.append(u)
                return us

            for t in range(4):
                normA(0, t)
                normB(0, t)
            for tb in range(nblk):
                us = units(tb)
                nxt = tb + 1 if tb + 1 < nblk else None
                nt_next = blk_info(nxt)[1] if nxt is not None else 0
                nu = len(us)
                slots = {}
                if nxt is not None:
                    step = max(1, nu // (nt_next + 1))
                    for t in range(nt_next):
                        slots.setdefault(min(nu - 1, step * t), []).append(("A", t))
                        slots.setdefault(min(nu - 1, step * (t + 1)), []).append(("B", t))
                for ui, u in enumerate(us):
                    u()
                    for kind, t in slots.get(ui, []):
                        if kind == "A":
                            normA(nxt, t)
                        else:
                            normB(nxt, t)
            S.barrier()
            check(f"p1{l}")
            sb.top = CONV_KEEP
            wpw = sb.alloc("wpw", [128, 2, 256], BF16)
            dma("gpsimd", wpw[:], wpw_d[l].rearrange("(k p) c -> p k c", p=128), [], ["wpw"])
            acc = [sb.alloc("acc", [128, 2, 512], F32) for _ in range(3)]
            dg = sb.alloc("dg", [128, 62, 128], BF16)
            for j in range(2):
                for kk in range(31):
                    S.add("vector", lambda e, j=j, kk=kk: e.tensor_scalar(out=dg[:, j * 31 + kk, :], in0=identf[:], scalar1=dwc[:, j, kk:kk + 1],
                                                                          scalar2=None, op0=ALU.mult), ["identf", "dwc"], ["dg"])
            sq = sb.alloc("sq", [128, 2, 512], F32)
            ctmp = sb.alloc("ctmp", [128, 512], F32)
            var = sb.alloc("var", [128, 512], F32)
            msq = sb.alloc("msq", [128, 512], F32)
            tn = sb.alloc("tn", [128, 2, 512], F32)
            zz = [sb.alloc("zz", [128, 2, 512], BF16) for _ in range(2)]
            xcs = [sb.alloc("xcs", [128, 2, 512], BF16) for _ in range(2)]
            def convA(tb):
                ctxb = (tb == 8)
                ntok = 256 if ctxb else 512
                c0 = tb * 512
                ab = tb % 3
                zb_ = tb % 2
                src = uTc if ctxb else uT
                off = 0 if ctxb else c0
                skey = "uTc" if ctxb else "uT"
                for j in range(2):
                    ps, pk = next_pb()

                    def fconv(e, ps=ps, j=j, src=src, off=off, ntok=ntok):
                        ins = None
                        for kk in range(31):
                            ins = e.matmul(ps[:, 0:ntok], lhsT=dg[:, j * 31 + kk, :], rhs=src[:, j, off + kk:off + kk + ntok],
                                           start=(kk == 0), stop=(kk == 30))
                        return ins
                    S.add("tensor", fconv, [skey, "dg"], [pk])
                    S.add("vector", lambda e, ps=ps, j=j, ab=ab, zb_=zb_, ntok=ntok: e.tensor_scalar(
                        out=acc[ab][:, j, 0:ntok], in0=ps[:, 0:ntok], scalar1=cvec[:, 0, j:j + 1], scalar2=None, op0=ALU.add),
                        [pk, "cvec"], [f"acc{ab}_{j}"])
            def convB(tb):
                ctxb = (tb == 8)
                ntok = 256 if ctxb else 512
                c0 = tb * 512
                ab = tb % 3
                zb_ = tb % 2
                akeys = [f"acc{ab}_0", f"acc{ab}_1"]
                S.add("scalar", lambda e, ab=ab, zb_=zb_, ntok=ntok: e.activation(out=sq[:, :, 0:ntok], in_=acc[ab][:, :, 0:ntok], func=AF.Square),
                      akeys, ["sq"])
                psm, pkm = next_pb()
                pss, pks = next_pb()

                def fstat(e, psm=psm, pss=pss, ab=ab, zb_=zb_, ntok=ntok):
                    ins = None
                    for j in range(2):
                        ins = e.matmul(psm[:, 0:ntok], lhsT=onesf[:], rhs=acc[ab][:, j, 0:ntok], start=(j == 0), stop=(j == 1))
                    for j in range(2):
                        ins = e.matmul(pss[:, 0:ntok], lhsT=onesf[:], rhs=sq[:, j, 0:ntok], start=(j == 0), stop=(j == 1))
                    return ins
                S.add("tensor", fstat, akeys + ["sq", "onesf"], [pkm, pks])
                S.add("scalar", lambda e, psm=psm, ntok=ntok: e.activation(out=var[:, 0:ntok], in_=psm[:, 0:ntok], func=AF.Square,
                                                                          scale=1.0 / 256), [pkm], ["var"])
                S.add("vector", lambda e, pss=pss, ntok=ntok: e.scalar_tensor_tensor(
                    out=var[:, 0:ntok], in0=pss[:, 0:ntok], scalar=1.0 / 256, in1=var[:, 0:ntok], op0=ALU.mult, op1=ALU.subtract),
                    [pks, "var"], ["var"])
                S.add("scalar", lambda e, ntok=ntok: e.activation(out=var[:, 0:ntok], in_=var[:, 0:ntok], func=AF.Sqrt, bias=epsc[:, 0:1]),
                      ["var", "epsc"], ["var"])
                S.add("vector", lambda e, ntok=ntok: e.reciprocal(out=var[:, 0:ntok], in_=var[:, 0:ntok]), ["var"], ["var"])
                for j in range(2):
                    S.add("vector", lambda e, j=j, psm=psm, ab=ab, zb_=zb_, ntok=ntok: e.scalar_tensor_tensor(
                        out=tn[:, j, 0:ntok], in0=psm[:, 0:ntok], scalar=-1.0 / 256, in1=acc[ab][:, j, 0:ntok],
                        op0=ALU.mult, op1=ALU.add), [pkm] + akeys, [f"tn{j}"])
                    S.add("vector", lambda e, j=j, ntok=ntok: e.tensor_tensor(out=tn[:, j, 0:ntok], in0=tn[:, j, 0:ntok], in1=var[:, 0:ntok],
                                                                            op=ALU.mult), [f"tn{j}", "var"], [f"tn{j}"])
                    S.add("scalar", lambda e, j=j, ab=ab, zb_=zb_, ntok=ntok: e.activation(out=zz[zb_][:, j, 0:ntok], in_=tn[:, j, 0:ntok], func=AF.Silu,
                                                                               scale=cvec[:, 1, j:j + 1], bias=cvec[:, 2, j:j + 1]),
                          [f"tn{j}", "cvec"], [f"zz{zb_}"])
            def convB2(tb):
                ctxb = (tb == 8)
                ntok = 256 if ctxb else 512
                c0 = tb * 512
                ab = tb % 3
                zb_ = tb % 2
                for mc in range(2):
                    ps, pk = next_pb()

                    def fpw(e, ps=ps, mc=mc, ab=ab, zb_=zb_, ntok=ntok):
                        ins = None
                        for j in range(2):
                            ins = e.matmul(ps[:, 0:ntok], lhsT=wpw[:, j, mc * 128:(mc + 1) * 128], rhs=zz[zb_][:, j, 0:ntok],
                                           start=(j == 0), stop=(j == 1))
                        return ins
                    S.add("tensor", fpw, ["wpw", f"zz{zb_}"], [pk])
                    copy_op(ev_eng(), xcs[zb_][:, mc, 0:ntok], ps[:, 0:ntok], [pk], [f"xcs{zb_}"])
                dma("sync", XcT_d[:, :, c0:c0 + ntok], xcs[zb_][:, :, 0:ntok], [f"xcs{zb_}"], ["XcT_d"])

            ncb = nblk if not last else 8
            wfr = sb.alloc("wfr", [128, 2, 256], BF16)
            dma("gpsimd", wfr[:], wfour_d[l].rearrange("(k p) c -> p k c", p=128), [], ["wfr"])
            zstack = sb.alloc("zstack", [128, 16384], BF16)
            astack = zstack[:, :].rearrange("p (k m) -> p k m", m=256)
            asb = [sb.alloc("asb", [128, 2048], BF16) for _ in range(2)]
            fourT = sb.alloc("fourT", [128, 2, NT], BF16)
            xfs = [sb.alloc("xfs", [128, 2, 512], BF16) for _ in range(2)]
            A_dv = A_d.rearrange("r k n m -> (r k) (n m)")
            fT4 = [fourT[:, mc, 0:NL].rearrange("p (a b) -> p a b", b=64) for mc in range(2)]

            def fft_load_z():
                for r in range(2):
                    dma("sync", zstack[r * 64:(r + 1) * 64, :], Z_d[r].rearrange("(a b) m -> a (b m)", b=64), ["Z_d"], ["zstack"])

            def fft_stage1(g0, g1):
                for g in range(g0, g1):
                    gb = g % 2
                    for cbk in range(4):
                        ps, pk = next_pb()
                        col = g * 2048 + cbk * 512
                        S.add("tensor", lambda e, ps=ps, col=col: e.matmul(ps[:, :], lhsT=m1[:], rhs=zstack[:, col:col + 512], start=True, stop=True),
                              ["m1", "zstack"], [pk])
                        copy_op(ev_eng(), asb[gb][:, cbk * 512:(cbk + 1) * 512], ps[:, :], [pk], [f"asb{gb}"])
                    dma("sync", A_dv[:, g * 2048:(g + 1) * 2048], asb[gb][:], [f"asb{gb}"], ["A_d"])

            def fft_load_a():
                for r in range(2):
                    dma("sync", astack[r * 64:(r + 1) * 64, :, :], A_d[r].rearrange("k n m -> n k m"), ["A_d"], ["zstack"])

            def fft_stage2(g0, g1):
                for g in range(g0, g1):
                    for mc in range(2):
                        ps, pk = next_pb()
                        psv = ps[:, :].rearrange("p (a b) -> p a b", b=8)

                        def fs2(e, psv=psv, g=g, mc=mc):
                            ins = None
                            for kl in range(8):
                                k1 = g * 8 + kl
                                ins = e.matmul(psv[:, :, kl], lhsT=astack[:, k1, mc * 128:(mc + 1) * 128], rhs=g2[:, k1, :], start=True, stop=True)
                            return ins
                        S.add("tensor", fs2, ["zstack", "g2"], [pk])
                        copy_op(ev_eng(), fT4[mc][:, :, g * 8:(g + 1) * 8], psv, [pk], ["fourT"])

            def fft_ctx():
                for mc in range(2):
                    ps, pk = next_pb()

                    def fc(e, ps=ps, mc=mc):
                        ins = None
                        n = 0
                        for nt_ in range(2):
                            for r, tabm in ((0, c256), (1, s256)):
                                ins = e.matmul(ps[:, 0:256], lhsT=zc[:, nt_, r, mc * 128:(mc + 1) * 128], rhs=tabm[:, nt_, :],
                                               start=(n == 0), stop=(n == 3))
                                n += 1
                        return ins
                    S.add("tensor", fc, ["zc", "c256", "s256"], [pk])
                    copy_op(ev_eng(), fourT[:, mc, NL:NT], ps[:, 0:256], [pk], ["fourT"])

            def fft_fw(t0, t1):
                for tb in range(t0, t1):
                    ntok = 256 if tb == 8 else 512
                    c0 = tb * 512
                    fb = tb % 2
                    for mc in range(2):
                        ps, pk = next_pb()

                        def fw(e, ps=ps, mc=mc, c0=c0, ntok=ntok):
                            ins = None
                            for j in range(2):
                                ins = e.matmul(ps[:, 0:ntok], lhsT=wfr[:, j, mc * 128:(mc + 1) * 128], rhs=fourT[:, j, c0:c0 + ntok],
                                               start=(j == 0), stop=(j == 1))
                            return ins
                        S.add("tensor", fw, ["wfr", "fourT"], [pk])
                        copy_op(ev_eng(), xfs[fb][:, mc, 0:ntok], ps[:, 0:ntok], [pk], [f"xfs{fb}"])
                    dma("sync", XfT_d[:, :, c0:c0 + ntok], xfs[fb][:, :, 0:ntok], [f"xfs{fb}"], ["XfT_d"])

            nfw = 9 if not last else 8
            fft_sched = {0: [lambda: fft_stage1(0, 4)], 1: [lambda: fft_stage1(4, 8), fft_load_a],
                         3: [lambda: fft_stage2(0, 4)], 4: [lambda: fft_stage2(4, 8)] + ([fft_ctx] if not last else []),
                         5: [lambda: fft_fw(0, 3)], 6: [lambda: fft_fw(3, 6)], 7: [lambda: fft_fw(6, nfw)]}
            fft_load_z()
            convA(0)
            if ncb > 1:
                convA(1)
            for tb in range(ncb):
                convB(tb)
                if tb + 2 < ncb:
                    convA(tb + 2)
                for fn_ in fft_sched.get(tb, []):
                    fn_()
                if l + 1 < nlayers:
                    if tb == 0:
                        mod_loads(l + 1)
                    if tb < 6:
                        mod_slice(l + 1, tb)
                    if tb == 6:
                        mod_finish(l + 1)
                convB2(tb)
            S.barrier()

            check(f"fft{l}")
            sb.top = MOD_TOP
            qz = [sb.alloc("qz", [128, 4, 2, 512], BF16) for _ in range(2)]
            for b_ in range(2):
                S.add("vector", lambda e, b_=b_: e.memset(qz[b_][:], 0.0), [], [f"qz{b_}"])

            def load_q(c):
                ca = c * 512
                n_ = min(NT, ca + 512) - ca
                for par in range(2):
                    dma("sync", qz[c % 2][par * 64:(par + 1) * 64, :, par, 0:n_], qT_d[par * 64:(par + 1) * 64, :, ca:ca + n_],
                        ["qT_d"], [f"qz{c % 2}"])
            ksb = sb.alloc("ksb", [128, 4, NT], BF16)
            vsb = sb.alloc("vsb", [128, 34, 520], BF16)
            vdv = v_d.rearrange("(t p) f -> p t f", p=128)
            for c in [8] + list(range(8)):
                ca = c * 512
                cb_ = min(NT, ca + 512)
                ta, tb_ = c * 4, min(34, c * 4 + 4)
                dma("sync", ksb[:, :, ca:cb_], kT_d[:, :, ca:cb_], ["kT_d"], [f"ksb_{c}"])
                dma("sync", vsb[:, ta:tb_, :], vdv[:, ta:tb_, :], ["v_d"], [f"vsb_{c}"])
            etab = sb.alloc("etab", [128, 8, 5, 128], BF16)
            eedge = sb.alloc("eedge", [128, 8, 4, 128], BF16)
            tstg = [sb.alloc("tstg", [128, 5, 128], F32) for _ in range(2)]
            pT = [sb.alloc("pT", [128, 8, 128], BF16) for _ in range(3)]
            atile = [sb.alloc("atile", [128, 512], BF16) for _ in range(2)]
            xas = [sb.alloc("xas", [128, 4, 512], BF16) for _ in range(2)]
            rec = [sb.alloc("rec", [128, 1], F32) for _ in range(2)]
            for h in range(8):
                tsb = h % 2
                dma("sync", tstg[tsb][:], tabi_d[l, h].rearrange("j k q -> k j q"), [], [f"tstg{tsb}"])
                S.add("scalar", lambda e, h=h, tsb=tsb: e.activation(out=etab[:, h, :, :], in_=tstg[tsb][:], func=AF.Exp),
                      [f"tstg{tsb}"], ["etab"])
            nqb = 34 if not last else 32
            items = []
            hcount = 0
            for i in range(nqb):
                ctxq = (i >= 32)
                r0 = 2 * i
                edge = None
                if ctxq:
                    nwt = 0; kt0 = 0
                elif r0 < 4:
                    nwt = 4; kt0 = 0; edge = r0 // 2
                elif r0 >= 60:
                    nwt = 4; kt0 = 28; edge = 2 + (r0 - 60) // 2
                else:
                    nwt = 5; kt0 = (r0 - 4) // 2
                ktiles = [kt0 + j for j in range(nwt)] + [32, 33]
                for h in range(8):
                    items.append((i, h, edge, nwt, ktiles, hcount))
                    hcount += 1

            nchunk = (nqb + 3) // 4
            load_q(0)

            def stage_a(i, h, edge, nwt, ktiles, cnt_):
                pb_ = cnt_ % 2
                pi_ = cnt_ % 3
                nk = len(ktiles)
                if h == 0 and i % 4 == 0 and i // 4 + 1 < nchunk:
                    load_q(i // 4 + 1)
                if edge is not None and h == 0:
                    for hh in range(8):
                        tsb = hh % 2
                        dma("sync", tstg[tsb][:, 0:4, :], tabe_d[l, edge, hh].rearrange("j k q -> k j q"), [], [f"tstg{tsb}"])
                        S.add("scalar", lambda e, hh=hh, tsb=tsb: e.activation(out=eedge[:, hh, :, :], in_=tstg[tsb][:, 0:4, :], func=AF.Exp),
                              [f"tstg{tsb}"], ["eedge"])
                hp = h // 2; po = (h % 2) * 64
                psA, pkA = PB[2 * pb_], pbk[2 * pb_]
                psB, pkB = PB[2 * pb_ + 1], pbk[2 * pb_ + 1]
                psAv = psA[:, :].rearrange("p (j q) -> p j q", q=128)
                psBv = psB[:, :].rearrange("p (j q) -> p j q", q=128)

                def fqk(e):
                    ins = None
                    for j, kt in enumerate(ktiles):
                        o = psAv[:, j, :] if j < 4 else psBv[:, j - 4, :]
                        ins = e.matmul(o, lhsT=ksb[:, hp, kt * 128:(kt + 1) * 128], rhs=qz[(i // 4) % 2][:, hp, h % 2, (i % 4) * 128:(i % 4 + 1) * 128],
                                       start=True, stop=True)
                    return ins
                S.add("tensor", fqk, sorted({f"ksb_{kt // 4}" for kt in ktiles}) + [f"qz{(i // 4) % 2}"], [pkA, pkB])
                na = min(4, nk)
                S.add("scalar", lambda e: e.activation(out=pT[pi_][:, 0:na, :], in_=psAv[:, 0:na, :], func=AF.Exp, scale=0.125),
                      [pkA], [f"pT{pi_}_lo"])
                if nk > 4:
                    S.add("scalar", lambda e: e.activation(out=pT[pi_][:, 4:nk, :], in_=psBv[:, 0:nk - 4, :], func=AF.Exp, scale=0.125),
                          [pkB], [f"pT{pi_}_hi"])
                if nwt:
                    tab = etab if edge is None else eedge
                    tkey = "etab" if edge is None else "eedge"
                    nlo = min(nwt, 4)
                    S.add("vector", lambda e: e.tensor_tensor(out=pT[pi_][:, 0:nlo, :], in0=pT[pi_][:, 0:nlo, :], in1=tab[:, h, 0:nlo, :], op=ALU.mult),
                          [f"pT{pi_}_lo", tkey], [f"pT{pi_}_lo"])
                    if nwt > 4:
                        S.add("gpsimd", lambda e: e.tensor_tensor(out=pT[pi_][:, 4:nwt, :], in0=pT[pi_][:, 4:nwt, :], in1=tab[:, h, 4:nwt, :], op=ALU.mult),
                              [f"pT{pi_}_hi", tkey], [f"pT{pi_}_hi"])

            def stage_b(i, h, edge, nwt, ktiles, cnt_):
                pb_ = cnt_ % 2
                pi_ = cnt_ % 3
                nk = len(ktiles)
                ab = i % 2
                psO, pkO = PB[4 + pb_], pbk[4 + pb_]

                def fpv(e):
                    ins = None
                    for j, kt in enumerate(ktiles):
                        ins = e.matmul(psO[:, 0:65], lhsT=pT[pi_][:, j, :], rhs=vsb[:, kt, h * 65:(h + 1) * 65], start=(j == 0), stop=(j == nk - 1))
                    return ins
                S.add("tensor", fpv, [f"pT{pi_}_lo", f"pT{pi_}_hi"] + sorted({f"vsb_{kt // 4}" for kt in ktiles}), [pkO])
                S.add("vector", lambda e: e.reciprocal(out=rec[pb_][:], in_=psO[:, 64:65]), [pkO], [f"rec{pb_}"])
                S.add("vector", lambda e: e.tensor_scalar(out=atile[ab][:, h * 64:(h + 1) * 64], in0=psO[:, 0:64], scalar1=rec[pb_][:, 0:1], scalar2=None, op0=ALU.mult),
                      [pkO, f"rec{pb_}"], [f"atile{ab}"])
                if h == 7:
                    pt, ptkey = next_pt()
                    ptv = pt[:, 0:512].rearrange("p (k c) -> p k c", c=128)

                    def tr4(e):
                        ins = None
                        for k in range(4):
                            ins = e.transpose(ptv[:, k, :], atile[ab][:, k * 128:(k + 1) * 128], identb[:])
                        return ins
                    S.add("tensor", tr4, [f"atile{ab}", "identb"], [ptkey])
                    sbi = (i // 4) % 2
                    copy_op("vector", xas[sbi][:, :, (i % 4) * 128:(i % 4 + 1) * 128], ptv, [ptkey], [f"xas{sbi}"])
                    if i % 4 == 3 or i == nqb - 1:
                        c0 = (i // 4) * 512
                        ntok = ((i % 4) + 1) * 128
                        dma("sync", XaT_d[:, :, c0:c0 + ntok], xas[sbi][:, :, 0:ntok], [f"xas{sbi}"], ["XaT_d"])

            for n, it in enumerate(items):
                stage_a(*it)
                if n >= 2:
                    stage_b(*items[n - 2])
            for it in items[-2:]:
                stage_b(*it)
            S.barrier()

            check(f"att{l}")
            sb.top = MOD_TOP
            w2 = sb.alloc("w2", [128, 8, W2C], BF16)
            w2v = w2_d[l].rearrange("(k p) c -> p k c", p=128)
            for cg in range(8):
                dma("gpsimd", w2[:, :, cg * 512:(cg + 1) * 512], w2v[:, :, cg * 512:(cg + 1) * 512], [], [f"w2_{cg}"])
            pat = sb.alloc("pat", [128, 4, D], BF16)
            pfo = sb.alloc("pfo", [128, 2, D], BF16)
            pco = sb.alloc("pco", [128, 2, D], BF16)
            wo = sb.alloc("wo", [128, 8, D], BF16)
            dma("gpsimd", pat[:], pattn_d[l].rearrange("(k p) c -> p k c", p=128), [], ["pat"])
            dma("gpsimd", pfo[:], pfour_d[l].rearrange("(k p) c -> p k c", p=128), [], ["pfo"])
            dma("gpsimd", pco[:], pconv_d[l].rearrange("(k p) c -> p k c", p=128), [], ["pco"])
            dma("gpsimd", wo[:], wout_d[l].rearrange("(k p) c -> p k c", p=128), [], ["wo"])
            h2 = [sb.alloc("h2", [128, 8, 512], BF16) for _ in range(2)]
            xg = [sb.alloc("xg", [128, 8, 512], BF16) for _ in range(2)]
            gsl = sb.alloc("gsl", [128, 512], BF16)
            sgs = [sb.alloc("sgs", [128, 512], F32) for _ in range(3)]
            mtmp = sb.alloc("mtmp", [128, 512], F32)
            macc = sb.alloc("macc", [128, 512], F32)
            mT_ = sb.alloc("mT", [128, 8, 512], BF16); mT = [mT_, mT_]
            xr_ = sb.alloc("xr", [128, D], F32); xr = [xr_, xr_]
            yn = [sb.alloc("yn", [128, D], F32) for _ in range(2)]
            junk2 = sb.alloc("junk2", [128, 512], BF16)
            ss2 = [sb.alloc("ss2", [128, 2], F32) for _ in range(2)]
            rs2 = [sb.alloc("rs2", [128, 1], F32) for _ in range(2)]
            tcount = 0
            for tb in range((9 if not last else 8) if P2_BLOCKS is None else P2_BLOCKS):
                ctxb = (tb == 8)
                ntile = 2 if ctxb else 4
                ntok = ntile * 128
                c0 = tb * 512
                v = 1 if ctxb else 0
                hb = tb % 2
                def p2_loads(tb2):
                    n2 = 256 if tb2 == 8 else 512
                    cc = tb2 * 512
                    b2 = tb2 % 2
                    dma("sync", h2[b2][:, :, 0:n2], hT_d[:, :, cc:cc + n2], [f"hT_d{tb2}"], [f"h2{b2}"])
                    dma("sync", xg[b2][:, 0:4, 0:n2], XaT_d[:, :, cc:cc + n2], ["XaT_d"], [f"xg{b2}"])
                    dma("sync", xg[b2][:, 4:6, 0:n2], XfT_d[:, :, cc:cc + n2], ["XfT_d"], [f"xg{b2}"])
                    dma("sync", xg[b2][:, 6:8, 0:n2], XcT_d[:, :, cc:cc + n2], ["XcT_d"], [f"xg{b2}"])
                np2 = (9 if not last else 8) if P2_BLOCKS is None else P2_BLOCKS
                if tb == 0:
                    p2_loads(0)
                if tb + 1 < np2:
                    p2_loads(tb + 1)

                def proj2(col0, hb=hb, ntok=ntok):
                    ps, pk = next_pb()

                    def f(e, ps=ps, col0=col0):
                        ins = None
                        for k in range(8):
                            ins = e.matmul(ps[:, 0:ntok], lhsT=w2[:, k, col0:col0 + 128], rhs=h2[hb][:, k, 0:ntok], start=(k == 0), stop=(k == 7))
                        return ins
                    S.add("tensor", f, [f"w2_{col0 // 512}", f"h2{hb}"], [pk])
                    return ps, pk
                for cch in range(8):
                    ps, pk = proj2(cch * 128)
                    S.add("scalar", lambda e, ps=ps, ntok=ntok: e.activation(out=gsl[:, 0:ntok], in_=ps[:, 0:ntok], func=AF.Silu), [pk], ["gsl"])
                    S.add("vector", lambda e, cch=cch, hb=hb, ntok=ntok: e.tensor_tensor(out=xg[hb][:, cch, 0:ntok], in0=xg[hb][:, cch, 0:ntok],
                                                                                      in1=gsl[:, 0:ntok], op=ALU.mult), ["gsl", f"xg{hb}"], [f"xg{hb}"])
                check("p2g")
                for oc in range(8):
                    first = True
                    for bi, (wt, wk, kc0, nkc, scol) in enumerate(((pat, "pat", 0, 4, 1024), (pfo, "pfo", 4, 2, 2048), (pco, "pco", 6, 2, 3072))):
                        pss, pks = proj2(scol + oc * 128)
                        S.add("scalar", lambda e, pss=pss, bi=bi, ntok=ntok: e.activation(out=sgs[bi][:, 0:ntok], in_=pss[:, 0:ntok], func=AF.Sigmoid),
                              [pks], [f"sgs{bi}"])
                        psy, pky = next_pb()

                        def fy(e, psy=psy, wt=wt, kc0=kc0, nkc=nkc, oc=oc, hb=hb, ntok=ntok):
                            ins = None
                            for k in range(nkc):
                                ins = e.matmul(psy[:, 0:ntok], lhsT=wt[:, k, oc * 128:(oc + 1) * 128], rhs=xg[hb][:, kc0 + k, 0:ntok],
                                               start=(k == 0), stop=(k == nkc - 1))
                            return ins
                        S.add("tensor", fy, [wk, f"xg{hb}"], [pky])
                        if bi == 0:
                            S.add("vector", lambda e, psy=psy, bi=bi, ntok=ntok: e.tensor_tensor(out=macc[:, 0:ntok], in0=psy[:, 0:ntok], in1=sgs[bi][:, 0:ntok], op=ALU.mult),
                                  [pky, f"sgs{bi}"], ["macc"])
                        elif bi == 1:
                            S.add("vector", lambda e, psy=psy, bi=bi, ntok=ntok: e.tensor_tensor(out=mtmp[:, 0:ntok], in0=psy[:, 0:ntok], in1=sgs[bi][:, 0:ntok], op=ALU.mult),
                                  [pky, f"sgs{bi}"], ["mtmp"])
                            S.add("gpsimd", lambda e, ntok=ntok: e.tensor_tensor(out=macc[:, 0:ntok], in0=macc[:, 0:ntok], in1=mtmp[:, 0:ntok], op=ALU.add),
                                  ["macc", "mtmp"], ["macc"])
                        else:
                            S.add("vector", lambda e, psy=psy, bi=bi, ntok=ntok: e.tensor_tensor(out=mtmp[:, 0:ntok], in0=psy[:, 0:ntok], in1=sgs[bi][:, 0:ntok], op=ALU.mult),
                                  [pky, f"sgs{bi}"], ["mtmp"])
                            S.add("gpsimd", lambda e, oc=oc, hb=hb, ntok=ntok: e.tensor_tensor(out=mT[hb][:, oc, 0:ntok], in0=macc[:, 0:ntok], in1=mtmp[:, 0:ntok], op=ALU.add),
                                  ["macc", "mtmp"], ["mT0"])
                check("p2m")
                for t in range(ntile):
                    r0 = c0 + t * 128
                    xb = tcount % 2
                    dma("sync", xr[xb][:], xsrc[r0:r0 + 128, :], [xsrc_k + f"_{r0 // 128}"], ["xr0"])
                    S.add("vector", lambda e, xb=xb: e.memset(ss2[xb][:], 0.0), [], [f"ss2{xb}"])
                    halves = []
                    for hf in range(2):
                        ps, pk = next_pb()

                        def fo(e, ps=ps, hf=hf, t=t, hb=hb):
                            ins = None
                            for k in range(8):
                                ins = e.matmul(ps[:, :], lhsT=mT[hb][:, k, t * 128:(t + 1) * 128], rhs=wo[:, k, hf * 512:(hf + 1) * 512],
                                               start=(k == 0), stop=(k == 7))
                            return ins
                        S.add("tensor", fo, ["wo", "mT0"], [pk])
                        S.add("scalar", lambda e, ps=ps, xb=xb, hf=hf: e.activation(out=junk2[:], in_=ps[:, :], func=AF.Square, accum_out=ss2[xb][:, hf:hf + 1]),
                              [pk, f"ss2{xb}"], ["junk2", f"ss2{xb}"])
                        copy_op("vector", yn[xb][:, hf * 512:(hf + 1) * 512], ps[:, :], [pk], [f"yn{xb}"])
                    S.add("vector", lambda e, xb=xb: e.tensor_tensor(out=rs2[xb][:], in0=ss2[xb][:, 0:1], in1=ss2[xb][:, 1:2], op=ALU.add),
                          [f"ss2{xb}"], [f"rs2{xb}"])
                    S.add("scalar", lambda e, xb=xb: e.activation(out=rs2[xb][:], in_=rs2[xb][:], func=AF.Sqrt, scale=1.0 / D, bias=epsc[:, 0:1]),
                          [f"rs2{xb}", "epsc"], [f"rs2{xb}"])
                    S.add("vector", lambda e, xb=xb: e.reciprocal(out=rs2[xb][:], in_=rs2[xb][:]), [f"rs2{xb}"], [f"rs2{xb}"])
                    S.add("vector", lambda e, xb=xb, v=v: e.scalar_tensor_tensor(out=yn[xb][:], in0=yn[xb][:], scalar=rs2[xb][:, 0:1], in1=ggbc[v][:],
                                                                              op0=ALU.mult, op1=ALU.mult), [f"yn{xb}", f"rs2{xb}", f"ggbc{v}"], [f"yn{xb}"])
                    S.add("vector", lambda e, xb=xb: e.tensor_tensor(out=yn[xb][:], in0=yn[xb][:], in1=xr[xb][:], op=ALU.add),
                          [f"yn{xb}", "xr0"], [f"yn{xb}"])
                    if last:
                        o = dma("sync", yout[r0:r0 + 128, :], yn[xb][:], [f"yn{xb}"], [f"y_{r0 // 128}"])
                        S.final.append(o)
                    else:
                        dma("sync", x1_d[r0:r0 + 128, :], yn[xb][:], [f"yn{xb}"], [f"x1_d_{r0 // 128}"])
                    tcount += 1
            S.barrier()
    except _Stop:
        pass

    if nlayers < DEPTH:
        pass
    return nc, S


def _emit(nc, S):
    import contextlib
    with contextlib.ExitStack() as st:
        csem = {e: [st.enter_context(nc.semaphore(f"c_{e}_{i}")) for i in range(8)] for e in Sched.ENGS}
        dsem = {e: [st.enter_context(nc.semaphore(f"d_{e}_{i}")) for i in range(Sched.K)] for e in ("sync", "gpsimd", "scalar")}
        block = st.enter_context(nc.Block())
        S.emit(nc, block, csem, dsem)


def _fm(vec, n):
    return np.ascontiguousarray(np.asarray(vec, np.float32).reshape(n, 128).T)


def _consts():
    bf = ml_dtypes.bfloat16
    c = {}
    c["identb"] = np.eye(128, dtype=np.float32).astype(bf)
    c["identf"] = np.eye(128, dtype=np.float32)
    c["onesf"] = np.ones((128, 128), np.float32)
    j = np.arange(64)
    ang = 2 * np.pi * np.outer(j, j) / 64.0
    cjm = np.zeros((128, 128)); sjm = np.zeros((128, 128))
    for b in range(2):
        cjm[b * 64:(b + 1) * 64, b * 64:(b + 1) * 64] = np.cos(ang) / 8
        sjm[b * 64:(b + 1) * 64, b * 64:(b + 1) * 64] = -np.sin(ang) / 8
    c["cj"] = cjm.astype(np.float32).astype(bf); c["sj"] = sjm.astype(np.float32).astype(bf)
    cc = np.cos(ang) / 8; ss = np.sin(ang) / 8
    m1 = np.zeros((128, 128))
    m1[0:64, 0:64] = cc; m1[64:128, 0:64] = ss; m1[0:64, 64:128] = -ss; m1[64:128, 64:128] = cc
    c["m1"] = m1.astype(np.float32).astype(bf)
    n2 = np.arange(64)[:, None, None]; k1 = np.arange(64)[None, :, None]; k2 = np.arange(64)[None, None, :]
    th = 2 * np.pi * (n2 * k2 / 64.0 + n2 * k1 / 4096.0)
    g2 = np.concatenate([np.cos(th) / 8, np.sin(th) / 8], 0)
    c["g2"] = g2.astype(np.float32).astype(bf)
    n = np.arange(256)
    psi = 2 * np.pi * np.outer(n, n) / 256.0
    c["c256"] = np.ascontiguousarray((np.cos(psi) / 16).reshape(2, 128, 256).transpose(1, 0, 2)).astype(np.float32).astype(bf)
    c["s256"] = np.ascontiguousarray((np.sin(psi) / 16).reshape(2, 128, 256).transpose(1, 0, 2)).astype(np.float32).astype(bf)
    return c


def _bias_tables(rpb):
    rpb = np.asarray(rpb, np.float32)
    L = rpb.shape[0]
    cols = np.arange(64)
    cs = np.clip(cols - 8, 0, 48)

    def table(r0, kb, ntile):
        out = np.full((L, 8, ntile, 128, 128), NEG, np.float32)
        for j in range(ntile):
            for krl in range(2):
                kr = kb + 2 * j + krl
                for rl in range(2):
                    r = r0 + rl
                    rs = min(max(r - 4, 0), 56)
                    if not (rs <= kr <= rs + 7) or kr > 63:
                        continue
                    dr = kr - r + 7
                    kc = cols[:, None]; c = cols[None, :]
                    valid = (kc >= cs[None, :]) & (kc <= cs[None, :] + 15)
                    dc = np.clip(kc - c + 15, 0, 30)
                    blk = np.where(valid[None, None], rpb[:, :, dr][:, :, dc], NEG)
                    out[:, :, j, krl * 64:(krl + 1) * 64, rl * 64:(rl + 1) * 64] = blk
        return out
    tabi = table(4, 0, 5)
    edges = [table(0, 0, 4), table(2, 0, 4), table(60, 56, 4), table(62, 56, 4)]
    tabe = np.stack(edges, 1)
    return np.ascontiguousarray(tabi), np.ascontiguousarray(tabe)


_CACHE = {}


def kernel(x, c, ctx, c_ctx, w_mod, b_mod, g_pre, g_post, w_in, rpb, w_four, conv_dw, conv_db, conv_ln_g, conv_ln_b,
           w_pw, p_attn, p_four, p_conv, w_out):
    f32 = np.float32
    x = np.asarray(x, f32); ctx = np.asarray(ctx, f32); c = np.asarray(c, f32); c_ctx = np.asarray(c_ctx, f32)
    w_in = np.asarray(w_in, f32)
    w1 = np.ascontiguousarray(np.concatenate([w_in[:, :, 0:1536], w_in[:, :, 2048:2304], w_in[:, :, 2560:3072]], -1))
    w2 = np.ascontiguousarray(np.concatenate([w_in[:, :, 1536:2048], w_in[:, :, 2304:2560], w_in[:, :, 3072:3328], w_in[:, :, 3328:6400]], -1))
    tabi, tabe = _bias_tables(rpb)
    shared = dict(
        w_mod=np.asarray(w_mod, f32),
        bmod=np.stack([_fm(b_mod[l], 24) for l in range(DEPTH)]),
        gpre=np.stack([_fm(g_pre[l], 8) for l in range(DEPTH)]),
        gpost=np.stack([_fm(g_post[l], 8) for l in range(DEPTH)]),
        w1=w1, w2=w2, tabi=tabi, tabe=tabe,
        w_four=np.asarray(w_four, f32),
        dwfm=np.ascontiguousarray(np.asarray(conv_dw, f32).transpose(0, 2, 1).reshape(DEPTH, 2, 128, 31).transpose(0, 2, 1, 3)),
        cvec=np.stack([np.stack([_fm(conv_db[l], 2), _fm(conv_ln_g[l], 2), _fm(conv_ln_b[l], 2)], 1) for l in range(DEPTH)]),
        w_pw=np.asarray(w_pw, f32), p_attn=np.asarray(p_attn, f32), p_four=np.asarray(p_four, f32),
        p_conv=np.asarray(p_conv, f32), w_out=np.asarray(w_out, f32),
    )
    shared.update(_consts())
    in_maps = []
    for core in range(8):
        b = core % 4
        m = dict(shared)
        m["xin"] = np.ascontiguousarray(np.concatenate([x[b], ctx[b]], 0))
        m["cfm"] = np.ascontiguousarray(np.stack([_fm(c[b], 8), _fm(c_ctx, 8)], -1))
        in_maps.append(m)
    if "nc" not in _CACHE:
        nc, S = build(debug=DEBUG, nlayers=ACTIVE_LAYERS)
        _emit(nc, S)
        _CACHE["nc"] = nc
    nc = _CACHE["nc"]
    res = run_bass_kernel_spmd(nc, in_maps, core_ids=list(range(8)))
    _CACHE["res"] = res
    out = np.stack([np.asarray(res.results[b]["y"], f32) for b in range(4)], 0)
    return out
```

```python
import numpy as np
import ml_dtypes
import concourse.bass as bass
import concourse.mybir as mybir
from concourse.bass_utils import run_bass_kernel_spmd

F32 = mybir.dt.float32
BF16 = mybir.dt.bfloat16
AF = mybir.ActivationFunctionType
ALU = mybir.AluOpType

D = 1024; NL = 4096; NC_ = 256; NT = NL + NC_; DEPTH = 2
W1C = 2304; W2C = 4096
EPS = 1e-6
NEG = -30000.0
DEBUG = False
ACTIVE_LAYERS = 2


class Op:
    __slots__ = ("eng", "fn", "dma", "idx", "waits", "needs_inc", "semi", "val", "slot")


class Sched:
    ENGS = ("tensor", "vector", "scalar", "gpsimd", "sync")
    K = 8
    ROT = 8000

    def __init__(self):
        self.ops = {e: [] for e in self.ENGS}
        self.lw = {}
        self.rd = {}
        self.waited = {e: {} for e in self.ENGS}
        self.waited_dma = {e: set() for e in self.ENGS}
        self.dma_ops = {e: [] for e in self.ENGS}
        self.final = []
        self._pending = {}

    def add(self, eng, fn, reads=(), writes=(), dma=False):
        op = Op()
        op.eng = eng; op.fn = fn; op.dma = dma; op.waits = []; op.needs_inc = dma
        op.semi = 0; op.val = 0; op.slot = 0
        deps = []
        pb = self._pending.pop(eng, None)
        if pb:
            deps.extend(pb)
        for k in reads:
            w = self.lw.get(k)
            if w is not None:
                deps.append(w)
            if k.startswith("pb") or k.startswith("pt"):
                r = self.rd.get(k)
                if r:
                    deps.extend(o for e2, o in r[0].items() if e2 != eng)
        for k in writes:
            w = self.lw.get(k)
            if w is not None:
                deps.append(w)
            r = self.rd.get(k)
            if r:
                deps.extend(r[0].values()); deps.extend(r[1])
        if dma:
            n = len(self.dma_ops[eng])
            op.slot = n % self.K; op.val = 16 * (n // self.K + 1)
            if n >= self.K:
                deps.append(self.dma_ops[eng][n - self.K])
            self.dma_ops[eng].append(op)
        for d in deps:
            if d is op:
                continue
            if d.dma:
                if d in self.waited_dma[eng]:
                    continue
                self.waited_dma[eng].add(d); op.waits.append(d)
            else:
                if d.eng == eng and eng == "tensor":
                    continue
                if self.waited[eng].get(d.eng, -1) >= d.idx:
                    continue
                self.waited[eng][d.eng] = d.idx
                d.needs_inc = True; op.waits.append(d)
        op.idx = len(self.ops[eng]); self.ops[eng].append(op)
        for k in reads:
            r = self.rd.setdefault(k, [{}, []])
            if dma:
                r[1].append(op)
            else:
                r[0][eng] = op
        for k in writes:
            self.lw[k] = op; self.rd[k] = [{}, []]
        return op

    def barrier(self):
        lasts = []
        for e in self.ENGS:
            for o in reversed(self.ops[e]):
                if not o.dma:
                    lasts.append(o)
                    break
            lasts.extend(self.dma_ops[e][-self.K:])
        for e in self.ENGS:
            self._pending[e] = list(lasts)

    def emit(self, nc, block, csem, dsem):
        for e in self.ENGS:
            cnt = 0
            for op in self.ops[e]:
                if not op.dma and op.needs_inc:
                    op.semi = cnt // self.ROT; op.val = cnt % self.ROT + 1; cnt += 1
            assert cnt // self.ROT < len(csem[e]), (e, cnt)
        final = self.final

        def run(e, eng):
            for op in self.ops[e]:
                for d in op.waits:
                    if d.dma:
                        eng.wait_ge(dsem[d.eng][d.slot], d.val)
                    else:
                        eng.wait_ge(csem[d.eng][d.semi], d.val)
                ins = op.fn(eng)
                if op.needs_inc:
                    if op.dma:
                        ins.then_inc(dsem[e][op.slot], 16)
                    else:
                        ins.then_inc(csem[e][op.semi], 1)
            if e == "sync":
                for d in final:
                    eng.wait_ge(dsem[d.eng][d.slot], d.val)

        block.tensor(lambda t: run("tensor", t))
        block.vector(lambda t: run("vector", t))
        block.scalar(lambda t: run("scalar", t))
        block.gpsimd(lambda t: run("gpsimd", t))
        block.sync(lambda t: run("sync", t))


class SBA:
    BASE = 16640
    END = 229376

    def __init__(self, nc):
        self.nc = nc; self.top = self.BASE; self.n = 0

    def alloc(self, name, shape, dtype):
        per = 1
        for s in shape[1:]:
            per *= s
        nbytes = per * (4 if dtype == F32 else 2)
        off = (self.top + 63) // 64 * 64
        assert off + nbytes <= self.END, (name, off, nbytes)
        self.top = off + nbytes
        self.hw = max(getattr(self, "hw", 0), self.top)
        self.n += 1
        key = (name, tuple(shape), str(dtype), off)
        cache = self.__dict__.setdefault("cache", {})
        if key not in cache:
            cache[key] = self.nc.alloc_sbuf_tensor_at(f"{name}{self.n}", list(shape), dtype, offset=off)
        return cache[key]


class _Stop(Exception):
    pass


STOP = None
P2_BLOCKS = None
P2_PART = None


def build(debug=False, nlayers=2):
    nc = bass.Bass("TRN2", target_bir_lowering=False)

    def check(name):
        if STOP == name:
            raise _Stop()
    S = Sched()
    sb = SBA(nc)
    S.sb = sb

    def din(name, shape, dt=F32):
        return nc.dram_tensor(name, list(shape), dt, kind="ExternalInput")

    def dscr(name, shape, dt=BF16):
        return nc.dram_tensor(name, list(shape), dt, kind="ExternalOutput" if debug else "Internal")

    xin = din("xin", [NT, D])
    cfm = din("cfm", [128, 8, 2])
    w_mod = din("w_mod", [DEPTH, D, 3 * D])
    bmod_d = din("bmod", [DEPTH, 128, 24])
    gpre_d = din("gpre", [DEPTH, 128, 8])
    gpost_d = din("gpost", [DEPTH, 128, 8])
    w1_d = din("w1", [DEPTH, D, W1C])
    w2_d = din("w2", [DEPTH, D, W2C])
    tabi_d = din("tabi", [DEPTH, 8, 5, 128, 128])
    tabe_d = din("tabe", [DEPTH, 4, 8, 4, 128, 128])
    wfour_d = din("w_four", [DEPTH, 256, 256])
    dw_d = din("dwfm", [DEPTH, 128, 2, 31])
    cvec_d = din("cvec", [DEPTH, 128, 3, 2])
    wpw_d = din("w_pw", [DEPTH, 256, 256])
    pattn_d = din("p_attn", [DEPTH, 512, D])
    pfour_d = din("p_four", [DEPTH, 256, D])
    pconv_d = din("p_conv", [DEPTH, 256, D])
    wout_d = din("w_out", [DEPTH, D, D])
    identb_d = din("identb", [128, 128], BF16)
    identf_d = din("identf", [128, 128])
    onesf_d = din("onesf", [128, 128])
    cj_d = din("cj", [128, 128], BF16)
    sj_d = din("sj", [128, 128], BF16)
    m1_d = din("m1", [128, 128], BF16)
    g2_d = din("g2", [128, 64, 64], BF16)
    c256_d = din("c256", [128, 2, 256], BF16)
    s256_d = din("s256", [128, 2, 256], BF16)
    yout = nc.dram_tensor("y", [NL, D], F32, kind="ExternalOutput")

    hT_d = dscr("hT_d", [128, 8, NT])
    qT_d = dscr("qT_d", [128, 4, NT])
    kT_d = dscr("kT_d", [128, 4, NT])
    v_d = dscr("v_d", [NT, 520])
    Z_d = dscr("Z_d", [2, NL, 256])
    A_d = dscr("A_d", [2, 64, 64, 256])
    XaT_d = dscr("XaT_d", [128, 4, NT])
    XfT_d = dscr("XfT_d", [128, 2, NT])
    XcT_d = dscr("XcT_d", [128, 2, NT])
    x1_d = dscr("x1_d", [NT, D], F32)
    dbg_mod = dscr("dbg_mod", [128, 24, 2], F32)

    PB = [nc.alloc_psum_tensor(f"pb{i}", [128, 512], F32) for i in range(6)]
    PT = [nc.alloc_psum_tensor(f"pt{i}", [128, 1024], BF16) for i in range(2)]
    pbk = [f"pb{i}" for i in range(6)]
    ptk = [f"pt{i}" for i in range(2)]
    rot = {"pb": 0, "pt": 0, "ev": 0}

    def next_pb():
        i = rot["pb"]; rot["pb"] = (i + 1) % 6
        return PB[i], pbk[i]

    def next_pt():
        i = rot["pt"]; rot["pt"] = (i + 1) % 2
        return PT[i], ptk[i]

    def ev_eng():
        rot["ev"] ^= 1
        return "vector" if rot["ev"] else "scalar"

    def dma(eng, out, in_, reads, writes):
        return S.add(eng, lambda e, o=out, i=in_: e.dma_start(out=o, in_=i), reads, writes, dma=True)

    def copy_op(eng, out, in_, reads, writes):
        if eng == "scalar":
            return S.add(eng, lambda e, o=out, i=in_: e.activation(out=o, in_=i, func=AF.Identity), reads, writes)
        return S.add(eng, lambda e, o=out, i=in_: e.tensor_copy(out=o, in_=i), reads, writes)

    identb = sb.alloc("identb", [128, 128], BF16)
    identf = sb.alloc("identf", [128, 128], F32)
    onesf = sb.alloc("onesf", [128, 128], F32)
    cj = sb.alloc("cj", [128, 128], BF16)
    sj = sb.alloc("sj", [128, 128], BF16)
    m1 = sb.alloc("m1", [128, 128], BF16)
    g2 = sb.alloc("g2", [128, 64, 64], BF16)
    c256 = sb.alloc("c256", [128, 2, 256], BF16)
    s256 = sb.alloc("s256", [128, 2, 256], BF16)
    for t, dsrc, k in ((identb, identb_d, "identb"), (identf, identf_d, "identf"), (onesf, onesf_d, "onesf"),
                       (cj, cj_d, "cj"), (sj, sj_d, "sj"), (m1, m1_d, "m1"), (g2, g2_d, "g2"),
                       (c256, c256_d, "c256"), (s256, s256_d, "s256")):
        dma("sync", t[:], dsrc[:], [], [k])
    craw = sb.alloc("craw", [128, 8, 2], F32)
    sc2 = sb.alloc("sc2", [128, 8, 2], F32)
    dma("sync", craw[:], cfm[:], [], ["craw"])
    S.add("scalar", lambda e: e.activation(out=sc2[:], in_=craw[:], func=AF.Silu), ["craw"], ["sc2"])
    bmod = sb.alloc("bmod", [128, 24], F32)
    gpre = sb.alloc("gpre", [128, 8], F32)
    gpost = sb.alloc("gpost", [128, 8], F32)
    mod = sb.alloc("mod", [128, 24, 2], F32)
    amod = sb.alloc("amod", [128, 8, 2], F32)
    ggm = sb.alloc("ggm", [128, 8, 2], F32)
    ggbc = [sb.alloc("ggbc", [128, D], F32) for _ in range(2)]
    dwc = sb.alloc("dwc", [128, 2, 31], F32)
    cvec = sb.alloc("cvec", [128, 3, 2], F32)
    dgt = sb.alloc("dgt", [128, 128], F32)
    epsc = sb.alloc("epsc", [128, 1], F32)
    zc = sb.alloc("zc", [128, 2, 2, 256], BF16)
    S.add("vector", lambda e: e.memset(epsc[:], EPS), [], ["epsc"])
    GLOBAL_TOP = sb.top

    try:
        for l in range(nlayers):
            last = (l == DEPTH - 1)
            xsrc = xin if l == 0 else x1_d
            xsrc_k = "xin" if l == 0 else "x1_d"
            sb.top = GLOBAL_TOP
            dma("sync", bmod[:], bmod_d[l], [], ["bmod"])
            dma("sync", gpre[:], gpre_d[l], [], ["gpre"])
            dma("sync", gpost[:], gpost_d[l], [], ["gpost"])
            dma("sync", dwc[:], dw_d[l], [], ["dwc"])
            dma("sync", cvec[:], cvec_d[l], [], ["cvec"])
            wm = [sb.alloc("wm", [128, 8, 512], F32) for _ in range(2)]
            wmv = w_mod[l].rearrange("(k p) c -> p k c", p=128)
            psM, psMk = PB[0], pbk[0]
            psMv = psM[:, 0:48].rearrange("p (j v) -> p j v", v=2)
            for s in range(6):
                b = s % 2
                dma("sync", wm[b][:], wmv[:, :, s * 512:(s + 1) * 512], [], [f"wm{b}"])

                def mm_mod(e, s=s, b=b):
                    ins = None
                    for cc in range(4):
                        j = s * 4 + cc
                        for k in range(8):
                            ins = e.matmul(psMv[:, j, :], lhsT=wm[b][:, k, cc * 128:(cc + 1) * 128], rhs=sc2[:, k, :],
                                           start=(k == 0), stop=(k == 7))
                    return ins
                S.add("tensor", mm_mod, [f"wm{b}", "sc2"], [psMk])
            for v in range(2):
                S.add("vector", lambda e, v=v: e.tensor_tensor(out=mod[:, :, v], in0=psMv[:, :, v], in1=bmod[:], op=ALU.add),
                      [psMk, "bmod"], ["mod"])
            for v in range(2):
                S.add("vector", lambda e, v=v: e.scalar_tensor_tensor(out=amod[:, :, v], in0=mod[:, 8:16, v], scalar=1.0,
                                                                       in1=gpre[:], op0=ALU.add, op1=ALU.mult),
                      ["mod", "gpre"], ["amod"])
                S.add("vector", lambda e, v=v: e.tensor_tensor(out=ggm[:, :, v], in0=mod[:, 16:24, v], in1=gpost[:], op=ALU.mult),
                      ["mod", "gpost"], ["ggm"])
            if debug and l == 0:
                dma("sync", dbg_mod[:], mod[:], ["mod"], ["dbg_mod"])
            for v in range(2):
                for half in range(2):
                    ps, pk = next_pb()
                    for jj in range(4):
                        j = half * 4 + jj
                        S.add("vector", lambda e, j=j, v=v: e.tensor_scalar(out=dgt[:], in0=identf[:], scalar1=ggm[:, j, v:v + 1],
                                                                              scalar2=None, op0=ALU.mult),
                              ["identf", "ggm"], ["dgt"])
                        S.add("tensor", lambda e, ps=ps, jj=jj: e.matmul(ps[:, jj * 128:(jj + 1) * 128], lhsT=onesf[:], rhs=dgt[:],
                                                                          start=True, stop=True),
                              ["onesf", "dgt"], [pk])
                    copy_op("vector", ggbc[v][:, half * 512:(half + 1) * 512], ps[:, :], [pk], [f"ggbc{v}"])
            MOD_TOP = GLOBAL_TOP
            check(f"mod{l}")

            sb.top = MOD_TOP + 2 * 16384 + 128
            P1_BASE = sb.top
            uT = sb.alloc("uT", [128, 2, NL + 30], BF16)
            uTc = sb.alloc("uTc", [128, 2, NC_ + 30], BF16)
            for j in range(2):
                S.add("gpsimd", lambda e, j=j: e.memset(uT[:, j, 0:15], 0.0), [], ["uT"])
                S.add("gpsimd", lambda e, j=j: e.memset(uT[:, j, NL + 15:NL + 30], 0.0), [], ["uT"])
                S.add("gpsimd", lambda e, j=j: e.memset(uTc[:, j, 0:15], 0.0), [], ["uTc"])
                S.add("gpsimd", lambda e, j=j: e.memset(uTc[:, j, NC_ + 15:NC_ + 30], 0.0), [], ["uTc"])
            CONV_KEEP = sb.top
            w1 = sb.alloc("w1", [128, 8, W1C], BF16)
            w1v = w1_d[l].rearrange("(k p) c -> p k c", p=128)
            for cg in range(5):
                ca, cb_ = cg * 512, min(W1C, (cg + 1) * 512)
                dma("gpsimd", w1[:, :, ca:cb_], w1v[:, :, ca:cb_], [], [f"w1_{cg}"])
            xt = [sb.alloc("xt", [128, D], F32) for _ in range(2)]
            junk = sb.alloc("junk", [128, D], BF16)
            xn = [sb.alloc("xn", [128, D], BF16) for _ in range(2)]
            ssq = [sb.alloc("ssq", [128, 1], F32) for _ in range(2)]
            rstd = [sb.alloc("rstd", [128, 1], F32) for _ in range(2)]
            hT = [sb.alloc("hT", [128, 8, 512], BF16) for _ in range(2)]
            qs = [sb.alloc("qs", [128, 4, 512], BF16) for _ in range(2)]
            ks = [sb.alloc("ks", [128, 4, 512], BF16) for _ in range(2)]
            vt = [sb.alloc("vt", [128, 8, 65], BF16) for _ in range(3)]
            ufT = [sb.alloc("ufT", [128, 2, 512], BF16) for _ in range(2)]
            zst = [sb.alloc("zst", [128, 2, 256], BF16) for _ in range(2)]
            sig = [sb.alloc("sig", [128, 512], F32) for _ in range(2)]
            for i in range(3):
                S.add("gpsimd", lambda e, i=i: e.memset(vt[i][:, :, 64:65], 1.0), [], [f"vt{i}"])
            nblk = 9

            def blk_info(tb):
                ctxb = (tb == 8)
                ntile = 2 if ctxb else 4
                return ctxb, ntile, ntile * 128, (1 if ctxb else 0), tb % 2

            def normA(tb, t):
                ctxb, ntile, ntok, v, hb = blk_info(tb)
                r0 = tb * 512 + t * 128
                xb = (tb * 4 + t) % 2
                dma("sync", xt[xb][:], xsrc[r0:r0 + 128, :], [xsrc_k + f"_{r0 // 128}"], [f"xt{xb}"])
                S.add("vector", lambda e: e.memset(ssq[xb][:], 0.0), [], [f"ssq{xb}"])
                S.add("scalar", lambda e: e.activation(out=junk[:], in_=xt[xb][:], func=AF.Square, accum_out=ssq[xb][:]),
                      [f"xt{xb}", f"ssq{xb}"], ["junk", f"ssq{xb}"])
                S.add("scalar", lambda e: e.activation(out=rstd[xb][:], in_=ssq[xb][:], func=AF.Sqrt, scale=1.0 / D, bias=epsc[:, 0:1]),
                      [f"ssq{xb}", "epsc"], [f"rstd{xb}"])
                S.add("vector", lambda e: e.reciprocal(out=rstd[xb][:], in_=rstd[xb][:]), [f"rstd{xb}"], [f"rstd{xb}"])
                S.add("vector", lambda e: e.tensor_scalar(out=xn[xb][:], in0=xt[xb][:], scalar1=rstd[xb][:, 0:1], scalar2=None, op0=ALU.mult),
                      [f"xt{xb}", f"rstd{xb}"], [f"xn{xb}"])

            def normB(tb, t):
                ctxb, ntile, ntok, v, hb = blk_info(tb)
                xb = (tb * 4 + t) % 2
                pt, ptkey = next_pt()
                ptv = pt[:, :].rearrange("p (k c) -> p k c", c=128)

                def tr8(e):
                    ins = None
                    for k in range(8):
                        ins = e.transpose(ptv[:, k, :], xn[xb][:, k * 128:(k + 1) * 128], identb[:])
                    return ins
                S.add("tensor", tr8, [f"xn{xb}", "identb"], [ptkey])
                for k in range(8):
                    if k % 2 == 0:
                        S.add("vector", lambda e, k=k: e.tensor_scalar(
                            out=hT[hb][:, k, t * 128:(t + 1) * 128], in0=ptv[:, k, :], scalar1=amod[:, k, v:v + 1],
                            scalar2=mod[:, k, v:v + 1], op0=ALU.mult, op1=ALU.add),
                            [ptkey, "amod", "mod"], [f"hT{hb}"])
                    else:
                        S.add("scalar", lambda e, k=k: e.activation(
                            out=hT[hb][:, k, t * 128:(t + 1) * 128], in_=ptv[:, k, :], func=AF.Identity,
                            scale=amod[:, k, v:v + 1], bias=mod[:, k, v:v + 1]),
                            [ptkey, "amod", "mod"], [f"hT{hb}"])
                if t == ntile - 1:
                    c0 = tb * 512
                    dma("sync", hT_d[:, :, c0:c0 + ntok], hT[hb][:, :, 0:ntok], [f"hT{hb}"], [f"hT_d{tb}"])

            def units(tb):
                ctxb, ntile, ntok, v, hb = blk_info(tb)
                c0 = tb * 512
                us = []

                def proj_fm(col0):
                    ps, pk = next_pb()

                    def f(e):
                        ins = None
                        for k in range(8):
                            ins = e.matmul(ps[:, 0:ntok], lhsT=w1[:, k, col0:col0 + 128], rhs=hT[hb][:, k, 0:ntok],
                                           start=(k == 0), stop=(k == 7))
                        return ins
                    S.add("tensor", f, [f"w1_{col0 // 512}", f"hT{hb}"], [pk])
                    return ps, pk
                need_q = not (ctxb and last)
                for nm, stg, dd, base, need in (("q", qs, qT_d, 0, need_q), ("k", ks, kT_d, 512, True)):
                    if not need:
                        continue
                    for cch in range(4):
                        def u(nm=nm, stg=stg, dd=dd, base=base, cch=cch):
                            ps, pk = proj_fm(base + cch * 128)
                            copy_op(ev_eng(), stg[hb][:, cch, 0:ntok], ps[:, 0:ntok], [pk], [f"{nm}s{hb}"])
                            if cch == 3:
                                dma("sync", dd[:, :, c0:c0 + ntok], stg[hb][:, :, 0:ntok], [f"{nm}s{hb}"], [f"{nm}T_d"])
                        us.append(u)
                for t in range(ntile):
                    def u(t=t):
                        ps, pk = next_pb()
                        vb = (tb * 4 + t) % 3

                        def fv(e):
                            ins = None
                            for k in range(8):
                                ins = e.matmul(ps[:, :], lhsT=hT[hb][:, k, t * 128:(t + 1) * 128], rhs=w1[:, k, 1024:1536],
                                               start=(k == 0), stop=(k == 7))
                            return ins
                        S.add("tensor", fv, ["w1_2", f"hT{hb}"], [pk])
                        copy_op(ev_eng(), vt[vb][:, :, 0:64], ps[:, :].rearrange("p (h d) -> p h d", d=64), [pk], [f"vt{vb}"])
                        r0 = c0 + t * 128
                        dma("sync", v_d[r0:r0 + 128, :], vt[vb][:].rearrange("p h d -> p (h d)"), [f"vt{vb}"], ["v_d"])
                    us.append(u)
                if not (ctxb and last):
                    for j in range(2):
                        def u(j=j):
                            ps, pk = proj_fm(1536 + j * 128)
                            copy_op(ev_eng(), ufT[hb][:, j, 0:ntok], ps[:, 0:ntok], [pk], [f"ufT{hb}"])
                        us.append(u)
                    for t in range(ntile):
                        def u(t=t):
                            ps, pk = next_pb()
                            psz = ps[:, :].rearrange("p (r m) -> p r m", m=256)

                            def fz(e):
                                ins = None
                                for r, tabm in ((0, cj), (1, sj)):
                                    for j in range(2):
                                        ins = e.matmul(psz[:, r, j * 128:(j + 1) * 128], lhsT=ufT[hb][:, j, t * 128:(t + 1) * 128],
                                                       rhs=tabm[:], start=True, stop=True)
                                return ins
                            S.add("tensor", fz, [f"ufT{hb}", "cj", "sj"], [pk])
                            if ctxb:
                                copy_op(ev_eng(), zc[:, t, :, :], psz, [pk], ["zc"])
                            else:
                                zb = t % 2
                                copy_op(ev_eng(), zst[zb][:], psz, [pk], [f"zst{zb}"])
                                r0 = c0 + t * 128
                                for r in range(2):
                                    dma("sync", Z_d[r, r0:r0 + 128, :], zst[zb][:, r, :], [f"zst{zb}"], ["Z_d"])
                        us.append(u)
                    for j in range(2):
                        def u(j=j):
                            sgb = j
                            ps, pk = proj_fm(2048 + j * 128)
                            S.add("scalar", lambda e: e.activation(out=sig[sgb][:, 0:ntok], in_=ps[:, 0:ntok], func=AF.Sigmoid), [pk], [f"sig{sgb}"])
                            ps2, pk2 = proj_fm(1792 + j * 128)
                            dst = uTc[:, j, 15:15 + ntok] if ctxb else uT[:, j, 15 + c0:15 + c0 + ntok]
                            S.add("vector", lambda e: e.tensor_tensor(out=dst, in0=ps2[:, 0:ntok], in1=sig[sgb][:, 0:ntok], op=ALU.mult),
                                  [pk2, f"sig{sgb}"], ["uTc" if ctxb else "uT"])
                        us.append(u)
                return us

            for t in range(4):
                normA(0, t)
                normB(0, t)
            for tb in range(nblk):
                us = units(tb)
                nxt = tb + 1 if tb + 1 < nblk else None
                nt_next = blk_info(nxt)[1] if nxt is not None else 0
                nu = len(us)
                slots = {}
                if nxt is not None:
                    step = max(1, nu // (nt_next + 1))
                    for t in range(nt_next):
                        slots.setdefault(min(nu - 1, step * t), []).append(("A", t))
                        slots.setdefault(min(nu - 1, step * (t + 1)), []).append(("B", t))
                for ui, u in enumerate(us):
                    u()
                    for kind, t in slots.get(ui, []):
                        if kind == "A":
                            normA(nxt, t)
                        else:
                            normB(nxt, t)
            S.barrier()
            check(f"p1{l}")
            sb.top = CONV_KEEP
            wpw = sb.alloc("wpw", [128, 2, 256], BF16)
            dma("gpsimd", wpw[:], wpw_d[l].rearrange("(k p) c -> p k c", p=128), [], ["wpw"])
            acc = [sb.alloc("acc", [128, 2, 512], F32) for _ in range(3)]
            dg = sb.alloc("dg", [128, 62, 128], BF16)
            for j in range(2):
                for kk in range(31):
                    S.add("vector", lambda e, j=j, kk=kk: e.tensor_scalar(out=dg[:, j * 31 + kk, :], in0=identf[:], scalar1=dwc[:, j, kk:kk + 1],
                                                                          scalar2=None, op0=ALU.mult), ["identf", "dwc"], ["dg"])
            sq = sb.alloc("sq", [128, 2, 512], F32)
            ctmp = sb.alloc("ctmp", [128, 512], F32)
            var = sb.alloc("var", [128, 512], F32)
            msq = sb.alloc("msq", [128, 512], F32)
            tn = sb.alloc("tn", [128, 2, 512], F32)
            zz = [sb.alloc("zz", [128, 2, 512], BF16) for _ in range(2)]
            xcs = [sb.alloc("xcs", [128, 2, 512], BF16) for _ in range(2)]
            def convA(tb):
                ctxb = (tb == 8)
                ntok = 256 if ctxb else 512
                c0 = tb * 512
                ab = tb % 3
                zb_ = tb % 2
                src = uTc if ctxb else uT
                off = 0 if ctxb else c0
                skey = "uTc" if ctxb else "uT"
                for j in range(2):
                    ps, pk = next_pb()

                    def fconv(e, ps=ps, j=j, src=src, off=off, ntok=ntok):
                        ins = None
                        for kk in range(31):
                            ins = e.matmul(ps[:, 0:ntok], lhsT=dg[:, j * 31 + kk, :], rhs=src[:, j, off + kk:off + kk + ntok],
                                           start=(kk == 0), stop=(kk == 30))
                        return ins
                    S.add("tensor", fconv, [skey, "dg"], [pk])
                    S.add("vector", lambda e, ps=ps, j=j, ab=ab, zb_=zb_, ntok=ntok: e.tensor_scalar(
                        out=acc[ab][:, j, 0:ntok], in0=ps[:, 0:ntok], scalar1=cvec[:, 0, j:j + 1], scalar2=None, op0=ALU.add),
                        [pk, "cvec"], [f"acc{ab}_{j}"])
            def convB(tb):
                ctxb = (tb == 8)
                ntok = 256 if ctxb else 512
                c0 = tb * 512
                ab = tb % 3
                zb_ = tb % 2
                akeys = [f"acc{ab}_0", f"acc{ab}_1"]
                S.add("scalar", lambda e, ab=ab, zb_=zb_, ntok=ntok: e.activation(out=sq[:, :, 0:ntok], in_=acc[ab][:, :, 0:ntok], func=AF.Square),
                      akeys, ["sq"])
                psm, pkm = next_pb()
                pss, pks = next_pb()

                def fstat(e, psm=psm, pss=pss, ab=ab, zb_=zb_, ntok=ntok):
                    ins = None
                    for j in range(2):
                        ins = e.matmul(psm[:, 0:ntok], lhsT=onesf[:], rhs=acc[ab][:, j, 0:ntok], start=(j == 0), stop=(j == 1))
                    for j in range(2):
                        ins = e.matmul(pss[:, 0:ntok], lhsT=onesf[:], rhs=sq[:, j, 0:ntok], start=(j == 0), stop=(j == 1))
                    return ins
                S.add("tensor", fstat, akeys + ["sq", "onesf"], [pkm, pks])
                S.add("scalar", lambda e, psm=psm, ntok=ntok: e.activation(out=var[:, 0:ntok], in_=psm[:, 0:ntok], func=AF.Square,
                                                                          scale=1.0 / 256), [pkm], ["var"])
                S.add("vector", lambda e, pss=pss, ntok=ntok: e.scalar_tensor_tensor(
                    out=var[:, 0:ntok], in0=pss[:, 0:ntok], scalar=1.0 / 256, in1=var[:, 0:ntok], op0=ALU.mult, op1=ALU.subtract),
                    [pks, "var"], ["var"])
                S.add("scalar", lambda e, ntok=ntok: e.activation(out=var[:, 0:ntok], in_=var[:, 0:ntok], func=AF.Sqrt, bias=epsc[:, 0:1]),
                      ["var", "epsc"], ["var"])
                S.add("vector", lambda e, ntok=ntok: e.reciprocal(out=var[:, 0:ntok], in_=var[:, 0:ntok]), ["var"], ["var"])
                for j in range(2):
                    S.add("vector", lambda e, j=j, psm=psm, ab=ab, zb_=zb_, ntok=ntok: e.scalar_tensor_tensor(
                        out=tn[:, j, 0:ntok], in0=psm[:, 0:ntok], scalar=-1.0 / 256, in1=acc[ab][:, j, 0:ntok],
                        op0=ALU.mult, op1=ALU.add), [pkm] + akeys, [f"tn{j}"])
                    S.add("vector", lambda e, j=j, ntok=ntok: e.tensor_tensor(out=tn[:, j, 0:ntok], in0=tn[:, j, 0:ntok], in1=var[:, 0:ntok],
                                                                            op=ALU.mult), [f"tn{j}", "var"], [f"tn{j}"])
                    S.add("scalar", lambda e, j=j, ab=ab, zb_=zb_, ntok=ntok: e.activation(out=zz[zb_][:, j, 0:ntok], in_=tn[:, j, 0:ntok], func=AF.Silu,
                                                                               scale=cvec[:, 1, j:j + 1], bias=cvec[:, 2, j:j + 1]),
                          [f"tn{j}", "cvec"], [f"zz{zb_}"])
            def convB2(tb):
                ctxb = (tb == 8)
                ntok = 256 if ctxb else 512
                c0 = tb * 512
                ab = tb % 3
                zb_ = tb % 2
                for mc in range(2):
                    ps, pk = next_pb()

                    def fpw(e, ps=ps, mc=mc, ab=ab, zb_=zb_, ntok=ntok):
                        ins = None
                        for j in range(2):
                            ins = e.matmul(ps[:, 0:ntok], lhsT=wpw[:, j, mc * 128:(mc + 1) * 128], rhs=zz[zb_][:, j, 0:ntok],
                                           start=(j == 0), stop=(j == 1))
                        return ins
                    S.add("tensor", fpw, ["wpw", f"zz{zb_}"], [pk])
                    copy_op(ev_eng(), xcs[zb_][:, mc, 0:ntok], ps[:, 0:ntok], [pk], [f"xcs{zb_}"])
                dma("sync", XcT_d[:, :, c0:c0 + ntok], xcs[zb_][:, :, 0:ntok], [f"xcs{zb_}"], ["XcT_d"])

            ncb = nblk if not last else 8
            wfr = sb.alloc("wfr", [128, 2, 256], BF16)
            dma("gpsimd", wfr[:], wfour_d[l].rearrange("(k p) c -> p k c", p=128), [], ["wfr"])
            zstack = sb.alloc("zstack", [128, 16384], BF16)
            astack = zstack[:, :].rearrange("p (k m) -> p k m", m=256)
            asb = [sb.alloc("asb", [128, 2048], BF16) for _ in range(2)]
            fourT = sb.alloc("fourT", [128, 2, NT], BF16)
            xfs = [sb.alloc("xfs", [128, 2, 512], BF16) for _ in range(2)]
            A_dv = A_d.rearrange("r k n m -> (r k) (n m)")
            fT4 = [fourT[:, mc, 0:NL].rearrange("p (a b) -> p a b", b=64) for mc in range(2)]

            def fft_load_z():
                for r in range(2):
                    dma("sync", zstack[r * 64:(r + 1) * 64, :], Z_d[r].rearrange("(a b) m -> a (b m)", b=64), ["Z_d"], ["zstack"])

            def fft_stage1(g0, g1):
                for g in range(g0, g1):
                    gb = g % 2
                    for cbk in range(4):
                        ps, pk = next_pb()
                        col = g * 2048 + cbk * 512
                        S.add("tensor", lambda e, ps=ps, col=col: e.matmul(ps[:, :], lhsT=m1[:], rhs=zstack[:, col:col + 512], start=True, stop=True),
                              ["m1", "zstack"], [pk])
                        copy_op(ev_eng(), asb[gb][:, cbk * 512:(cbk + 1) * 512], ps[:, :], [pk], [f"asb{gb}"])
                    dma("sync", A_dv[:, g * 2048:(g + 1) * 2048], asb[gb][:], [f"asb{gb}"], ["A_d"])

            def fft_load_a():
                for r in range(2):
                    dma("sync", astack[r * 64:(r + 1) * 64, :, :], A_d[r].rearrange("k n m -> n k m"), ["A_d"], ["zstack"])

            def fft_stage2(g0, g1):
                for g in range(g0, g1):
                    for mc in range(2):
                        ps, pk = next_pb()
                        psv = ps[:, :].rearrange("p (a b) -> p a b", b=8)

                        def fs2(e, psv=psv, g=g, mc=mc):
                            ins = None
                            for kl in range(8):
                                k1 = g * 8 + kl
                                ins = e.matmul(psv[:, :, kl], lhsT=astack[:, k1, mc * 128:(mc + 1) * 128], rhs=g2[:, k1, :], start=True, stop=True)
                            return ins
                        S.add("tensor", fs2, ["zstack", "g2"], [pk])
                        copy_op(ev_eng(), fT4[mc][:, :, g * 8:(g + 1) * 8], psv, [pk], ["fourT"])

            def fft_ctx():
                for mc in range(2):
                    ps, pk = next_pb()

                    def fc(e, ps=ps, mc=mc):
                        ins = None
                        n = 0
                        for nt_ in range(2):
                            for r, tabm in ((0, c256), (1, s256)):
                                ins = e.matmul(ps[:, 0:256], lhsT=zc[:, nt_, r, mc * 128:(mc + 1) * 128], rhs=tabm[:, nt_, :],
                                               start=(n == 0), stop=(n == 3))
                                n += 1
                        return ins
                    S.add("tensor", fc, ["zc", "c256", "s256"], [pk])
                    copy_op(ev_eng(), fourT[:, mc, NL:NT], ps[:, 0:256], [pk], ["fourT"])

            def fft_fw(t0, t1):
                for tb in range(t0, t1):
                    ntok = 256 if tb == 8 else 512
                    c0 = tb * 512
                    fb = tb % 2
                    for mc in range(2):
                        ps, pk = next_pb()

                        def fw(e, ps=ps, mc=mc, c0=c0, ntok=ntok):
                            ins = None
                            for j in range(2):
                                ins = e.matmul(ps[:, 0:ntok], lhsT=wfr[:, j, mc * 128:(mc + 1) * 128], rhs=fourT[:, j, c0:c0 + ntok],
                                               start=(j == 0), stop=(j == 1))
                            return ins
                        S.add("tensor", fw, ["wfr", "fourT"], [pk])
                        copy_op(ev_eng(), xfs[fb][:, mc, 0:ntok], ps[:, 0:ntok], [pk], [f"xfs{fb}"])
                    dma("sync", XfT_d[:, :, c0:c0 + ntok], xfs[fb][:, :, 0:ntok], [f"xfs{fb}"], ["XfT_d"])

            nfw = 9 if not last else 8
            fft_sched = {0: [lambda: fft_stage1(0, 4)], 1: [lambda: fft_stage1(4, 8), fft_load_a],
                         3: [lambda: fft_stage2(0, 4)], 4: [lambda: fft_stage2(4, 8)] + ([fft_ctx] if not last else []),
                         5: [lambda: fft_fw(0, 3)], 6: [lambda: fft_fw(3, 6)], 7: [lambda: fft_fw(6, nfw)]}
            fft_load_z()
            convA(0)
            if ncb > 1:
                convA(1)
            for tb in range(ncb):
                convB(tb)
                if tb + 2 < ncb:
                    convA(tb + 2)
                for fn_ in fft_sched.get(tb, []):
                    fn_()
                convB2(tb)
            S.barrier()

            check(f"fft{l}")
            sb.top = MOD_TOP
            qz = [sb.alloc("qz", [128, 4, 2, 512], BF16) for _ in range(2)]
            for b_ in range(2):
                S.add("vector", lambda e, b_=b_: e.memset(qz[b_][:], 0.0), [], [f"qz{b_}"])

            def load_q(c):
                ca = c * 512
                n_ = min(NT, ca + 512) - ca
                for par in range(2):
                    dma("sync", qz[c % 2][par * 64:(par + 1) * 64, :, par, 0:n_], qT_d[par * 64:(par + 1) * 64, :, ca:ca + n_],
                        ["qT_d"], [f"qz{c % 2}"])
            ksb = sb.alloc("ksb", [128, 4, NT], BF16)
            vsb = sb.alloc("vsb", [128, 34, 520], BF16)
            vdv = v_d.rearrange("(t p) f -> p t f", p=128)
            for c in [8] + list(range(8)):
                ca = c * 512
                cb_ = min(NT, ca + 512)
                ta, tb_ = c * 4, min(34, c * 4 + 4)
                dma("sync", ksb[:, :, ca:cb_], kT_d[:, :, ca:cb_], ["kT_d"], [f"ksb_{c}"])
                dma("sync", vsb[:, ta:tb_, :], vdv[:, ta:tb_, :], ["v_d"], [f"vsb_{c}"])
            etab = sb.alloc("etab", [128, 8, 5, 128], BF16)
            eedge = sb.alloc("eedge", [128, 8, 4, 128], BF16)
            tstg = [sb.alloc("tstg", [128, 5, 128], F32) for _ in range(2)]
            pT = [sb.alloc("pT", [128, 8, 128], BF16) for _ in range(3)]
            atile = [sb.alloc("atile", [128, 512], BF16) for _ in range(2)]
            xas = [sb.alloc("xas", [128, 4, 512], BF16) for _ in range(2)]
            rec = [sb.alloc("rec", [128, 1], F32) for _ in range(2)]
            for h in range(8):
                tsb = h % 2
                dma("sync", tstg[tsb][:], tabi_d[l, h].rearrange("j k q -> k j q"), [], [f"tstg{tsb}"])
                S.add("scalar", lambda e, h=h, tsb=tsb: e.activation(out=etab[:, h, :, :], in_=tstg[tsb][:], func=AF.Exp),
                      [f"tstg{tsb}"], ["etab"])
            nqb = 34 if not last else 32
            items = []
            hcount = 0
            for i in range(nqb):
                ctxq = (i >= 32)
                r0 = 2 * i
                edge = None
                if ctxq:
                    nwt = 0; kt0 = 0
                elif r0 < 4:
                    nwt = 4; kt0 = 0; edge = r0 // 2
                elif r0 >= 60:
                    nwt = 4; kt0 = 28; edge = 2 + (r0 - 60) // 2
                else:
                    nwt = 5; kt0 = (r0 - 4) // 2
                ktiles = [kt0 + j for j in range(nwt)] + [32, 33]
                for h in range(8):
                    items.append((i, h, edge, nwt, ktiles, hcount))
                    hcount += 1

            nchunk = (nqb + 3) // 4
            load_q(0)

            def stage_a(i, h, edge, nwt, ktiles, cnt_):
                pb_ = cnt_ % 2
                pi_ = cnt_ % 3
                nk = len(ktiles)
                if h == 0 and i % 4 == 0 and i // 4 + 1 < nchunk:
                    load_q(i // 4 + 1)
                if edge is not None and h == 0:
                    for hh in range(8):
                        tsb = hh % 2
                        dma("sync", tstg[tsb][:, 0:4, :], tabe_d[l, edge, hh].rearrange("j k q -> k j q"), [], [f"tstg{tsb}"])
                        S.add("scalar", lambda e, hh=hh, tsb=tsb: e.activation(out=eedge[:, hh, :, :], in_=tstg[tsb][:, 0:4, :], func=AF.Exp),
                              [f"tstg{tsb}"], ["eedge"])
                hp = h // 2; po = (h % 2) * 64
                psA, pkA = PB[2 * pb_], pbk[2 * pb_]
                psB, pkB = PB[2 * pb_ + 1], pbk[2 * pb_ + 1]
                psAv = psA[:, :].rearrange("p (j q) -> p j q", q=128)
                psBv = psB[:, :].rearrange("p (j q) -> p j q", q=128)

                def fqk(e):
                    ins = None
                    for j, kt in enumerate(ktiles):
                        o = psAv[:, j, :] if j < 4 else psBv[:, j - 4, :]
                        ins = e.matmul(o, lhsT=ksb[:, hp, kt * 128:(kt + 1) * 128], rhs=qz[(i // 4) % 2][:, hp, h % 2, (i % 4) * 128:(i % 4 + 1) * 128],
                                       start=True, stop=True)
                    return ins
                S.add("tensor", fqk, sorted({f"ksb_{kt // 4}" for kt in ktiles}) + [f"qz{(i // 4) % 2}"], [pkA, pkB])
                na = min(4, nk)
                S.add("scalar", lambda e: e.activation(out=pT[pi_][:, 0:na, :], in_=psAv[:, 0:na, :], func=AF.Exp, scale=0.125),
                      [pkA], [f"pT{pi_}_lo"])
                if nk > 4:
                    S.add("scalar", lambda e: e.activation(out=pT[pi_][:, 4:nk, :], in_=psBv[:, 0:nk - 4, :], func=AF.Exp, scale=0.125),
                          [pkB], [f"pT{pi_}_hi"])
                if nwt:
                    tab = etab if edge is None else eedge
                    tkey = "etab" if edge is None else "eedge"
                    nlo = min(nwt, 4)
                    S.add("vector", lambda e: e.tensor_tensor(out=pT[pi_][:, 0:nlo, :], in0=pT[pi_][:, 0:nlo, :], in1=tab[:, h, 0:nlo, :], op=ALU.mult),
                          [f"pT{pi_}_lo", tkey], [f"pT{pi_}_lo"])
                    if nwt > 4:
                        S.add("gpsimd", lambda e: e.tensor_tensor(out=pT[pi_][:, 4:nwt, :], in0=pT[pi_][:, 4:nwt, :], in1=tab[:, h, 4:nwt, :], op=ALU.mult),
                              [f"pT{pi_}_hi", tkey], [f"pT{pi_}_hi"])

            def stage_b(i, h, edge, nwt, ktiles, cnt_):
                pb_ = cnt_ % 2
                pi_ = cnt_ % 3
                nk = len(ktiles)
                ab = i % 2
                psO, pkO = PB[4 + pb_], pbk[4 + pb_]

                def fpv(e):
                    ins = None
                    for j, kt in enumerate(ktiles):
                        ins = e.matmul(psO[:, 0:65], lhsT=pT[pi_][:, j, :], rhs=vsb[:, kt, h * 65:(h + 1) * 65], start=(j == 0), stop=(j == nk - 1))
                    return ins
                S.add("tensor", fpv, [f"pT{pi_}_lo", f"pT{pi_}_hi"] + sorted({f"vsb_{kt // 4}" for kt in ktiles}), [pkO])
                S.add("vector", lambda e: e.reciprocal(out=rec[pb_][:], in_=psO[:, 64:65]), [pkO], [f"rec{pb_}"])
                S.add("vector", lambda e: e.tensor_scalar(out=atile[ab][:, h * 64:(h + 1) * 64], in0=psO[:, 0:64], scalar1=rec[pb_][:, 0:1], scalar2=None, op0=ALU.mult),
                      [pkO, f"rec{pb_}"], [f"atile{ab}"])
                if h == 7:
                    pt, ptkey = next_pt()
                    ptv = pt[:, 0:512].rearrange("p (k c) -> p k c", c=128)

                    def tr4(e):
                        ins = None
                        for k in range(4):
                            ins = e.transpose(ptv[:, k, :], atile[ab][:, k * 128:(k + 1) * 128], identb[:])
                        return ins
                    S.add("tensor", tr4, [f"atile{ab}", "identb"], [ptkey])
                    sbi = (i // 4) % 2
                    copy_op("vector", xas[sbi][:, :, (i % 4) * 128:(i % 4 + 1) * 128], ptv, [ptkey], [f"xas{sbi}"])
                    if i % 4 == 3 or i == nqb - 1:
                        c0 = (i // 4) * 512
                        ntok = ((i % 4) + 1) * 128
                        dma("sync", XaT_d[:, :, c0:c0 + ntok], xas[sbi][:, :, 0:ntok], [f"xas{sbi}"], ["XaT_d"])

            for n, it in enumerate(items):
                stage_a(*it)
                if n >= 2:
                    stage_b(*items[n - 2])
            for it in items[-2:]:
                stage_b(*it)
            S.barrier()

            check(f"att{l}")
            sb.top = MOD_TOP
            w2 = sb.alloc("w2", [128, 8, W2C], BF16)
            w2v = w2_d[l].rearrange("(k p) c -> p k c", p=128)
            for cg in range(8):
                dma("gpsimd", w2[:, :, cg * 512:(cg + 1) * 512], w2v[:, :, cg * 512:(cg + 1) * 512], [], [f"w2_{cg}"])
            pat = sb.alloc("pat", [128, 4, D], BF16)
            pfo = sb.alloc("pfo", [128, 2, D], BF16)
            pco = sb.alloc("pco", [128, 2, D], BF16)
            wo = sb.alloc("wo", [128, 8, D], BF16)
            dma("gpsimd", pat[:], pattn_d[l].rearrange("(k p) c -> p k c", p=128), [], ["pat"])
            dma("gpsimd", pfo[:], pfour_d[l].rearrange("(k p) c -> p k c", p=128), [], ["pfo"])
            dma("gpsimd", pco[:], pconv_d[l].rearrange("(k p) c -> p k c", p=128), [], ["pco"])
            dma("gpsimd", wo[:], wout_d[l].rearrange("(k p) c -> p k c", p=128), [], ["wo"])
            h2 = [sb.alloc("h2", [128, 8, 512], BF16) for _ in range(2)]
            xg = [sb.alloc("xg", [128, 8, 512], BF16) for _ in range(2)]
            gsl = sb.alloc("gsl", [128, 512], BF16)
            sgs = [sb.alloc("sgs", [128, 512], F32) for _ in range(3)]
            mtmp = sb.alloc("mtmp", [128, 512], F32)
            macc = sb.alloc("macc", [128, 512], F32)
            mT_ = sb.alloc("mT", [128, 8, 512], BF16); mT = [mT_, mT_]
            xr_ = sb.alloc("xr", [128, D], F32); xr = [xr_, xr_]
            yn = [sb.alloc("yn", [128, D], F32) for _ in range(2)]
            junk2 = sb.alloc("junk2", [128, 512], BF16)
            ss2 = [sb.alloc("ss2", [128, 2], F32) for _ in range(2)]
            rs2 = [sb.alloc("rs2", [128, 1], F32) for _ in range(2)]
            tcount = 0
            def p2_loads(tb2):
                n2 = 256 if tb2 == 8 else 512
                cc = tb2 * 512
                b2 = tb2 % 2
                dma("sync", h2[b2][:, :, 0:n2], hT_d[:, :, cc:cc + n2], [f"hT_d{tb2}"], [f"h2{b2}"])
                dma("sync", xg[b2][:, 0:4, 0:n2], XaT_d[:, :, cc:cc + n2], ["XaT_d"], [f"xg{b2}"])
                dma("sync", xg[b2][:, 4:6, 0:n2], XfT_d[:, :, cc:cc + n2], ["XfT_d"], [f"xg{b2}"])
                dma("sync", xg[b2][:, 6:8, 0:n2], XcT_d[:, :, cc:cc + n2], ["XcT_d"], [f"xg{b2}"])
            def p2_gates(tb):
                ctxb = (tb == 8)
                ntile = 2 if ctxb else 4
                ntok = ntile * 128
                c0 = tb * 512
                v = 1 if ctxb else 0
                hb = tb % 2
                def proj2(col0, hb=hb, ntok=ntok):
                    ps, pk = next_pb()

                    def f(e, ps=ps, col0=col0):
                        ins = None
                        for k in range(8):
                            ins = e.matmul(ps[:, 0:ntok], lhsT=w2[:, k, col0:col0 + 128], rhs=h2[hb][:, k, 0:ntok], start=(k == 0), stop=(k == 7))
                        return ins
                    S.add("tensor", f, [f"w2_{col0 // 512}", f"h2{hb}"], [pk])
                    return ps, pk
                for cch in range(8):
                    ps, pk = proj2(cch * 128)
                    S.add("scalar", lambda e, ps=ps, ntok=ntok: e.activation(out=gsl[:, 0:ntok], in_=ps[:, 0:ntok], func=AF.Silu), [pk], ["gsl"])
                    S.add("vector", lambda e, cch=cch, hb=hb, ntok=ntok: e.tensor_tensor(out=xg[hb][:, cch, 0:ntok], in0=xg[hb][:, cch, 0:ntok],
                                                                                      in1=gsl[:, 0:ntok], op=ALU.mult), ["gsl", f"xg{hb}"], [f"xg{hb}"])
            def p2_merge(tb):
                ctxb = (tb == 8)
                ntile = 2 if ctxb else 4
                ntok = ntile * 128
                c0 = tb * 512
                v = 1 if ctxb else 0
                hb = tb % 2
                def proj2(col0, hb=hb, ntok=ntok):
                    ps, pk = next_pb()

                    def f(e, ps=ps, col0=col0):
                        ins = None
                        for k in range(8):
                            ins = e.matmul(ps[:, 0:ntok], lhsT=w2[:, k, col0:col0 + 128], rhs=h2[hb][:, k, 0:ntok], start=(k == 0), stop=(k == 7))
                        return ins
                    S.add("tensor", f, [f"w2_{col0 // 512}", f"h2{hb}"], [pk])
                    return ps, pk
                for oc in range(8):
                    first = True
                    for bi, (wt, wk, kc0, nkc, scol) in enumerate(((pat, "pat", 0, 4, 1024), (pfo, "pfo", 4, 2, 2048), (pco, "pco", 6, 2, 3072))):
                        pss, pks = proj2(scol + oc * 128)
                        S.add("scalar", lambda e, pss=pss, bi=bi, ntok=ntok: e.activation(out=sgs[bi][:, 0:ntok], in_=pss[:, 0:ntok], func=AF.Sigmoid),
                              [pks], [f"sgs{bi}"])
                        psy, pky = next_pb()

                        def fy(e, psy=psy, wt=wt, kc0=kc0, nkc=nkc, oc=oc, hb=hb, ntok=ntok):
                            ins = None
                            for k in range(nkc):
                                ins = e.matmul(psy[:, 0:ntok], lhsT=wt[:, k, oc * 128:(oc + 1) * 128], rhs=xg[hb][:, kc0 + k, 0:ntok],
                                               start=(k == 0), stop=(k == nkc - 1))
                            return ins
                        S.add("tensor", fy, [wk, f"xg{hb}"], [pky])
                        if bi == 0:
                            S.add("vector", lambda e, psy=psy, bi=bi, ntok=ntok: e.tensor_tensor(out=macc[:, 0:ntok], in0=psy[:, 0:ntok], in1=sgs[bi][:, 0:ntok], op=ALU.mult),
                                  [pky, f"sgs{bi}"], ["macc"])
                        elif bi == 1:
                            S.add("vector", lambda e, psy=psy, bi=bi, ntok=ntok: e.tensor_tensor(out=mtmp[:, 0:ntok], in0=psy[:, 0:ntok], in1=sgs[bi][:, 0:ntok], op=ALU.mult),
                                  [pky, f"sgs{bi}"], ["mtmp"])
                            S.add("gpsimd", lambda e, ntok=ntok: e.tensor_tensor(out=macc[:, 0:ntok], in0=macc[:, 0:ntok], in1=mtmp[:, 0:ntok], op=ALU.add),
                                  ["macc", "mtmp"], ["macc"])
                        else:
                            S.add("vector", lambda e, psy=psy, bi=bi, ntok=ntok: e.tensor_tensor(out=mtmp[:, 0:ntok], in0=psy[:, 0:ntok], in1=sgs[bi][:, 0:ntok], op=ALU.mult),
                                  [pky, f"sgs{bi}"], ["mtmp"])
                            S.add("gpsimd", lambda e, oc=oc, hb=hb, ntok=ntok: e.tensor_tensor(out=mT[hb][:, oc, 0:ntok], in0=macc[:, 0:ntok], in1=mtmp[:, 0:ntok], op=ALU.add),
                                  ["macc", "mtmp"], ["mT0"])
            def p2_resid(tb):
                ctxb = (tb == 8)
                ntile = 2 if ctxb else 4
                ntok = ntile * 128
                c0 = tb * 512
                v = 1 if ctxb else 0
                hb = tb % 2
                for t in range(ntile):
                    r0 = c0 + t * 128
                    xb = (tb * 4 + t) % 2
                    dma("sync", xr[xb][:], xsrc[r0:r0 + 128, :], [xsrc_k + f"_{r0 // 128}"], ["xr0"])
                    S.add("vector", lambda e, xb=xb: e.memset(ss2[xb][:], 0.0), [], [f"ss2{xb}"])
                    halves = []
                    for hf in range(2):
                        ps, pk = next_pb()

                        def fo(e, ps=ps, hf=hf, t=t, hb=hb):
                            ins = None
                            for k in range(8):
                                ins = e.matmul(ps[:, :], lhsT=mT[hb][:, k, t * 128:(t + 1) * 128], rhs=wo[:, k, hf * 512:(hf + 1) * 512],
                                               start=(k == 0), stop=(k == 7))
                            return ins
                        S.add("tensor", fo, ["wo", "mT0"], [pk])
                        S.add("scalar", lambda e, ps=ps, xb=xb, hf=hf: e.activation(out=junk2[:], in_=ps[:, :], func=AF.Square, accum_out=ss2[xb][:, hf:hf + 1]),
                              [pk, f"ss2{xb}"], ["junk2", f"ss2{xb}"])
                        copy_op("vector", yn[xb][:, hf * 512:(hf + 1) * 512], ps[:, :], [pk], [f"yn{xb}"])
                    S.add("vector", lambda e, xb=xb: e.tensor_tensor(out=rs2[xb][:], in0=ss2[xb][:, 0:1], in1=ss2[xb][:, 1:2], op=ALU.add),
                          [f"ss2{xb}"], [f"rs2{xb}"])
                    S.add("scalar", lambda e, xb=xb: e.activation(out=rs2[xb][:], in_=rs2[xb][:], func=AF.Sqrt, scale=1.0 / D, bias=epsc[:, 0:1]),
                          [f"rs2{xb}", "epsc"], [f"rs2{xb}"])
                    S.add("vector", lambda e, xb=xb: e.reciprocal(out=rs2[xb][:], in_=rs2[xb][:]), [f"rs2{xb}"], [f"rs2{xb}"])
                    S.add("vector", lambda e, xb=xb, v=v: e.scalar_tensor_tensor(out=yn[xb][:], in0=yn[xb][:], scalar=rs2[xb][:, 0:1], in1=ggbc[v][:],
                                                                              op0=ALU.mult, op1=ALU.mult), [f"yn{xb}", f"rs2{xb}", f"ggbc{v}"], [f"yn{xb}"])
                    S.add("vector", lambda e, xb=xb: e.tensor_tensor(out=yn[xb][:], in0=yn[xb][:], in1=xr[xb][:], op=ALU.add),
                          [f"yn{xb}", "xr0"], [f"yn{xb}"])
                    if last:
                        o = dma("sync", yout[r0:r0 + 128, :], yn[xb][:], [f"yn{xb}"], [f"y_{r0 // 128}"])
                        S.final.append(o)
                    else:
                        dma("sync", x1_d[r0:r0 + 128, :], yn[xb][:], [f"yn{xb}"], [f"x1_d_{r0 // 128}"])
            np2 = (9 if not last else 8) if P2_BLOCKS is None else P2_BLOCKS
            p2_loads(0)
            p2_gates(0)
            for tb in range(np2):
                if tb + 1 < np2:
                    p2_loads(tb + 1)
                p2_merge(tb)
                if tb + 1 < np2:
                    p2_gates(tb + 1)
                p2_resid(tb)
            S.barrier()
    except _Stop:
        pass

    if nlayers < DEPTH:
        pass
    return nc, S


def _emit(nc, S):
    import contextlib
    with contextlib.ExitStack() as st:
        csem = {e: [st.enter_context(nc.semaphore(f"c_{e}_{i}")) for i in range(8)] for e in Sched.ENGS}
        dsem = {e: [st.enter_context(nc.semaphore(f"d_{e}_{i}")) for i in range(Sched.K)] for e in ("sync", "gpsimd", "scalar")}
        block = st.enter_context(nc.Block())
        S.emit(nc, block, csem, dsem)


def _fm(vec, n):
    return np.ascontiguousarray(np.asarray(vec, np.float32).reshape(n, 128).T)


def _consts():
    bf = ml_dtypes.bfloat16
    c = {}
    c["identb"] = np.eye(128, dtype=np.float32).astype(bf)
    c["identf"] = np.eye(128, dtype=np.float32)
    c["onesf"] = np.ones((128, 128), np.float32)
    j = np.arange(64)
    ang = 2 * np.pi * np.outer(j, j) / 64.0
    cjm = np.zeros((128, 128)); sjm = np.zeros((128, 128))
    for b in range(2):
        cjm[b * 64:(b + 1) * 64, b * 64:(b + 1) * 64] = np.cos(ang) / 8
        sjm[b * 64:(b + 1) * 64, b * 64:(b + 1) * 64] = -np.sin(ang) / 8
    c["cj"] = cjm.astype(np.float32).astype(bf); c["sj"] = sjm.astype(np.float32).astype(bf)
    cc = np.cos(ang) / 8; ss = np.sin(ang) / 8
    m1 = np.zeros((128, 128))
    m1[0:64, 0:64] = cc; m1[64:128, 0:64] = ss; m1[0:64, 64:128] = -ss; m1[64:128, 64:128] = cc
    c["m1"] = m1.astype(np.float32).astype(bf)
    n2 = np.arange(64)[:, None, None]; k1 = np.arange(64)[None, :, None]; k2 = np.arange(64)[None, None, :]
    th = 2 * np.pi * (n2 * k2 / 64.0 + n2 * k1 / 4096.0)
    g2 = np.concatenate([np.cos(th) / 8, np.sin(th) / 8], 0)
    c["g2"] = g2.astype(np.float32).astype(bf)
    n = np.arange(256)
    psi = 2 * np.pi * np.outer(n, n) / 256.0
    c["c256"] = np.ascontiguousarray((np.cos(psi) / 16).reshape(2, 128, 256).transpose(1, 0, 2)).astype(np.float32).astype(bf)
    c["s256"] = np.ascontiguousarray((np.sin(psi) / 16).reshape(2, 128, 256).transpose(1, 0, 2)).astype(np.float32).astype(bf)
    return c


def _bias_tables(rpb):
    rpb = np.asarray(rpb, np.float32)
    L = rpb.shape[0]
    cols = np.arange(64)
    cs = np.clip(cols - 8, 0, 48)

    def table(r0, kb, ntile):
        out = np.full((L, 8, ntile, 128, 128), NEG, np.float32)
        for j in range(ntile):
            for krl in range(2):
                kr = kb + 2 * j + krl
                for rl in range(2):
                    r = r0 + rl
                    rs = min(max(r - 4, 0), 56)
                    if not (rs <= kr <= rs + 7) or kr > 63:
                        continue
                    dr = kr - r + 7
                    kc = cols[:, None]; c = cols[None, :]
                    valid = (kc >= cs[None, :]) & (kc <= cs[None, :] + 15)
                    dc = np.clip(kc - c + 15, 0, 30)
                    blk = np.where(valid[None, None], rpb[:, :, dr][:, :, dc], NEG)
                    out[:, :, j, krl * 64:(krl + 1) * 64, rl * 64:(rl + 1) * 64] = blk
        return out
    tabi = table(4, 0, 5)
    edges = [table(0, 0, 4), table(2, 0, 4), table(60, 56, 4), table(62, 56, 4)]
    tabe = np.stack(edges, 1)
    return np.ascontiguousarray(tabi), np.ascontiguousarray(tabe)


_CACHE = {}


def kernel(x, c, ctx, c_ctx, w_mod, b_mod, g_pre, g_post, w_in, rpb, w_four, conv_dw, conv_db, conv_ln_g, conv_ln_b,
           w_pw, p_attn, p_four, p_conv, w_out):
    f32 = np.float32
    x = np.asarray(x, f32); ctx = np.asarray(ctx, f32); c = np.asarray(c, f32); c_ctx = np.asarray(c_ctx, f32)
    w_in = np.asarray(w_in, f32)
    w1 = np.ascontiguousarray(np.concatenate([w_in[:, :, 0:1536], w_in[:, :, 2048:2304], w_in[:, :, 2560:3072]], -1))
    w2 = np.ascontiguousarray(np.concatenate([w_in[:, :, 1536:2048], w_in[:, :, 2304:2560], w_in[:, :, 3072:3328], w_in[:, :, 3328:6400]], -1))
    tabi, tabe = _bias_tables(rpb)
    shared = dict(
        w_mod=np.asarray(w_mod, f32),
        bmod=np.stack([_fm(b_mod[l], 24) for l in range(DEPTH)]),
        gpre=np.stack([_fm(g_pre[l], 8) for l in range(DEPTH)]),
        gpost=np.stack([_fm(g_post[l], 8) for l in range(DEPTH)]),
        w1=w1, w2=w2, tabi=tabi, tabe=tabe,
        w_four=np.asarray(w_four, f32),
        dwfm=np.ascontiguousarray(np.asarray(conv_dw, f32).transpose(0, 2, 1).reshape(DEPTH, 2, 128, 31).transpose(0, 2, 1, 3)),
        cvec=np.stack([np.stack([_fm(conv_db[l], 2), _fm(conv_ln_g[l], 2), _fm(conv_ln_b[l], 2)], 1) for l in range(DEPTH)]),
        w_pw=np.asarray(w_pw, f32), p_attn=np.asarray(p_attn, f32), p_four=np.asarray(p_four, f32),
        p_conv=np.asarray(p_conv, f32), w_out=np.asarray(w_out, f32),
    )
    shared.update(_consts())
    in_maps = []
    for core in range(8):
        b = core % 4
        m = dict(shared)
        m["xin"] = np.ascontiguousarray(np.concatenate([x[b], ctx[b]], 0))
        m["cfm"] = np.ascontiguousarray(np.stack([_fm(c[b], 8), _fm(c_ctx, 8)], -1))
        in_maps.append(m)
    if "nc" not in _CACHE:
        nc, S = build(debug=DEBUG, nlayers=ACTIVE_LAYERS)
        _emit(nc, S)
        _CACHE["nc"] = nc
    nc = _CACHE["nc"]
    res = run_bass_kernel_spmd(nc, in_maps, core_ids=list(range(8)))
    _CACHE["res"] = res
    out = np.stack([np.asarray(res.results[b]["y"], f32) for b in range(4)], 0)
    return out
```

```python
import numpy as np
import ml_dtypes
import concourse.bass as bass
import concourse.mybir as mybir
from concourse.bass_utils import run_bass_kernel_spmd

F32 = mybir.dt.float32
BF16 = mybir.dt.bfloat16
AF = mybir.ActivationFunctionType
ALU = mybir.AluOpType

D = 1024; NL = 4096; NC_ = 256; NT = NL + NC_; DEPTH = 2
W1C = 2304; W2C = 4096
EPS = 1e-6
NEG = -30000.0
DEBUG = False
ACTIVE_LAYERS = 2


class Op:
    __slots__ = ("eng", "fn", "dma", "idx", "waits", "needs_inc", "semi", "val", "slot")


class Sched:
    ENGS = ("tensor", "vector", "scalar", "gpsimd", "sync")
    K = 8
    ROT = 8000

    def __init__(self):
        self.ops = {e: [] for e in self.ENGS}
        self.lw = {}
        self.rd = {}
        self.waited = {e: {} for e in self.ENGS}
        self.waited_dma = {e: set() for e in self.ENGS}
        self.dma_ops = {e: [] for e in self.ENGS}
        self.final = []
        self._pending = {}

    def add(self, eng, fn, reads=(), writes=(), dma=False):
        op = Op()
        op.eng = eng; op.fn = fn; op.dma = dma; op.waits = []; op.needs_inc = dma
        op.semi = 0; op.val = 0; op.slot = 0
        deps = []
        pb = self._pending.pop(eng, None)
        if pb:
            deps.extend(pb)
        for k in reads:
            w = self.lw.get(k)
            if w is not None:
                deps.append(w)
            if k.startswith("pb") or k.startswith("pt"):
                r = self.rd.get(k)
                if r:
                    deps.extend(o for e2, o in r[0].items() if e2 != eng)
        for k in writes:
            w = self.lw.get(k)
            if w is not None:
                deps.append(w)
            r = self.rd.get(k)
            if r:
                deps.extend(r[0].values()); deps.extend(r[1])
        if dma:
            n = len(self.dma_ops[eng])
            op.slot = n % self.K; op.val = 16 * (n // self.K + 1)
            if n >= self.K:
                deps.append(self.dma_ops[eng][n - self.K])
            self.dma_ops[eng].append(op)
        for d in deps:
            if d is op:
                continue
            if d.dma:
                if d in self.waited_dma[eng]:
                    continue
                self.waited_dma[eng].add(d); op.waits.append(d)
            else:
                if d.eng == eng and eng == "tensor":
                    continue
                if self.waited[eng].get(d.eng, -1) >= d.idx:
                    continue
                self.waited[eng][d.eng] = d.idx
                d.needs_inc = True; op.waits.append(d)
        op.idx = len(self.ops[eng]); self.ops[eng].append(op)
        for k in reads:
            r = self.rd.setdefault(k, [{}, []])
            if dma:
                r[1].append(op)
            else:
                r[0][eng] = op
        for k in writes:
            self.lw[k] = op; self.rd[k] = [{}, []]
        return op

    def barrier(self):
        lasts = []
        for e in self.ENGS:
            for o in reversed(self.ops[e]):
                if not o.dma:
                    lasts.append(o)
                    break
            lasts.extend(self.dma_ops[e][-self.K:])
        for e in self.ENGS:
            self._pending[e] = list(lasts)

    def emit(self, nc, block, csem, dsem):
        for e in self.ENGS:
            cnt = 0
            for op in self.ops[e]:
                if not op.dma and op.needs_inc:
                    op.semi = cnt // self.ROT; op.val = cnt % self.ROT + 1; cnt += 1
            assert cnt // self.ROT < len(csem[e]), (e, cnt)
        final = self.final

        def run(e, eng):
            for op in self.ops[e]:
                for d in op.waits:
                    if d.dma:
                        eng.wait_ge(dsem[d.eng][d.slot], d.val)
                    else:
                        eng.wait_ge(csem[d.eng][d.semi], d.val)
                ins = op.fn(eng)
                if op.needs_inc:
                    if op.dma:
                        ins.then_inc(dsem[e][op.slot], 16)
                    else:
                        ins.then_inc(csem[e][op.semi], 1)
            if e == "sync":
                for d in final:
                    eng.wait_ge(dsem[d.eng][d.slot], d.val)

        block.tensor(lambda t: run("tensor", t))
        block.vector(lambda t: run("vector", t))
        block.scalar(lambda t: run("scalar", t))
        block.gpsimd(lambda t: run("gpsimd", t))
        block.sync(lambda t: run("sync", t))


class SBA:
    BASE = 16640
    END = 229376

    def __init__(self, nc):
        self.nc = nc; self.top = self.BASE; self.n = 0

    def alloc(self, name, shape, dtype):
        per = 1
        for s in shape[1:]:
            per *= s
        nbytes = per * (4 if dtype == F32 else 2)
        off = (self.top + 63) // 64 * 64
        assert off + nbytes <= self.END, (name, off, nbytes)
        self.top = off + nbytes
        self.hw = max(getattr(self, "hw", 0), self.top)
        self.n += 1
        key = (name, tuple(shape), str(dtype), off)
        cache = self.__dict__.setdefault("cache", {})
        if key not in cache:
            cache[key] = self.nc.alloc_sbuf_tensor_at(f"{name}{self.n}", list(shape), dtype, offset=off)
        return cache[key]


class _Stop(Exception):
    pass


STOP = None
P2_BLOCKS = None
P2_PART = None


def build(debug=False, nlayers=2):
    nc = bass.Bass("TRN2", target_bir_lowering=False)

    def check(name):
        if STOP == name:
            raise _Stop()
    S = Sched()
    sb = SBA(nc)
    S.sb = sb

    def din(name, shape, dt=F32):
        return nc.dram_tensor(name, list(shape), dt, kind="ExternalInput")

    def dscr(name, shape, dt=BF16):
        return nc.dram_tensor(name, list(shape), dt, kind="ExternalOutput" if debug else "Internal")

    xin = din("xin", [NT, D])
    cfm = din("cfm", [128, 8, 2])
    w_mod = din("w_mod", [DEPTH, D, 3 * D])
    bmod_d = din("bmod", [DEPTH, 128, 24])
    gpre_d = din("gpre", [DEPTH, 128, 8])
    gpost_d = din("gpost", [DEPTH, 128, 8])
    w1_d = din("w1", [DEPTH, D, W1C])
    w2_d = din("w2", [DEPTH, D, W2C])
    tabi_d = din("tabi", [DEPTH, 8, 5, 128, 128])
    tabe_d = din("tabe", [DEPTH, 4, 8, 4, 128, 128])
    wfour_d = din("w_four", [DEPTH, 256, 256])
    dw_d = din("dwfm", [DEPTH, 128, 2, 31])
    cvec_d = din("cvec", [DEPTH, 128, 3, 2])
    wpw_d = din("w_pw", [DEPTH, 256, 256])
    pattn_d = din("p_attn", [DEPTH, 512, D])
    pfour_d = din("p_four", [DEPTH, 256, D])
    pconv_d = din("p_conv", [DEPTH, 256, D])
    wout_d = din("w_out", [DEPTH, D, D])
    identb_d = din("identb", [128, 128], BF16)
    identf_d = din("identf", [128, 128])
    onesf_d = din("onesf", [128, 128])
    cj_d = din("cj", [128, 128], BF16)
    sj_d = din("sj", [128, 128], BF16)
    m1_d = din("m1", [128, 128], BF16)
    g2_d = din("g2", [128, 64, 64], BF16)
    c256_d = din("c256", [128, 2, 256], BF16)
    s256_d = din("s256", [128, 2, 256], BF16)
    yout = nc.dram_tensor("y", [NL, D], F32, kind="ExternalOutput")

    hT_d = dscr("hT_d", [128, 8, NT])
    qT_d = dscr("qT_d", [128, 4, NT])
    kT_d = dscr("kT_d", [128, 4, NT])
    v_d = dscr("v_d", [NT, 520])
    Z_d = dscr("Z_d", [2, NL, 256])
    A_d = dscr("A_d", [2, 64, 64, 256])
    XaT_d = dscr("XaT_d", [128, 4, NT])
    XfT_d = dscr("XfT_d", [128, 2, NT])
    XcT_d = dscr("XcT_d", [128, 2, NT])
    x1_d = dscr("x1_d", [NT, D], F32)
    dbg_mod = dscr("dbg_mod", [128, 24, 2], F32)

    PB = [nc.alloc_psum_tensor(f"pb{i}", [128, 512], F32) for i in range(6)]
    PT = [nc.alloc_psum_tensor(f"pt{i}", [128, 1024], BF16) for i in range(2)]
    pbk = [f"pb{i}" for i in range(6)]
    ptk = [f"pt{i}" for i in range(2)]
    rot = {"pb": 0, "pt": 0, "ev": 0}

    def next_pb():
        i = rot["pb"]; rot["pb"] = (i + 1) % 6
        return PB[i], pbk[i]

    def next_pt():
        i = rot["pt"]; rot["pt"] = (i + 1) % 2
        return PT[i], ptk[i]

    def ev_eng():
        rot["ev"] ^= 1
        return "vector" if rot["ev"] else "scalar"

    def dma(eng, out, in_, reads, writes):
        return S.add(eng, lambda e, o=out, i=in_: e.dma_start(out=o, in_=i), reads, writes, dma=True)

    def copy_op(eng, out, in_, reads, writes):
        if eng == "scalar":
            return S.add(eng, lambda e, o=out, i=in_: e.activation(out=o, in_=i, func=AF.Identity), reads, writes)
        return S.add(eng, lambda e, o=out, i=in_: e.tensor_copy(out=o, in_=i), reads, writes)

    identb = sb.alloc("identb", [128, 128], BF16)
    identf = sb.alloc("identf", [128, 128], F32)
    onesf = sb.alloc("onesf", [128, 128], F32)
    cj = sb.alloc("cj", [128, 128], BF16)
    sj = sb.alloc("sj", [128, 128], BF16)
    m1 = sb.alloc("m1", [128, 128], BF16)
    g2 = sb.alloc("g2", [128, 64, 64], BF16)
    c256 = sb.alloc("c256", [128, 2, 256], BF16)
    s256 = sb.alloc("s256", [128, 2, 256], BF16)
    for t, dsrc, k in ((identb, identb_d, "identb"), (identf, identf_d, "identf"), (onesf, onesf_d, "onesf"),
                       (cj, cj_d, "cj"), (sj, sj_d, "sj"), (m1, m1_d, "m1"), (g2, g2_d, "g2"),
                       (c256, c256_d, "c256"), (s256, s256_d, "s256")):
        dma("sync", t[:], dsrc[:], [], [k])
    craw = sb.alloc("craw", [128, 8, 2], F32)
    sc2 = sb.alloc("sc2", [128, 8, 2], F32)
    dma("sync", craw[:], cfm[:], [], ["craw"])
    S.add("scalar", lambda e: e.activation(out=sc2[:], in_=craw[:], func=AF.Silu), ["craw"], ["sc2"])
    bmod = sb.alloc("bmod", [128, 24], F32)
    gpre = sb.alloc("gpre", [128, 8], F32)
    gpost = sb.alloc("gpost", [128, 8], F32)
    mod = sb.alloc("mod", [128, 24, 2], F32)
    amod = sb.alloc("amod", [128, 8, 2], F32)
    ggm = sb.alloc("ggm", [128, 8, 2], F32)
    ggbc = [sb.alloc("ggbc", [128, D], F32) for _ in range(2)]
    dwc = sb.alloc("dwc", [128, 2, 31], F32)
    cvec = sb.alloc("cvec", [128, 3, 2], F32)
    dgt = sb.alloc("dgt", [128, 128], F32)
    epsc = sb.alloc("epsc", [128, 1], F32)
    zc = sb.alloc("zc", [128, 2, 2, 256], BF16)
    S.add("vector", lambda e: e.memset(epsc[:], EPS), [], ["epsc"])
    GLOBAL_TOP = sb.top

    try:
        for l in range(nlayers):
            last = (l == DEPTH - 1)
            xsrc = xin if l == 0 else x1_d
            xsrc_k = "xin" if l == 0 else "x1_d"
            sb.top = GLOBAL_TOP
            dma("sync", bmod[:], bmod_d[l], [], ["bmod"])
            dma("sync", gpre[:], gpre_d[l], [], ["gpre"])
            dma("sync", gpost[:], gpost_d[l], [], ["gpost"])
            dma("sync", dwc[:], dw_d[l], [], ["dwc"])
            dma("sync", cvec[:], cvec_d[l], [], ["cvec"])
            wm = [sb.alloc("wm", [128, 8, 512], F32) for _ in range(2)]
            wmv = w_mod[l].rearrange("(k p) c -> p k c", p=128)
            psM, psMk = PB[0], pbk[0]
            psMv = psM[:, 0:48].rearrange("p (j v) -> p j v", v=2)
            for s in range(6):
                b = s % 2
                dma("sync", wm[b][:], wmv[:, :, s * 512:(s + 1) * 512], [], [f"wm{b}"])

                def mm_mod(e, s=s, b=b):
                    ins = None
                    for cc in range(4):
                        j = s * 4 + cc
                        for k in range(8):
                            ins = e.matmul(psMv[:, j, :], lhsT=wm[b][:, k, cc * 128:(cc + 1) * 128], rhs=sc2[:, k, :],
                                           start=(k == 0), stop=(k == 7))
                    return ins
                S.add("tensor", mm_mod, [f"wm{b}", "sc2"], [psMk])
            for v in range(2):
                S.add("vector", lambda e, v=v: e.tensor_tensor(out=mod[:, :, v], in0=psMv[:, :, v], in1=bmod[:], op=ALU.add),
                      [psMk, "bmod"], ["mod"])
            for v in range(2):
                S.add("vector", lambda e, v=v: e.scalar_tensor_tensor(out=amod[:, :, v], in0=mod[:, 8:16, v], scalar=1.0,
                                                                       in1=gpre[:], op0=ALU.add, op1=ALU.mult),
                      ["mod", "gpre"], ["amod"])
                S.add("vector", lambda e, v=v: e.tensor_tensor(out=ggm[:, :, v], in0=mod[:, 16:24, v], in1=gpost[:], op=ALU.mult),
                      ["mod", "gpost"], ["ggm"])
            if debug and l == 0:
                dma("sync", dbg_mod[:], mod[:], ["mod"], ["dbg_mod"])
            for v in range(2):
                for half in range(2):
                    ps, pk = next_pb()
                    for jj in range(4):
                        j = half * 4 + jj
                        S.add("vector", lambda e, j=j, v=v: e.tensor_scalar(out=dgt[:], in0=identf[:], scalar1=ggm[:, j, v:v + 1],
                                                                              scalar2=None, op0=ALU.mult),
                              ["identf", "ggm"], ["dgt"])
                        S.add("tensor", lambda e, ps=ps, jj=jj: e.matmul(ps[:, jj * 128:(jj + 1) * 128], lhsT=onesf[:], rhs=dgt[:],
                                                                          start=True, stop=True),
                              ["onesf", "dgt"], [pk])
                    copy_op("vector", ggbc[v][:, half * 512:(half + 1) * 512], ps[:, :], [pk], [f"ggbc{v}"])
            MOD_TOP = GLOBAL_TOP
            check(f"mod{l}")

            sb.top = MOD_TOP + 2 * 16384 + 128
            P1_BASE = sb.top
            uT = sb.alloc("uT", [128, 2, NL + 30], BF16)
            uTc = sb.alloc("uTc", [128, 2, NC_ + 30], BF16)
            for j in range(2):
                S.add("gpsimd", lambda e, j=j: e.memset(uT[:, j, 0:15], 0.0), [], ["uT"])
                S.add("gpsimd", lambda e, j=j: e.memset(uT[:, j, NL + 15:NL + 30], 0.0), [], ["uT"])
                S.add("gpsimd", lambda e, j=j: e.memset(uTc[:, j, 0:15], 0.0), [], ["uTc"])
                S.add("gpsimd", lambda e, j=j: e.memset(uTc[:, j, NC_ + 15:NC_ + 30], 0.0), [], ["uTc"])
            CONV_KEEP = sb.top
            w1 = sb.alloc("w1", [128, 8, W1C], BF16)
            w1v = w1_d[l].rearrange("(k p) c -> p k c", p=128)
            for cg in range(5):
                ca, cb_ = cg * 512, min(W1C, (cg + 1) * 512)
                dma("gpsimd", w1[:, :, ca:cb_], w1v[:, :, ca:cb_], [], [f"w1_{cg}"])
            xt = [sb.alloc("xt", [128, D], F32) for _ in range(2)]
            junk = sb.alloc("junk", [128, D], BF16)
            xn = [sb.alloc("xn", [128, D], BF16) for _ in range(2)]
            ssq = [sb.alloc("ssq", [128, 1], F32) for _ in range(2)]
            rstd = [sb.alloc("rstd", [128, 1], F32) for _ in range(2)]
            hT = [sb.alloc("hT", [128, 8, 512], BF16) for _ in range(2)]
            qs = [sb.alloc("qs", [128, 4, 512], BF16) for _ in range(2)]
            ks = [sb.alloc("ks", [128, 4, 512], BF16) for _ in range(2)]
            vt = [sb.alloc("vt", [128, 8, 65], BF16) for _ in range(3)]
            ufT = [sb.alloc("ufT", [128, 2, 512], BF16) for _ in range(2)]
            zst = [sb.alloc("zst", [128, 2, 256], BF16) for _ in range(2)]
            sig = [sb.alloc("sig", [128, 512], F32) for _ in range(2)]
            for i in range(3):
                S.add("gpsimd", lambda e, i=i: e.memset(vt[i][:, :, 64:65], 1.0), [], [f"vt{i}"])
            nblk = 9

            def blk_info(tb):
                ctxb = (tb == 8)
                ntile = 2 if ctxb else 4
                return ctxb, ntile, ntile * 128, (1 if ctxb else 0), tb % 2

            def normA(tb, t):
                ctxb, ntile, ntok, v, hb = blk_info(tb)
                r0 = tb * 512 + t * 128
                xb = (tb * 4 + t) % 2
                dma("sync", xt[xb][:], xsrc[r0:r0 + 128, :], [xsrc_k + f"_{r0 // 128}"], [f"xt{xb}"])
                S.add("vector", lambda e: e.memset(ssq[xb][:], 0.0), [], [f"ssq{xb}"])
                S.add("scalar", lambda e: e.activation(out=junk[:], in_=xt[xb][:], func=AF.Square, accum_out=ssq[xb][:]),
                      [f"xt{xb}", f"ssq{xb}"], ["junk", f"ssq{xb}"])
                S.add("scalar", lambda e: e.activation(out=rstd[xb][:], in_=ssq[xb][:], func=AF.Sqrt, scale=1.0 / D, bias=epsc[:, 0:1]),
                      [f"ssq{xb}", "epsc"], [f"rstd{xb}"])
                S.add("vector", lambda e: e.reciprocal(out=rstd[xb][:], in_=rstd[xb][:]), [f"rstd{xb}"], [f"rstd{xb}"])
                S.add("vector", lambda e: e.tensor_scalar(out=xn[xb][:], in0=xt[xb][:], scalar1=rstd[xb][:, 0:1], scalar2=None, op0=ALU.mult),
                      [f"xt{xb}", f"rstd{xb}"], [f"xn{xb}"])

            def normB(tb, t):
                ctxb, ntile, ntok, v, hb = blk_info(tb)
                xb = (tb * 4 + t) % 2
                pt, ptkey = next_pt()
                ptv = pt[:, :].rearrange("p (k c) -> p k c", c=128)

                def tr8(e):
                    ins = None
                    for k in range(8):
                        ins = e.transpose(ptv[:, k, :], xn[xb][:, k * 128:(k + 1) * 128], identb[:])
                    return ins
                S.add("tensor", tr8, [f"xn{xb}", "identb"], [ptkey])
                for k in range(8):
                    if k % 2 == 0:
                        S.add("vector", lambda e, k=k: e.tensor_scalar(
                            out=hT[hb][:, k, t * 128:(t + 1) * 128], in0=ptv[:, k, :], scalar1=amod[:, k, v:v + 1],
                            scalar2=mod[:, k, v:v + 1], op0=ALU.mult, op1=ALU.add),
                            [ptkey, "amod", "mod"], [f"hT{hb}"])
                    else:
                        S.add("scalar", lambda e, k=k: e.activation(
                            out=hT[hb][:, k, t * 128:(t + 1) * 128], in_=ptv[:, k, :], func=AF.Identity,
                            scale=amod[:, k, v:v + 1], bias=mod[:, k, v:v + 1]),
                            [ptkey, "amod", "mod"], [f"hT{hb}"])
                if t == ntile - 1:
                    c0 = tb * 512
                    dma("sync", hT_d[:, :, c0:c0 + ntok], hT[hb][:, :, 0:ntok], [f"hT{hb}"], [f"hT_d{tb}"])

            def units(tb):
                ctxb, ntile, ntok, v, hb = blk_info(tb)
                c0 = tb * 512
                us = []

                def proj_fm(col0):
                    ps, pk = next_pb()

                    def f(e):
                        ins = None
                        for k in range(8):
                            ins = e.matmul(ps[:, 0:ntok], lhsT=w1[:, k, col0:col0 + 128], rhs=hT[hb][:, k, 0:ntok],
                                           start=(k == 0), stop=(k == 7))
                        return ins
                    S.add("tensor", f, [f"w1_{col0 // 512}", f"hT{hb}"], [pk])
                    return ps, pk
                need_q = not (ctxb and last)
                for nm, stg, dd, base, need in (("q", qs, qT_d, 0, need_q), ("k", ks, kT_d, 512, True)):
                    if not need:
                        continue
                    for cch in range(4):
                        def u(nm=nm, stg=stg, dd=dd, base=base, cch=cch):
                            ps, pk = proj_fm(base + cch * 128)
                            copy_op(ev_eng(), stg[hb][:, cch, 0:ntok], ps[:, 0:ntok], [pk], [f"{nm}s{hb}"])
                            if cch == 3:
                                dma("sync", dd[:, :, c0:c0 + ntok], stg[hb][:, :, 0:ntok], [f"{nm}s{hb}"], [f"{nm}T_d"])
                        us.append(u)
                for t in range(ntile):
                    def u(t=t):
                        ps, pk = next_pb()
                        vb = (tb * 4 + t) % 3

                        def fv(e):
                            ins = None
                            for k in range(8):
                                ins = e.matmul(ps[:, :], lhsT=hT[hb][:, k, t * 128:(t + 1) * 128], rhs=w1[:, k, 1024:1536],
                                               start=(k == 0), stop=(k == 7))
                            return ins
                        S.add("tensor", fv, ["w1_2", f"hT{hb}"], [pk])
                        copy_op(ev_eng(), vt[vb][:, :, 0:64], ps[:, :].rearrange("p (h d) -> p h d", d=64), [pk], [f"vt{vb}"])
                        r0 = c0 + t * 128
                        dma("sync", v_d[r0:r0 + 128, :], vt[vb][:].rearrange("p h d -> p (h d)"), [f"vt{vb}"], ["v_d"])
                    us.append(u)
                if not (ctxb and last):
                    for j in range(2):
                        def u(j=j):
                            ps, pk = proj_fm(1536 + j * 128)
                            copy_op(ev_eng(), ufT[hb][:, j, 0:ntok], ps[:, 0:ntok], [pk], [f"ufT{hb}"])
                        us.append(u)
                    for t in range(ntile):
                        def u(t=t):
                            ps, pk = next_pb()
                            psz = ps[:, :].rearrange("p (r m) -> p r m", m=256)

                            def fz(e):
                                ins = None
                                for r, tabm in ((0, cj), (1, sj)):
                                    for j in range(2):
                                        ins = e.matmul(psz[:, r, j * 128:(j + 1) * 128], lhsT=ufT[hb][:, j, t * 128:(t + 1) * 128],
                                                       rhs=tabm[:], start=True, stop=True)
                                return ins
                            S.add("tensor", fz, [f"ufT{hb}", "cj", "sj"], [pk])
                            if ctxb:
                                copy_op(ev_eng(), zc[:, t, :, :], psz, [pk], ["zc"])
                            else:
                                zb = t % 2
                                copy_op(ev_eng(), zst[zb][:], psz, [pk], [f"zst{zb}"])
                                r0 = c0 + t * 128
                                for r in range(2):
                                    dma("sync", Z_d[r, r0:r0 + 128, :], zst[zb][:, r, :], [f"zst{zb}"], ["Z_d"])
                        us.append(u)
                    for j in range(2):
                        def u(j=j):
                            sgb = j
                            ps, pk = proj_fm(2048 + j * 128)
                            S.add("scalar", lambda e: e.activation(out=sig[sgb][:, 0:ntok], in_=ps[:, 0:ntok], func=AF.Sigmoid), [pk], [f"sig{sgb}"])
                            ps2, pk2 = proj_fm(1792 + j * 128)
                            dst = uTc[:, j, 15:15 + ntok] if ctxb else uT[:, j, 15 + c0:15 + c0 + ntok]
                            S.add("vector", lambda e: e.tensor_tensor(out=dst, in0=ps2[:, 0:ntok], in1=sig[sgb][:, 0:ntok], op=ALU.mult),
                                  [pk2, f"sig{sgb}"], ["uTc" if ctxb else "uT"])
                        us.append(u)
                return us

            for t in range(4):
                normA(0, t)
                normB(0, t)
            for tb in range(nblk):
                us = units(tb)
                nxt = tb + 1 if tb + 1 < nblk else None
                nt_next = blk_info(nxt)[1] if nxt is not None else 0
                nu = len(us)
                slots = {}
                if nxt is not None:
                    step = max(1, nu // (nt_next + 1))
                    for t in range(nt_next):
                        slots.setdefault(min(nu - 1, step * t), []).append(("A", t))
                        slots.setdefault(min(nu - 1, step * (t + 1)), []).append(("B", t))
                for ui, u in enumerate(us):
                    u()
                    for kind, t in slots.get(ui, []):
                        if kind == "A":
                            normA(nxt, t)
                        else:
                            normB(nxt, t)
            S.barrier()
            check(f"p1{l}")
            sb.top = CONV_KEEP
            wpw = sb.alloc("wpw", [128, 2, 256], BF16)
            dma("gpsimd", wpw[:], wpw_d[l].rearrange("(k p) c -> p k c", p=128), [], ["wpw"])
            acc = [sb.alloc("acc", [128, 2, 512], F32) for _ in range(3)]
            dg = sb.alloc("dg", [128, 62, 128], BF16)
            for j in range(2):
                for kk in range(31):
                    S.add("vector", lambda e, j=j, kk=kk: e.tensor_scalar(out=dg[:, j * 31 + kk, :], in0=identf[:], scalar1=dwc[:, j, kk:kk + 1],
                                                                          scalar2=None, op0=ALU.mult), ["identf", "dwc"], ["dg"])
            sq = sb.alloc("sq", [128, 2, 512], F32)
            ctmp = sb.alloc("ctmp", [128, 512], F32)
            var = sb.alloc("var", [128, 512], F32)
            msq = sb.alloc("msq", [128, 512], F32)
            tn = sb.alloc("tn", [128, 2, 512], F32)
            zz = [sb.alloc("zz", [128, 2, 512], BF16) for _ in range(2)]
            xcs = [sb.alloc("xcs", [128, 2, 512], BF16) for _ in range(2)]
            def convA(tb):
                ctxb = (tb == 8)
                ntok = 256 if ctxb else 512
                c0 = tb * 512
                ab = tb % 3
                zb_ = tb % 2
                src = uTc if ctxb else uT
                off = 0 if ctxb else c0
                skey = "uTc" if ctxb else "uT"
                for j in range(2):
                    ps, pk = next_pb()

                    def fconv(e, ps=ps, j=j, src=src, off=off, ntok=ntok):
                        ins = None
                        for kk in range(31):
                            ins = e.matmul(ps[:, 0:ntok], lhsT=dg[:, j * 31 + kk, :], rhs=src[:, j, off + kk:off + kk + ntok],
                                           start=(kk == 0), stop=(kk == 30))
                        return ins
                    S.add("tensor", fconv, [skey, "dg"], [pk])
                    S.add("vector", lambda e, ps=ps, j=j, ab=ab, zb_=zb_, ntok=ntok: e.tensor_scalar(
                        out=acc[ab][:, j, 0:ntok], in0=ps[:, 0:ntok], scalar1=cvec[:, 0, j:j + 1], scalar2=None, op0=ALU.add),
                        [pk, "cvec"], [f"acc{ab}_{j}"])
            def convB(tb):
                ctxb = (tb == 8)
                ntok = 256 if ctxb else 512
                c0 = tb * 512
                ab = tb % 3
                zb_ = tb % 2
                akeys = [f"acc{ab}_0", f"acc{ab}_1"]
                S.add("scalar", lambda e, ab=ab, zb_=zb_, ntok=ntok: e.activation(out=sq[:, :, 0:ntok], in_=acc[ab][:, :, 0:ntok], func=AF.Square),
                      akeys, ["sq"])
                psm, pkm = next_pb()
                pss, pks = next_pb()

                def fstat(e, psm=psm, pss=pss, ab=ab, zb_=zb_, ntok=ntok):
                    ins = None
                    for j in range(2):
                        ins = e.matmul(psm[:, 0:ntok], lhsT=onesf[:], rhs=acc[ab][:, j, 0:ntok], start=(j == 0), stop=(j == 1))
                    for j in range(2):
                        ins = e.matmul(pss[:, 0:ntok], lhsT=onesf[:], rhs=sq[:, j, 0:ntok], start=(j == 0), stop=(j == 1))
                    return ins
                S.add("tensor", fstat, akeys + ["sq", "onesf"], [pkm, pks])
                S.add("scalar", lambda e, psm=psm, ntok=ntok: e.activation(out=var[:, 0:ntok], in_=psm[:, 0:ntok], func=AF.Square,
                                                                          scale=1.0 / 256), [pkm], ["var"])
                S.add("vector", lambda e, pss=pss, ntok=ntok: e.scalar_tensor_tensor(
                    out=var[:, 0:ntok], in0=pss[:, 0:ntok], scalar=1.0 / 256, in1=var[:, 0:ntok], op0=ALU.mult, op1=ALU.subtract),
                    [pks, "var"], ["var"])
                S.add("scalar", lambda e, ntok=ntok: e.activation(out=var[:, 0:ntok], in_=var[:, 0:ntok], func=AF.Sqrt, bias=epsc[:, 0:1]),
                      ["var", "epsc"], ["var"])
                S.add("vector", lambda e, ntok=ntok: e.reciprocal(out=var[:, 0:ntok], in_=var[:, 0:ntok]), ["var"], ["var"])
                for j in range(2):
                    S.add("vector", lambda e, j=j, psm=psm, ab=ab, zb_=zb_, ntok=ntok: e.scalar_tensor_tensor(
                        out=tn[:, j, 0:ntok], in0=psm[:, 0:ntok], scalar=-1.0 / 256, in1=acc[ab][:, j, 0:ntok],
                        op0=ALU.mult, op1=ALU.add), [pkm] + akeys, [f"tn{j}"])
                    S.add("vector", lambda e, j=j, ntok=ntok: e.tensor_tensor(out=tn[:, j, 0:ntok], in0=tn[:, j, 0:ntok], in1=var[:, 0:ntok],
                                                                            op=ALU.mult), [f"tn{j}", "var"], [f"tn{j}"])
                    S.add("scalar", lambda e, j=j, ab=ab, zb_=zb_, ntok=ntok: e.activation(out=zz[zb_][:, j, 0:ntok], in_=tn[:, j, 0:ntok], func=AF.Silu,
                                                                               scale=cvec[:, 1, j:j + 1], bias=cvec[:, 2, j:j + 1]),
                          [f"tn{j}", "cvec"], [f"zz{zb_}"])
            def convB2(tb):
                ctxb = (tb == 8)
                ntok = 256 if ctxb else 512
                c0 = tb * 512
                ab = tb % 3
                zb_ = tb % 2
                for mc in range(2):
                    ps, pk = next_pb()

                    def fpw(e, ps=ps, mc=mc, ab=ab, zb_=zb_, ntok=ntok):
                        ins = None
                        for j in range(2):
                            ins = e.matmul(ps[:, 0:ntok], lhsT=wpw[:, j, mc * 128:(mc + 1) * 128], rhs=zz[zb_][:, j, 0:ntok],
                                           start=(j == 0), stop=(j == 1))
                        return ins
                    S.add("tensor", fpw, ["wpw", f"zz{zb_}"], [pk])
                    copy_op(ev_eng(), xcs[zb_][:, mc, 0:ntok], ps[:, 0:ntok], [pk], [f"xcs{zb_}"])
                dma("sync", XcT_d[:, :, c0:c0 + ntok], xcs[zb_][:, :, 0:ntok], [f"xcs{zb_}"], ["XcT_d"])

            ncb = nblk if not last else 8
            wfr = sb.alloc("wfr", [128, 2, 256], BF16)
            dma("gpsimd", wfr[:], wfour_d[l].rearrange("(k p) c -> p k c", p=128), [], ["wfr"])
            zstack = sb.alloc("zstack", [128, 16384], BF16)
            astack = zstack[:, :].rearrange("p (k m) -> p k m", m=256)
            asb = [sb.alloc("asb", [128, 2048], BF16) for _ in range(2)]
            fourT = sb.alloc("fourT", [128, 2, NT], BF16)
            xfs = [sb.alloc("xfs", [128, 2, 512], BF16) for _ in range(2)]
            A_dv = A_d.rearrange("r k n m -> (r k) (n m)")
            fT4 = [fourT[:, mc, 0:NL].rearrange("p (a b) -> p a b", b=64) for mc in range(2)]

            def fft_load_z():
                for r in range(2):
                    dma("sync", zstack[r * 64:(r + 1) * 64, :], Z_d[r].rearrange("(a b) m -> a (b m)", b=64), ["Z_d"], ["zstack"])

            def fft_stage1(g0, g1):
                for g in range(g0, g1):
                    gb = g % 2
                    for cbk in range(4):
                        ps, pk = next_pb()
                        col = g * 2048 + cbk * 512
                        S.add("tensor", lambda e, ps=ps, col=col: e.matmul(ps[:, :], lhsT=m1[:], rhs=zstack[:, col:col + 512], start=True, stop=True),
                              ["m1", "zstack"], [pk])
                        copy_op(ev_eng(), asb[gb][:, cbk * 512:(cbk + 1) * 512], ps[:, :], [pk], [f"asb{gb}"])
                    dma("sync", A_dv[:, g * 2048:(g + 1) * 2048], asb[gb][:], [f"asb{gb}"], ["A_d"])

            def fft_load_a():
                for r in range(2):
                    dma("sync", astack[r * 64:(r + 1) * 64, :, :], A_d[r].rearrange("k n m -> n k m"), ["A_d"], ["zstack"])

            def fft_stage2(g0, g1):
                for g in range(g0, g1):
                    for mc in range(2):
                        ps, pk = next_pb()
                        psv = ps[:, :].rearrange("p (a b) -> p a b", b=8)

                        def fs2(e, psv=psv, g=g, mc=mc):
                            ins = None
                            for kl in range(8):
                                k1 = g * 8 + kl
                                ins = e.matmul(psv[:, :, kl], lhsT=astack[:, k1, mc * 128:(mc + 1) * 128], rhs=g2[:, k1, :], start=True, stop=True)
                            return ins
                        S.add("tensor", fs2, ["zstack", "g2"], [pk])
                        copy_op(ev_eng(), fT4[mc][:, :, g * 8:(g + 1) * 8], psv, [pk], ["fourT"])

            def fft_ctx():
                for mc in range(2):
                    ps, pk = next_pb()

                    def fc(e, ps=ps, mc=mc):
                        ins = None
                        n = 0
                        for nt_ in range(2):
                            for r, tabm in ((0, c256), (1, s256)):
                                ins = e.matmul(ps[:, 0:256], lhsT=zc[:, nt_, r, mc * 128:(mc + 1) * 128], rhs=tabm[:, nt_, :],
                                               start=(n == 0), stop=(n == 3))
                                n += 1
                        return ins
                    S.add("tensor", fc, ["zc", "c256", "s256"], [pk])
                    copy_op(ev_eng(), fourT[:, mc, NL:NT], ps[:, 0:256], [pk], ["fourT"])

            def fft_fw(t0, t1):
                for tb in range(t0, t1):
                    ntok = 256 if tb == 8 else 512
                    c0 = tb * 512
                    fb = tb % 2
                    for mc in range(2):
                        ps, pk = next_pb()

                        def fw(e, ps=ps, mc=mc, c0=c0, ntok=ntok):
                            ins = None
                            for j in range(2):
                                ins = e.matmul(ps[:, 0:ntok], lhsT=wfr[:, j, mc * 128:(mc + 1) * 128], rhs=fourT[:, j, c0:c0 + ntok],
                                               start=(j == 0), stop=(j == 1))
                            return ins
                        S.add("tensor", fw, ["wfr", "fourT"], [pk])
                        copy_op(ev_eng(), xfs[fb][:, mc, 0:ntok], ps[:, 0:ntok], [pk], [f"xfs{fb}"])
                    dma("sync", XfT_d[:, :, c0:c0 + ntok], xfs[fb][:, :, 0:ntok], [f"xfs{fb}"], ["XfT_d"])

            nfw = 9 if not last else 8
            fft_sched = {0: [lambda: fft_stage1(0, 4)], 1: [lambda: fft_stage1(4, 8), fft_load_a],
                         3: [lambda: fft_stage2(0, 4)], 4: [lambda: fft_stage2(4, 8)] + ([fft_ctx] if not last else []),
                         5: [lambda: fft_fw(0, 3)], 6: [lambda: fft_fw(3, 6)], 7: [lambda: fft_fw(6, nfw)]}
            fft_load_z()
            convA(0)
            if ncb > 1:
                convA(1)
            for tb in range(ncb):
                convB(tb)
                if tb + 2 < ncb:
                    convA(tb + 2)
                for fn_ in fft_sched.get(tb, []):
                    fn_()
                convB2(tb)
            S.barrier()

            check(f"fft{l}")
            sb.top = MOD_TOP
            qz = [sb.alloc("qz", [128, 4, 2, 512], BF16) for _ in range(2)]
            for b_ in range(2):
                S.add("vector", lambda e, b_=b_: e.memset(qz[b_][:], 0.0), [], [f"qz{b_}"])

            def load_q(c):
                ca = c * 512
                n_ = min(NT, ca + 512) - ca
                for par in range(2):
                    dma("sync", qz[c % 2][par * 64:(par + 1) * 64, :, par, 0:n_], qT_d[par * 64:(par + 1) * 64, :, ca:ca + n_],
                        ["qT_d"], [f"qz{c % 2}"])
            ksb = sb.alloc("ksb", [128, 4, NT], BF16)
            vsb = sb.alloc("vsb", [128, 34, 520], BF16)
            vdv = v_d.rearrange("(t p) f -> p t f", p=128)
            for c in [8] + list(range(8)):
                ca = c * 512
                cb_ = min(NT, ca + 512)
                ta, tb_ = c * 4, min(34, c * 4 + 4)
                dma("sync", ksb[:, :, ca:cb_], kT_d[:, :, ca:cb_], ["kT_d"], [f"ksb_{c}"])
                dma("sync", vsb[:, ta:tb_, :], vdv[:, ta:tb_, :], ["v_d"], [f"vsb_{c}"])
            etab = sb.alloc("etab", [128, 8, 5, 128], BF16)
            eedge = sb.alloc("eedge", [128, 8, 4, 128], BF16)
            tstg = [sb.alloc("tstg", [128, 5, 128], F32) for _ in range(2)]
            pT = [sb.alloc("pT", [128, 8, 128], BF16) for _ in range(3)]
            atile = [sb.alloc("atile", [128, 512], BF16) for _ in range(2)]
            xas = [sb.alloc("xas", [128, 4, 512], BF16) for _ in range(2)]
            rec = [sb.alloc("rec", [128, 1], F32) for _ in range(2)]
            for h in range(8):
                tsb = h % 2
                dma("sync", tstg[tsb][:], tabi_d[l, h].rearrange("j k q -> k j q"), [], [f"tstg{tsb}"])
                S.add("scalar", lambda e, h=h, tsb=tsb: e.activation(out=etab[:, h, :, :], in_=tstg[tsb][:], func=AF.Exp),
                      [f"tstg{tsb}"], ["etab"])
            nqb = 34 if not last else 32
            items = []
            hcount = 0
            for i in range(nqb):
                ctxq = (i >= 32)
                r0 = 2 * i
                edge = None
                if ctxq:
                    nwt = 0; kt0 = 0
                elif r0 < 4:
                    nwt = 4; kt0 = 0; edge = r0 // 2
                elif r0 >= 60:
                    nwt = 4; kt0 = 28; edge = 2 + (r0 - 60) // 2
                else:
                    nwt = 5; kt0 = (r0 - 4) // 2
                ktiles = [kt0 + j for j in range(nwt)] + [32, 33]
                for h in range(8):
                    items.append((i, h, edge, nwt, ktiles, hcount))
                    hcount += 1

            nchunk = (nqb + 3) // 4
            load_q(0)

            def stage_a(i, h, edge, nwt, ktiles, cnt_):
                pb_ = cnt_ % 2
                pi_ = cnt_ % 3
                nk = len(ktiles)
                if h == 0 and i % 4 == 0 and i // 4 + 1 < nchunk:
                    load_q(i // 4 + 1)
                if edge is not None and h == 0:
                    for hh in range(8):
                        tsb = hh % 2
                        dma("sync", tstg[tsb][:, 0:4, :], tabe_d[l, edge, hh].rearrange("j k q -> k j q"), [], [f"tstg{tsb}"])
                        S.add("scalar", lambda e, hh=hh, tsb=tsb: e.activation(out=eedge[:, hh, :, :], in_=tstg[tsb][:, 0:4, :], func=AF.Exp),
                              [f"tstg{tsb}"], ["eedge"])
                hp = h // 2; po = (h % 2) * 64
                psA, pkA = PB[2 * pb_], pbk[2 * pb_]
                psB, pkB = PB[2 * pb_ + 1], pbk[2 * pb_ + 1]
                psAv = psA[:, :].rearrange("p (j q) -> p j q", q=128)
                psBv = psB[:, :].rearrange("p (j q) -> p j q", q=128)

                def fqk(e):
                    ins = None
                    for j, kt in enumerate(ktiles):
                        o = psAv[:, j, :] if j < 4 else psBv[:, j - 4, :]
                        ins = e.matmul(o, lhsT=ksb[:, hp, kt * 128:(kt + 1) * 128], rhs=qz[(i // 4) % 2][:, hp, h % 2, (i % 4) * 128:(i % 4 + 1) * 128],
                                       start=True, stop=True)
                    return ins
                S.add("tensor", fqk, sorted({f"ksb_{kt // 4}" for kt in ktiles}) + [f"qz{(i // 4) % 2}"], [pkA, pkB])
                na = min(4, nk)
                S.add("scalar", lambda e: e.activation(out=pT[pi_][:, 0:na, :], in_=psAv[:, 0:na, :], func=AF.Exp, scale=0.125),
                      [pkA], [f"pT{pi_}_lo"])
                if nk > 4:
                    S.add("scalar", lambda e: e.activation(out=pT[pi_][:, 4:nk, :], in_=psBv[:, 0:nk - 4, :], func=AF.Exp, scale=0.125),
                          [pkB], [f"pT{pi_}_hi"])
                if nwt:
                    tab = etab if edge is None else eedge
                    tkey = "etab" if edge is None else "eedge"
                    nlo = min(nwt, 4)
                    S.add("vector", lambda e: e.tensor_tensor(out=pT[pi_][:, 0:nlo, :], in0=pT[pi_][:, 0:nlo, :], in1=tab[:, h, 0:nlo, :], op=ALU.mult),
                          [f"pT{pi_}_lo", tkey], [f"pT{pi_}_lo"])
                    if nwt > 4:
                        S.add("gpsimd", lambda e: e.tensor_tensor(out=pT[pi_][:, 4:nwt, :], in0=pT[pi_][:, 4:nwt, :], in1=tab[:, h, 4:nwt, :], op=ALU.mult),
                              [f"pT{pi_}_hi", tkey], [f"pT{pi_}_hi"])

            def stage_b(i, h, edge, nwt, ktiles, cnt_):
                pb_ = cnt_ % 2
                pi_ = cnt_ % 3
                nk = len(ktiles)
                ab = i % 2
                psO, pkO = PB[4 + pb_], pbk[4 + pb_]

                def fpv(e):
                    ins = None
                    for j, kt in enumerate(ktiles):
                        ins = e.matmul(psO[:, 0:65], lhsT=pT[pi_][:, j, :], rhs=vsb[:, kt, h * 65:(h + 1) * 65], start=(j == 0), stop=(j == nk - 1))
                    return ins
                S.add("tensor", fpv, [f"pT{pi_}_lo", f"pT{pi_}_hi"] + sorted({f"vsb_{kt // 4}" for kt in ktiles}), [pkO])
                S.add("vector", lambda e: e.reciprocal(out=rec[pb_][:], in_=psO[:, 64:65]), [pkO], [f"rec{pb_}"])
                S.add("vector", lambda e: e.tensor_scalar(out=atile[ab][:, h * 64:(h + 1) * 64], in0=psO[:, 0:64], scalar1=rec[pb_][:, 0:1], scalar2=None, op0=ALU.mult),
                      [pkO, f"rec{pb_}"], [f"atile{ab}"])
                if h == 7:
                    pt, ptkey = next_pt()
                    ptv = pt[:, 0:512].rearrange("p (k c) -> p k c", c=128)

                    def tr4(e):
                        ins = None
                        for k in range(4):
                            ins = e.transpose(ptv[:, k, :], atile[ab][:, k * 128:(k + 1) * 128], identb[:])
                        return ins
                    S.add("tensor", tr4, [f"atile{ab}", "identb"], [ptkey])
                    sbi = (i // 4) % 2
                    copy_op("vector", xas[sbi][:, :, (i % 4) * 128:(i % 4 + 1) * 128], ptv, [ptkey], [f"xas{sbi}"])
                    if i % 4 == 3 or i == nqb - 1:
                        c0 = (i // 4) * 512
                        ntok = ((i % 4) + 1) * 128
                        dma("sync", XaT_d[:, :, c0:c0 + ntok], xas[sbi][:, :, 0:ntok], [f"xas{sbi}"], ["XaT_d"])

            for n, it in enumerate(items):
                stage_a(*it)
                if n >= 2:
                    stage_b(*items[n - 2])
            for it in items[-2:]:
                stage_b(*it)
            S.barrier()

            check(f"att{l}")
            sb.top = MOD_TOP
            w2 = sb.alloc("w2", [128, 8, W2C], BF16)
            w2v = w2_d[l].rearrange("(k p) c -> p k c", p=128)
            for cg in range(8):
                dma("gpsimd", w2[:, :, cg * 512:(cg + 1) * 512], w2v[:, :, cg * 512:(cg + 1) * 512], [], [f"w2_{cg}"])
            pat = sb.alloc("pat", [128, 4, D], BF16)
            pfo = sb.alloc("pfo", [128, 2, D], BF16)
            pco = sb.alloc("pco", [128, 2, D], BF16)
            wo = sb.alloc("wo", [128, 8, D], BF16)
            dma("gpsimd", pat[:], pattn_d[l].rearrange("(k p) c -> p k c", p=128), [], ["pat"])
            dma("gpsimd", pfo[:], pfour_d[l].rearrange("(k p) c -> p k c", p=128), [], ["pfo"])
            dma("gpsimd", pco[:], pconv_d[l].rearrange("(k p) c -> p k c", p=128), [], ["pco"])
            dma("gpsimd", wo[:], wout_d[l].rearrange("(k p) c -> p k c", p=128), [], ["wo"])
            h2 = [sb.alloc("h2", [128, 8, 512], BF16) for _ in range(2)]
            xg = [sb.alloc("xg", [128, 8, 512], BF16) for _ in range(2)]
            gsl = sb.alloc("gsl", [128, 512], BF16)
            sgs = [sb.alloc("sgs", [128, 512], F32) for _ in range(3)]
            mtmp = sb.alloc("mtmp", [128, 512], F32)
            macc = sb.alloc("macc", [128, 512], F32)
            mT_ = sb.alloc("mT", [128, 8, 512], BF16); mT = [mT_, mT_]
            xr_ = sb.alloc("xr", [128, D], F32); xr = [xr_, xr_]
            yn = [sb.alloc("yn", [128, D], F32) for _ in range(2)]
            junk2 = sb.alloc("junk2", [128, 512], BF16)
            ss2 = [sb.alloc("ss2", [128, 2], F32) for _ in range(2)]
            rs2 = [sb.alloc("rs2", [128, 1], F32) for _ in range(2)]
            tcount = 0
            def p2_loads(tb2):
                n2 = 256 if tb2 == 8 else 512
                cc = tb2 * 512
                b2 = tb2 % 2
                dma("sync", h2[b2][:, :, 0:n2], hT_d[:, :, cc:cc + n2], [f"hT_d{tb2}"], [f"h2{b2}"])
                dma("sync", xg[b2][:, 0:4, 0:n2], XaT_d[:, :, cc:cc + n2], ["XaT_d"], [f"xg{b2}"])
                dma("sync", xg[b2][:, 4:6, 0:n2], XfT_d[:, :, cc:cc + n2], ["XfT_d"], [f"xg{b2}"])
                dma("sync", xg[b2][:, 6:8, 0:n2], XcT_d[:, :, cc:cc + n2], ["XcT_d"], [f"xg{b2}"])
            def p2_gates(tb):
                ctxb = (tb == 8)
                ntile = 2 if ctxb else 4
                ntok = ntile * 128
                c0 = tb * 512
                v = 1 if ctxb else 0
                hb = tb % 2
                def proj2(col0, hb=hb, ntok=ntok):
                    ps, pk = next_pb()

                    def f(e, ps=ps, col0=col0):
                        ins = None
                        for k in range(8):
                            ins = e.matmul(ps[:, 0:ntok], lhsT=w2[:, k, col0:col0 + 128], rhs=h2[hb][:, k, 0:ntok], start=(k == 0), stop=(k == 7))
                        return ins
                    S.add("tensor", f, [f"w2_{col0 // 512}", f"h2{hb}"], [pk])
                    return ps, pk
                for cch in range(8):
                    ps, pk = proj2(cch * 128)
                    S.add("scalar", lambda e, ps=ps, ntok=ntok: e.activation(out=gsl[:, 0:ntok], in_=ps[:, 0:ntok], func=AF.Silu), [pk], ["gsl"])
                    S.add("vector", lambda e, cch=cch, hb=hb, ntok=ntok: e.tensor_tensor(out=xg[hb][:, cch, 0:ntok], in0=xg[hb][:, cch, 0:ntok],
                                                                                      in1=gsl[:, 0:ntok], op=ALU.mult), ["gsl", f"xg{hb}"], [f"xg{hb}"])
            def p2_merge(tb):
                ctxb = (tb == 8)
                ntile = 2 if ctxb else 4
                ntok = ntile * 128
                c0 = tb * 512
                v = 1 if ctxb else 0
                hb = tb % 2
                def proj2(col0, hb=hb, ntok=ntok):
                    ps, pk = next_pb()

                    def f(e, ps=ps, col0=col0):
                        ins = None
                        for k in range(8):
                            ins = e.matmul(ps[:, 0:ntok], lhsT=w2[:, k, col0:col0 + 128], rhs=h2[hb][:, k, 0:ntok], start=(k == 0), stop=(k == 7))
                        return ins
                    S.add("tensor", f, [f"w2_{col0 // 512}", f"h2{hb}"], [pk])
                    return ps, pk
                for oc in range(8):
                    first = True
                    for bi, (wt, wk, kc0, nkc, scol) in enumerate(((pat, "pat", 0, 4, 1024), (pfo, "pfo", 4, 2, 2048), (pco, "pco", 6, 2, 3072))):
                        pss, pks = proj2(scol + oc * 128)
                        S.add("scalar", lambda e, pss=pss, bi=bi, ntok=ntok: e.activation(out=sgs[bi][:, 0:ntok], in_=pss[:, 0:ntok], func=AF.Sigmoid),
                              [pks], [f"sgs{bi}"])
                        psy, pky = next_pb()

                        def fy(e, psy=psy, wt=wt, kc0=kc0, nkc=nkc, oc=oc, hb=hb, ntok=ntok):
                            ins = None
                            for k in range(nkc):
                                ins = e.matmul(psy[:, 0:ntok], lhsT=wt[:, k, oc * 128:(oc + 1) * 128], rhs=xg[hb][:, kc0 + k, 0:ntok],
                                               start=(k == 0), stop=(k == nkc - 1))
                            return ins
                        S.add("tensor", fy, [wk, f"xg{hb}"], [pky])
                        if bi == 0:
                            S.add("vector", lambda e, psy=psy, bi=bi, ntok=ntok: e.tensor_tensor(out=macc[:, 0:ntok], in0=psy[:, 0:ntok], in1=sgs[bi][:, 0:ntok], op=ALU.mult),
                                  [pky, f"sgs{bi}"], ["macc"])
                        elif bi == 1:
                            S.add("vector", lambda e, psy=psy, bi=bi, ntok=ntok: e.tensor_tensor(out=mtmp[:, 0:ntok], in0=psy[:, 0:ntok], in1=sgs[bi][:, 0:ntok], op=ALU.mult),
                                  [pky, f"sgs{bi}"], ["mtmp"])
                            S.add("vector", lambda e, ntok=ntok: e.tensor_tensor(out=macc[:, 0:ntok], in0=macc[:, 0:ntok], in1=mtmp[:, 0:ntok], op=ALU.add),
                                  ["macc", "mtmp"], ["macc"])
                        else:
                            S.add("vector", lambda e, psy=psy, bi=bi, ntok=ntok: e.tensor_tensor(out=mtmp[:, 0:ntok], in0=psy[:, 0:ntok], in1=sgs[bi][:, 0:ntok], op=ALU.mult),
                                  [pky, f"sgs{bi}"], ["mtmp"])
                            S.add("vector", lambda e, oc=oc, hb=hb, ntok=ntok: e.tensor_tensor(out=mT[hb][:, oc, 0:ntok], in0=macc[:, 0:ntok], in1=mtmp[:, 0:ntok], op=ALU.add),
                                  ["macc", "mtmp"], ["mT0"])
            def p2_resid(tb):
                ctxb = (tb == 8)
                ntile = 2 if ctxb else 4
                ntok = ntile * 128
                c0 = tb * 512
                v = 1 if ctxb else 0
                hb = tb % 2
                for t in range(ntile):
                    r0 = c0 + t * 128
                    xb = (tb * 4 + t) % 2
                    dma("sync", xr[xb][:], xsrc[r0:r0 + 128, :], [xsrc_k + f"_{r0 // 128}"], ["xr0"])
                    S.add("vector", lambda e, xb=xb: e.memset(ss2[xb][:], 0.0), [], [f"ss2{xb}"])
                    halves = []
                    for hf in range(2):
                        ps, pk = next_pb()

                        def fo(e, ps=ps, hf=hf, t=t, hb=hb):
                            ins = None
                            for k in range(8):
                                ins = e.matmul(ps[:, :], lhsT=mT[hb][:, k, t * 128:(t + 1) * 128], rhs=wo[:, k, hf * 512:(hf + 1) * 512],
                                               start=(k == 0), stop=(k == 7))
                            return ins
                        S.add("tensor", fo, ["wo", "mT0"], [pk])
                        S.add("scalar", lambda e, ps=ps, xb=xb, hf=hf: e.activation(out=junk2[:], in_=ps[:, :], func=AF.Square, accum_out=ss2[xb][:, hf:hf + 1]),
                              [pk, f"ss2{xb}"], ["junk2", f"ss2{xb}"])
                        copy_op("vector", yn[xb][:, hf * 512:(hf + 1) * 512], ps[:, :], [pk], [f"yn{xb}"])
                    S.add("vector", lambda e, xb=xb: e.tensor_tensor(out=rs2[xb][:], in0=ss2[xb][:, 0:1], in1=ss2[xb][:, 1:2], op=ALU.add),
                          [f"ss2{xb}"], [f"rs2{xb}"])
                    S.add("scalar", lambda e, xb=xb: e.activation(out=rs2[xb][:], in_=rs2[xb][:], func=AF.Sqrt, scale=1.0 / D, bias=epsc[:, 0:1]),
                          [f"rs2{xb}", "epsc"], [f"rs2{xb}"])
                    S.add("vector", lambda e, xb=xb: e.reciprocal(out=rs2[xb][:], in_=rs2[xb][:]), [f"rs2{xb}"], [f"rs2{xb}"])
                    S.add("vector", lambda e, xb=xb, v=v: e.scalar_tensor_tensor(out=yn[xb][:], in0=yn[xb][:], scalar=rs2[xb][:, 0:1], in1=ggbc[v][:],
                                                                              op0=ALU.mult, op1=ALU.mult), [f"yn{xb}", f"rs2{xb}", f"ggbc{v}"], [f"yn{xb}"])
                    S.add("vector", lambda e, xb=xb: e.tensor_tensor(out=yn[xb][:], in0=yn[xb][:], in1=xr[xb][:], op=ALU.add),
                          [f"yn{xb}", "xr0"], [f"yn{xb}"])
                    if last:
                        o = dma("sync", yout[r0:r0 + 128, :], yn[xb][:], [f"yn{xb}"], [f"y_{r0 // 128}"])
                        S.final.append(o)
                    else:
                        dma("sync", x1_d[r0:r0 + 128, :], yn[xb][:], [f"yn{xb}"], [f"x1_d_{r0 // 128}"])
            np2 = (9 if not last else 8) if P2_BLOCKS is None else P2_BLOCKS
            p2_loads(0)
            p2_gates(0)
            for tb in range(np2):
                if tb + 1 < np2:
                    p2_loads(tb + 1)
                p2_merge(tb)
                if tb + 1 < np2:
                    p2_gates(tb + 1)
                p2_resid(tb)
            S.barrier()
    except _Stop:
        pass

    if nlayers < DEPTH:
        pass
    return nc, S


def _emit(nc, S):
    import contextlib
    with contextlib.ExitStack() as st:
        csem = {e: [st.enter_context(nc.semaphore(f"c_{e}_{i}")) for i in range(8)] for e in Sched.ENGS}
        dsem = {e: [st.enter_context(nc.semaphore(f"d_{e}_{i}")) for i in range(Sched.K)] for e in ("sync", "gpsimd", "scalar")}
        block = st.enter_context(nc.Block())
        S.emit(nc, block, csem, dsem)


def _fm(vec, n):
    return np.ascontiguousarray(np.asarray(vec, np.float32).reshape(n, 128).T)


def _consts():
    bf = ml_dtypes.bfloat16
    c = {}
    c["identb"] = np.eye(128, dtype=np.float32).astype(bf)
    c["identf"] = np.eye(128, dtype=np.float32)
    c["onesf"] = np.ones((128, 128), np.float32)
    j = np.arange(64)
    ang = 2 * np.pi * np.outer(j, j) / 64.0
    cjm = np.zeros((128, 128)); sjm = np.zeros((128, 128))
    for b in range(2):
        cjm[b * 64:(b + 1) * 64, b * 64:(b + 1) * 64] = np.cos(ang) / 8
        sjm[b * 64:(b + 1) * 64, b * 64:(b + 1) * 64] = -np.sin(ang) / 8
    c["cj"] = cjm.astype(np.float32).astype(bf); c["sj"] = sjm.astype(np.float32).astype(bf)
    cc = np.cos(ang) / 8; ss = np.sin(ang) / 8
    m1 = np.zeros((128, 128))
    m1[0:64, 0:64] = cc; m1[64:128, 0:64] = ss; m1[0:64, 64:128] = -ss; m1[64:128, 64:128] = cc
    c["m1"] = m1.astype(np.float32).astype(bf)
    n2 = np.arange(64)[:, None, None]; k1 = np.arange(64)[None, :, None]; k2 = np.arange(64)[None, None, :]
    th = 2 * np.pi * (n2 * k2 / 64.0 + n2 * k1 / 4096.0)
    g2 = np.concatenate([np.cos(th) / 8, np.sin(th) / 8], 0)
    c["g2"] = g2.astype(np.float32).astype(bf)
    n = np.arange(256)
    psi = 2 * np.pi * np.outer(n, n) / 256.0
    c["c256"] = np.ascontiguousarray((np.cos(psi) / 16).reshape(2, 128, 256).transpose(1, 0, 2)).astype(np.float32).astype(bf)
    c["s256"] = np.ascontiguousarray((np.sin(psi) / 16).reshape(2, 128, 256).transpose(1, 0, 2)).astype(np.float32).astype(bf)
    return c


def _bias_tables(rpb):
    rpb = np.asarray(rpb, np.float32)
    L = rpb.shape[0]
    cols = np.arange(64)
    cs = np.clip(cols - 8, 0, 48)

    def table(r0, kb, ntile):
        out = np.full((L, 8, ntile, 128, 128), NEG, np.float32)
        for j in range(ntile):
            for krl in range(2):
                kr = kb + 2 * j + krl
                for rl in range(2):
                    r = r0 + rl
                    rs = min(max(r - 4, 0), 56)
                    if not (rs <= kr <= rs + 7) or kr > 63:
                        continue
                    dr = kr - r + 7
                    kc = cols[:, None]; c = cols[None, :]
                    valid = (kc >= cs[None, :]) & (kc <= cs[None, :] + 15)
                    dc = np.clip(kc - c + 15, 0, 30)
                    blk = np.where(valid[None, None], rpb[:, :, dr][:, :, dc], NEG)
                    out[:, :, j, krl * 64:(krl + 1) * 64, rl * 64:(rl + 1) * 64] = blk
        return out
    tabi = table(4, 0, 5)
    edges = [table(0, 0, 4), table(2, 0, 4), table(60, 56, 4), table(62, 56, 4)]
    tabe = np.stack(edges, 1)
    return np.ascontiguousarray(tabi), np.ascontiguousarray(tabe)


_CACHE = {}


def kernel(x, c, ctx, c_ctx, w_mod, b_mod, g_pre, g_post, w_in, rpb, w_four, conv_dw, conv_db, conv_ln_g, conv_ln_b,
           w_pw, p_attn, p_four, p_conv, w_out):
    f32 = np.float32
    x = np.asarray(x, f32); ctx = np.asarray(ctx, f32); c = np.asarray(c, f32); c_ctx = np.asarray(c_ctx, f32)
    w_in = np.asarray(w_in, f32)
    w1 = np.ascontiguousarray(np.concatenate([w_in[:, :, 0:1536], w_in[:, :, 2048:2304], w_in[:, :, 2560:3072]], -1))
    w2 = np.ascontiguousarray(np.concatenate([w_in[:, :, 1536:2048], w_in[:, :, 2304:2560], w_in[:, :, 3072:3328], w_in[:, :, 3328:6400]], -1))
    tabi, tabe = _bias_tables(rpb)
    shared = dict(
        w_mod=np.asarray(w_mod, f32),
        bmod=np.stack([_fm(b_mod[l], 24) for l in range(DEPTH)]),
        gpre=np.stack([_fm(g_pre[l], 8) for l in range(DEPTH)]),
        gpost=np.stack([_fm(g_post[l], 8) for l in range(DEPTH)]),
        w1=w1, w2=w2, tabi=tabi, tabe=tabe,
        w_four=np.asarray(w_four, f32),
        dwfm=np.ascontiguousarray(np.asarray(conv_dw, f32).transpose(0, 2, 1).reshape(DEPTH, 2, 128, 31).transpose(0, 2, 1, 3)),
        cvec=np.stack([np.stack([_fm(conv_db[l], 2), _fm(conv_ln_g[l], 2), _fm(conv_ln_b[l], 2)], 1) for l in range(DEPTH)]),
        w_pw=np.asarray(w_pw, f32), p_attn=np.asarray(p_attn, f32), p_four=np.asarray(p_four, f32),
        p_conv=np.asarray(p_conv, f32), w_out=np.asarray(w_out, f32),
    )
    shared.update(_consts())
    in_maps = []
    for core in range(8):
        b = core % 4
        m = dict(shared)
        m["xin"] = np.ascontiguousarray(np.concatenate([x[b], ctx[b]], 0))
        m["cfm"] = np.ascontiguousarray(np.stack([_fm(c[b], 8), _fm(c_ctx, 8)], -1))
        in_maps.append(m)
    if "nc" not in _CACHE:
        nc, S = build(debug=DEBUG, nlayers=ACTIVE_LAYERS)
        _emit(nc, S)
        _CACHE["nc"] = nc
    nc = _CACHE["nc"]
    res = run_bass_kernel_spmd(nc, in_maps, core_ids=list(range(8)))
    _CACHE["res"] = res
    out = np.stack([np.asarray(res.results[b]["y"], f32) for b in range(4)], 0)
    return out
```
